# Optimizing a Trainium2 kernel written in Bass

```python
import jax, jax.numpy as jnp
from jax import lax
import numpy as np

D_MODEL = 2048
BATCH = 1
SEQ = 16384
DEPTH = 2

MIX_WIDTH = D_MODEL
CONV_CH = MIX_WIDTH // 2
CONV_GROUPS = 8
CONV_WIDTH = 31
GDN_HEADS = 8
HEAD_DIM = (MIX_WIDTH - CONV_CH) // GDN_HEADS
GDN_WIDTH = GDN_HEADS * HEAD_DIM
SHORT_CONV = 4
CHUNK = 64
D_FF = ((8 * D_MODEL // 3 + 255) // 256) * 256
IN_COLS = 2 * CONV_CH + 4 * GDN_WIDTH + 2 * GDN_HEADS
LN_EPS = 1e-5
RMS_EPS = 1e-6
L2_EPS = 1e-6

kernel_name = "hymba_conformer_gdn_deepnorm"


def layer_norm(x, g, b):
    xf = x.astype(jnp.float32)
    mu = jnp.mean(xf, -1, keepdims=True)
    var = jnp.mean(jnp.square(xf - mu), -1, keepdims=True)
    y = (xf - mu) * lax.rsqrt(var + LN_EPS) * g.astype(jnp.float32) + b.astype(jnp.float32)
    return y.astype(x.dtype)


def rms_norm(x, g):
    xf = x.astype(jnp.float32)
    return xf * lax.rsqrt(jnp.mean(xf * xf, -1, keepdims=True) + RMS_EPS) * g.astype(jnp.float32)


def l2norm(x):
    xf = x.astype(jnp.float32)
    return xf * lax.rsqrt(jnp.sum(xf * xf, -1, keepdims=True) + L2_EPS)


def causal_depthwise_conv(x, w):
    k_len, ch = w.shape
    return lax.conv_general_dilated(
        x, w.astype(x.dtype)[:, None, :], window_strides=(1,),
        padding=[(k_len - 1, 0)], dimension_numbers=('NWC', 'WIO', 'NWC'),
        feature_group_count=ch)


def conformer_conv_group(u_val, u_gate, dw_w, dw_b, ln_g, ln_b):
    h = u_val * jax.nn.sigmoid(u_gate)
    h = causal_depthwise_conv(h, dw_w) + dw_b.astype(h.dtype)
    h = layer_norm(h, ln_g, ln_b)
    return jax.nn.silu(h)


def chunk_gated_delta_rule(q, k, v, g, beta):
    bsz, seq, nh, dk = q.shape
    dv = v.shape[-1]
    n_chunks = seq // CHUNK
    q = q * (dk ** -0.5)

    def chunks(t):
        return t.reshape(bsz, n_chunks, CHUNK, nh, -1).transpose(0, 3, 1, 2, 4)

    q, k, v = chunks(q), chunks(k), chunks(v)
    beta = beta.reshape(bsz, n_chunks, CHUNK, nh).transpose(0, 3, 1, 2)
    g = jnp.cumsum(g.reshape(bsz, n_chunks, CHUNK, nh).transpose(0, 3, 1, 2), axis=-1)

    idx = jnp.arange(CHUNK)
    causal = idx[:, None] >= idx[None, :]
    strict = idx[:, None] > idx[None, :]
    decay = jnp.exp(jnp.where(causal, g[..., :, None] - g[..., None, :], -jnp.inf))

    k_beta = k * beta[..., None]
    v_beta = v * beta[..., None]
    L = jnp.einsum('bhnid,bhnjd->bhnij', k_beta, k) * jnp.where(strict, decay, 0.0)
    lhs = jnp.eye(CHUNK, dtype=jnp.float32) + L
    rhs = jnp.concatenate([v_beta, k_beta * jnp.exp(g)[..., None]], axis=-1)
    sol = lax.linalg.triangular_solve(lhs, rhs, left_side=True, lower=True, unit_diagonal=True)
    u_c = sol[..., :dv]
    w_c = sol[..., dv:]

    intra = jnp.einsum('bhnid,bhnjd->bhnij', q, k) * decay
    q_dec = q * jnp.exp(g)[..., None]
    g_last = g[..., -1]
    k_dec = k * jnp.exp(g_last[..., None] - g)[..., None]

    def step(state, xs):
        u_i, w_i, qd_i, kd_i, a_i, gl_i = xs
        v_new = u_i - jnp.einsum('bhcd,bhde->bhce', w_i, state)
        o_i = (jnp.einsum('bhcd,bhde->bhce', qd_i, state)
               + jnp.einsum('bhij,bhje->bhie', a_i, v_new))
        state = state * jnp.exp(gl_i)[..., None, None] + jnp.einsum('bhcd,bhce->bhde', kd_i, v_new)
        return state, o_i

    xs = tuple(jnp.moveaxis(t, 2, 0) for t in (u_c, w_c, q_dec, k_dec, intra, g_last))
    state0 = jnp.zeros((bsz, nh, dk, dv), jnp.float32)
    _, o = lax.scan(step, state0, xs)
    return o.transpose(1, 0, 3, 2, 4).reshape(bsz, seq, nh, dv)


def gated_deltanet_group(qkv, z, b_raw, a_raw, sc_w, A_log, dt_bias, norm_w):
    bsz, seq, _ = qkv.shape
    qkv = jax.nn.silu(causal_depthwise_conv(qkv, sc_w))
    q = qkv[..., :GDN_WIDTH].reshape(bsz, seq, GDN_HEADS, HEAD_DIM)
    k = qkv[..., GDN_WIDTH:2 * GDN_WIDTH].reshape(bsz, seq, GDN_HEADS, HEAD_DIM)
    v = qkv[..., 2 * GDN_WIDTH:].reshape(bsz, seq, GDN_HEADS, HEAD_DIM).astype(jnp.float32)
    q, k = l2norm(q), l2norm(k)
    beta = jax.nn.sigmoid(b_raw.astype(jnp.float32))
    g = -jnp.exp(A_log.astype(jnp.float32)) * jax.nn.softplus(
        a_raw.astype(jnp.float32) + dt_bias.astype(jnp.float32))
    o = chunk_gated_delta_rule(q, k, v, g, beta)
    zh = z.reshape(bsz, seq, GDN_HEADS, HEAD_DIM).astype(jnp.float32)
    o = rms_norm(o, norm_w) * jax.nn.silu(zh)
    return o.reshape(bsz, seq, GDN_WIDTH).astype(z.dtype)


def setup_inputs(seed: int = 0) -> dict:
    key = jax.random.key(seed)
    ks = jax.random.split(key, 20)
    f32 = jnp.float32
    beta_init = (8 * DEPTH) ** -0.25
    nrm = lambda k, shape, s: jax.random.normal(k, shape, f32) * s
    dt = jnp.exp(jax.random.uniform(ks[8], (DEPTH, GDN_HEADS), f32, np.log(1e-3), np.log(1e-1)))
    return {
        "x": jax.random.normal(ks[0], (BATCH, SEQ, D_MODEL), f32),
        "w_in": nrm(ks[1], (DEPTH, D_MODEL, IN_COLS), D_MODEL ** -0.5),
        "conv_dw_w": nrm(ks[2], (DEPTH, CONV_WIDTH, CONV_CH), CONV_WIDTH ** -0.5),
        "conv_dw_b": nrm(ks[3], (DEPTH, CONV_CH), 0.01),
        "conv_ln_g": 1.0 + nrm(ks[4], (DEPTH, CONV_CH), 0.02),
        "conv_ln_b": nrm(ks[5], (DEPTH, CONV_CH), 0.01),
        "gdn_conv_w": nrm(ks[6], (DEPTH, SHORT_CONV, 3 * GDN_WIDTH), SHORT_CONV ** -0.5),
        "gdn_A_log": jnp.log(jax.random.uniform(ks[7], (DEPTH, GDN_HEADS), f32, 1.0, 16.0)),
        "gdn_dt_bias": dt + jnp.log(-jnp.expm1(-dt)),
        "gdn_norm_w": 1.0 + nrm(ks[9], (DEPTH, HEAD_DIM), 0.02),
        "w_out": nrm(ks[10], (DEPTH, MIX_WIDTH, D_MODEL), MIX_WIDTH ** -0.5 * beta_init),
        "ln1_g": 1.0 + nrm(ks[11], (DEPTH, D_MODEL), 0.02),
        "ln1_b": nrm(ks[12], (DEPTH, D_MODEL), 0.01),
        "w_gate_up": nrm(ks[13], (DEPTH, D_MODEL, 2 * D_FF), D_MODEL ** -0.5),
        "w_down": nrm(ks[14], (DEPTH, D_FF, D_MODEL), D_FF ** -0.5 * beta_init),
        "ln2_g": 1.0 + nrm(ks[15], (DEPTH, D_MODEL), 0.02),
        "ln2_b": nrm(ks[16], (DEPTH, D_MODEL), 0.01),
    }


def reference(x, w_in, conv_dw_w, conv_dw_b, conv_ln_g, conv_ln_b, gdn_conv_w, gdn_A_log,
              gdn_dt_bias, gdn_norm_w, w_out, ln1_g, ln1_b, w_gate_up, w_down, ln2_g, ln2_b):
    alpha = (2 * DEPTH) ** 0.25
    c0 = 2 * CONV_CH
    c1 = c0 + 3 * GDN_WIDTH
    c2 = c1 + GDN_WIDTH
    c3 = c2 + GDN_HEADS
    for l in range(DEPTH):
        u = jnp.einsum('bsd,dc->bsc', x, w_in[l])
        conv_out = conformer_conv_group(u[..., :CONV_CH], u[..., CONV_CH:c0],
                                        conv_dw_w[l], conv_dw_b[l], conv_ln_g[l], conv_ln_b[l])
        gdn_out = gated_deltanet_group(u[..., c0:c1], u[..., c1:c2], u[..., c2:c3], u[..., c3:],
                                       gdn_conv_w[l], gdn_A_log[l], gdn_dt_bias[l], gdn_norm_w[l])
        mix = jnp.einsum('bsm,md->bsd', jnp.concatenate([conv_out, gdn_out], axis=-1), w_out[l])
        x = layer_norm(alpha * x + mix, ln1_g[l], ln1_b[l])
        gu = jnp.einsum('bsd,df->bsf', x, w_gate_up[l])
        hid = jax.nn.silu(gu[..., :D_FF]) * gu[..., D_FF:]
        ffn = jnp.einsum('bsf,fd->bsd', hid, w_down[l])
        x = layer_norm(alpha * x + ffn, ln2_g[l], ln2_b[l])
    return x
```

```python
import numpy as np
from contextlib import ExitStack
import concourse.bass as bass
import concourse.mybir as mybir
from concourse.bass_utils import run_bass_kernel_spmd

F32 = mybir.dt.float32
BF16 = mybir.dt.bfloat16
ALU = mybir.AluOpType
AF = mybir.ActivationFunctionType

ENGS = ("pe", "act", "dve", "pool", "sp")

D_MODEL = 2048
SEQ = 16384
DEPTH = 2
NCORES = 8
NT = SEQ // NCORES
HALO = 32
NTX = NT + HALO
KC = D_MODEL // 128
CONV_CH = 1024
GDN_W = 1024
NHEAD = 8
D_FF = 5632
FC = D_FF // 128
IN_COLS = 6160
ALPHA = (2 * DEPTH) ** 0.25
LN_EPS = 1e-5
RMS_EPS = 1e-6
L2_EPS = 1e-6


class _Op:
    __slots__ = ("eng", "fn", "deps", "is_dma", "semkey", "signal", "count")

    def __init__(self, eng, fn, is_dma, semkey):
        self.eng = eng
        self.fn = fn
        self.deps = []
        self.is_dma = is_dma
        self.semkey = semkey
        self.signal = False
        self.count = 0


class Prog:
    def __init__(self, nc):
        self.nc = nc
        self.ops = {e: [] for e in ENGS}
        self.last_w = {}
        self.readers = {}
        self.final_waits = []

    def _add(self, eng, fn, reads, writes, is_dma=False, semkey=None):
        op = _Op(eng, fn, is_dma, semkey)
        deps = {}
        for r in reads:
            w = self.last_w.get(r)
            if w is not None:
                deps[id(w)] = (w, "raw")
        for wkey in writes:
            for rd in self.readers.get(wkey, ()):
                if id(rd) not in deps:
                    deps[id(rd)] = (rd, "war")
            w = self.last_w.get(wkey)
            if w is not None and id(w) not in deps:
                deps[id(w)] = (w, "waw")
        for r in reads:
            self.readers.setdefault(r, []).append(op)
        for wkey in writes:
            self.last_w[wkey] = op
            self.readers[wkey] = []
        for (d, kind) in deps.values():
            if d is op:
                continue
            if (not d.is_dma) and (not is_dma) and d.eng == eng:
                if eng == "pe" or kind != "raw":
                    continue
            op.deps.append(d)
            d.signal = True
        self.ops[eng].append(op)
        return op

    def op(self, eng, fn, reads=(), writes=()):
        return self._add(eng, fn, tuple(reads), tuple(writes))

    def dma(self, eng, out, in_, reads=(), writes=(), key=None):
        fn = lambda e: e.dma_start(out=out, in_=in_)
        return self._add(eng, fn, tuple(reads), tuple(writes), is_dma=True, semkey=key)

    def must_finish(self, op):
        op.signal = True
        self.final_waits.append(op)

    def emit(self):
        nc = self.nc
        keycount = {}
        for e in ENGS:
            c = 0
            for op in self.ops[e]:
                if op.is_dma:
                    k = op.semkey
                    keycount[k] = keycount.get(k, 0) + 16
                    op.count = keycount[k]
                elif op.signal:
                    c += 1
                    op.count = c
        with ExitStack() as st:
            esem = {e: st.enter_context(nc.semaphore("s_" + e)) for e in ENGS}
            ksem = {k: st.enter_context(nc.semaphore("k_" + str(k))) for k in keycount}
            block = st.enter_context(nc.Block())

            def run(engname, eh):
                waited = {}
                for op in self.ops[engname]:
                    need = {}
                    for d in op.deps:
                        s = ("k", d.semkey) if d.is_dma else ("e", d.eng)
                        if d.count > need.get(s, 0):
                            need[s] = d.count
                    for s, v in need.items():
                        if waited.get(s, 0) >= v:
                            continue
                        sem = ksem[s[1]] if s[0] == "k" else esem[s[1]]
                        eh.wait_ge(sem, v)
                        waited[s] = v
                    ins = op.fn(eh)
                    if op.is_dma:
                        ins.then_inc(ksem[op.semkey], 16)
                    elif op.signal:
                        ins.then_inc(esem[engname], 1)
                if engname == "sp":
                    for op in self.final_waits:
                        sem = ksem[op.semkey] if op.is_dma else esem[op.eng]
                        eh.wait_ge(sem, op.count)

            block.tensor(lambda eh: run("pe", eh))
            block.scalar(lambda eh: run("act", eh))
            block.vector(lambda eh: run("dve", eh))
            block.gpsimd(lambda eh: run("pool", eh))
            block.sync(lambda eh: run("sp", eh))


class Ctx:
    def __init__(self, nc, st):
        self.nc = nc
        self.st = st
        self.P = Prog(nc)
        self.nbank = 0

    def sb(self, name, shape, dt):
        return self.st.enter_context(self.nc.sbuf_tensor(name, list(shape), dt))

    def bank(self, name):
        self.nbank += 1
        assert self.nbank <= 8
        return self.st.enter_context(self.nc.psum_tensor(name, [128, 512], F32))

    def din(self, name, shape, dt=F32):
        return self.nc.dram_tensor(name, list(shape), dt, kind="ExternalInput").ap()

    def dout(self, name, shape, dt=F32):
        return self.nc.dram_tensor(name, list(shape), dt, kind="ExternalOutput").ap()

    def mm(self, out, lhsT, rhs, start, stop, r, w):
        self.P.op("pe", lambda e: e.matmul(out, lhsT=lhsT, rhs=rhs, start=start, stop=stop), r, w)

    def tr(self, out, in_, ident, r, w):
        self.P.op("pe", lambda e: e.transpose(out, in_, ident), r, w)

    def act(self, out, in_, func, r, w, bias=None, scale=None, accum=None, eng="act"):
        kw = {}
        if bias is not None:
            kw["bias"] = bias
        if scale is not None:
            kw["scale"] = scale
        if accum is not None:
            kw["accum_out"] = accum
        self.P.op(eng, lambda e: e.activation(out=out, in_=in_, func=func, **kw), r, w)

    def tt(self, eng, out, in0, in1, op, r, w):
        self.P.op(eng, lambda e: e.tensor_tensor(out=out, in0=in0, in1=in1, op=op), r, w)

    def ts(self, eng, out, in0, s1, op0, r, w, s2=None, op1=None):
        if op1 is None:
            self.P.op(eng, lambda e: e.tensor_scalar(out=out, in0=in0, scalar1=s1, scalar2=None, op0=op0), r, w)
        else:
            self.P.op(eng, lambda e: e.tensor_scalar(out=out, in0=in0, scalar1=s1, scalar2=s2, op0=op0, op1=op1), r, w)

    def stt(self, eng, out, in0, scalar, in1, op0, op1, r, w):
        self.P.op(eng, lambda e: e.scalar_tensor_tensor(out=out, in0=in0, scalar=scalar, in1=in1, op0=op0, op1=op1), r, w)

    def cp(self, eng, out, in_, r, w):
        if eng == "act":
            self.P.op(eng, lambda e: e.copy(out=out, in_=in_), r, w)
        else:
            self.P.op(eng, lambda e: e.tensor_copy(out=out, in_=in_), r, w)

    def recip(self, out, in_, r, w):
        self.P.op("dve", lambda e: e.reciprocal(out=out, in_=in_), r, w)

    def memset(self, eng, ap, val, w):
        self.P.op(eng, lambda e: e.memset(ap, val), (), w)


def build_A():
    nc = bass.Bass("TRN2", target_bir_lowering=False)
    with ExitStack() as st:
        C = Ctx(nc, st)
        P = C.P
        xT = C.din("xT", [D_MODEL, NTX])
        winr = C.din("winr", [D_MODEL, IN_COLS])
        cdw = C.din("cdw", [CONV_CH, 31])
        cvec = C.din("cvec", [CONV_CH, 3])
        gcw = C.din("gcw", [3 * GDN_W, 4])
        hv = C.din("hv", [NHEAD, 2])
        identd = C.din("ident", [128, 128])
        convT = C.dout("convT", [CONV_CH, NT], BF16)
        qkvT = C.dout("qkvT", [3 * GDN_W, NT], BF16)
        gzT = C.dout("gzT", [GDN_W, NT], BF16)
        bgo = C.dout("bg", [2 * NHEAD, NT], F32)

        xs = C.sb("xs", [128, KC, NTX], BF16)
        wb = [C.sb("wb%d" % i, [128, KC, 256], BF16) for i in range(2)]
        wsm = C.sb("wsm", [128, KC, 16], BF16)
        identf = C.sb("identf", [128, 128], F32)
        identb = C.sb("identb", [128, 128], BF16)
        onesb = C.sb("onesb", [128, 128], BF16)
        cdws = C.sb("cdws", [128, 8, 31], F32)
        cvecs = C.sb("cvecs", [128, 8, 3], F32)
        gcws = C.sb("gcws", [128, 24, 4], F32)
        hvs = C.sb("hvs", [NHEAD, 2], F32)
        negA = C.sb("negA", [NHEAD, 1], F32)
        Dc = [C.sb("Dc%d" % i, [128, 31, 128], BF16) for i in range(2)]
        Dq = [C.sb("Dq%d" % i, [128, 4, 128], BF16) for i in range(2)]
        hbuf = [C.sb("hbuf%d" % i, [128, NTX], BF16) for i in range(2)]
        ubuf = [C.sb("ubuf%d" % i, [128, NTX], BF16) for i in range(2)]
        ybuf = C.sb("ybuf", [128, 8, NT], BF16)
        sg = [C.sb("sg%d" % i, [128, 512], F32) for i in range(2)]
        ysq = [C.sb("ysq%d" % i, [128, 512], BF16) for i in range(2)]
        ost = [C.sb("ost%d" % i, [128, NT], BF16) for i in range(3)]
        mean = C.sb("mean", [128, 512], F32)
        rstd = C.sb("rstd", [128, 512], F32)
        tmpa = C.sb("tmpa", [128, 512], F32)
        t1 = [C.sb("t1_%d" % i, [128, 512], F32) for i in range(2)]
        t2 = [C.sb("t2_%d" % i, [128, 512], F32) for i in range(2)]
        sv = [C.sb("sv%d" % i, [128, 512], F32) for i in range(2)]
        bgs = C.sb("bgs", [NHEAD, 2, NT], F32)
        sp1 = C.sb("sp1", [NHEAD, 512], F32)
        sp2 = C.sb("sp2", [NHEAD, 512], F32)

        pu = [C.bank("pu%d" % i) for i in range(4)]
        pc = [C.bank("pc%d" % i) for i in range(2)]
        pst = [C.bank("pst%d" % i) for i in range(2)]

        xv = xT.rearrange("(k p) t -> p k t", p=128)
        for k in range(KC):
            P.dma("pool", xs[:, k, :], xv[:, k, :], writes=["xs%d" % k], key="xs%d" % k)
        P.dma("sp", identf[:], identd, writes=["identf"], key="c0")
        P.dma("sp", cdws[:], cdw.rearrange("(i p) j -> p i j", p=128), writes=["cdws"], key="c1")
        P.dma("sp", cvecs[:], cvec.rearrange("(i p) j -> p i j", p=128), writes=["cvecs"], key="c2")
        P.dma("sp", gcws[:], gcw.rearrange("(i p) j -> p i j", p=128), writes=["gcws"], key="c3")
        P.dma("sp", hvs[:], hv, writes=["hvs"], key="c4")
        C.cp("dve", identb[:], identf[:], ["identf"], ["identb"])
        C.memset("dve", onesb[:], 1.0, ["onesb"])
        C.act(negA[:], hvs[:, 0:1], AF.Exp, ["hvs"], ["negA"])
        C.ts("dve", negA[:], negA[:], -1.0, ALU.mult, ["negA"], ["negA"])

        xs_keys = ["xs%d" % k for k in range(KC)]
        wv = winr.rearrange("(k p) c -> p k c", p=128)
        blocks = [(0, HALO)] + [(HALO + 512 * i, 512) for i in range(4)]

        state = {"wslot": 0, "pu": 0, "pc": 0, "ost": 0, "ngrp": 0}

        def load_w(col0):
            s = state["wslot"]
            state["wslot"] ^= 1
            for h in range(4):
                P.dma("pool", wb[s][:, 4 * h:4 * h + 4, :], wv[:, 4 * h:4 * h + 4, col0:col0 + 256],
                      writes=["wb%d" % s], key="wb%d" % s)
            return s

        def inproj(s, off, c0, n):
            b = state["pu"]
            state["pu"] = (b + 1) % 4
            for k in range(KC):
                C.mm(pu[b][:, 0:n], wb[s][:, k, off:off + 128], xs[:, k, c0:c0 + n], k == 0, k == KC - 1,
                     ["wb%d" % s, "xs%d" % k], ["pu%d" % b])
            return b

        def next_ost():
            o = state["ost"]
            state["ost"] = (o + 1) % 3
            return o

        for i in range(8):
            s = load_w(256 * i)
            hs = i % 2
            P.op("pool", lambda e, hs=hs, i=i: e.tensor_tensor(
                out=Dc[hs][:], in0=identb[:].unsqueeze(1).to_broadcast([128, 31, 128]),
                in1=cdws[:, i, :].unsqueeze(2).to_broadcast([128, 31, 128]), op=ALU.mult),
                ["identb", "cdws"], ["Dc%d" % hs])
            for bi, (c0, n) in enumerate(blocks):
                ba = inproj(s, 0, c0, n)
                bb = inproj(s, 128, c0, n)
                q = (ba // 2) % 2
                C.act(sg[q][:, 0:n], pu[bb][:, 0:n], AF.Sigmoid, [], ["pu%d" % bb, "sg%d" % q])
                C.tt("dve", hbuf[hs][:, c0:c0 + n], pu[ba][:, 0:n], sg[q][:, 0:n], ALU.mult,
                     ["sg%d" % q], ["pu%d" % ba, "hbuf%d_%d" % (hs, bi)])
            for tb in range(4):
                b = state["pc"]
                state["pc"] ^= 1
                base = HALO + 512 * tb - 30
                for j in range(31):
                    C.mm(pc[b][:], Dc[hs][:, j, :], hbuf[hs][:, base + j:base + j + 512], j == 0, j == 30,
                         ["Dc%d" % hs, "hbuf%d_%d" % (hs, tb), "hbuf%d_%d" % (hs, tb + 1)], ["pc%d" % b])
                C.act(ybuf[:, i, 512 * tb:512 * tb + 512], pc[b][:], AF.Identity, ["cvecs"], ["pc%d" % b, "y%d_%d" % (i, tb)],
                      bias=cvecs[:, i, 0:1])

        for tb in range(4):
            tsl = slice(512 * tb, 512 * tb + 512)
            for i in range(8):
                C.mm(pst[0][:], onesb[:], ybuf[:, i, tsl], i == 0, i == 7, ["onesb", "y%d_%d" % (i, tb)], ["pst0"])
            for i in range(8):
                q = i % 2
                C.act(ysq[q][:], ybuf[:, i, tsl], AF.Square, ["y%d_%d" % (i, tb)], ["ysq%d" % q])
                C.mm(pst[1][:], onesb[:], ysq[q][:], i == 0, i == 7, ["onesb", "ysq%d" % q], ["pst1"])
            C.ts("dve", mean[:], pst[0][:], 1.0 / CONV_CH, ALU.mult, [], ["pst0", "mean"])
            C.tt("dve", tmpa[:], mean[:], mean[:], ALU.mult, ["mean"], ["tmpa"])
            C.stt("dve", tmpa[:], pst[1][:], 1.0 / CONV_CH, tmpa[:], ALU.mult, ALU.subtract, ["tmpa"], ["pst1", "tmpa"])
            C.act(tmpa[:], tmpa[:], AF.Sqrt, ["tmpa"], ["tmpa"], bias=LN_EPS)
            C.recip(rstd[:], tmpa[:], ["tmpa"], ["rstd"])
            for i in range(8):
                q = i % 2
                C.tt("dve", t1[q][:], ybuf[:, i, tsl], mean[:], ALU.subtract, ["y%d_%d" % (i, tb), "mean"], ["t1_%d" % q])
                C.tt("pool", t2[q][:], t1[q][:], rstd[:], ALU.mult, ["t1_%d" % q, "rstd"], ["t2_%d" % q])
                C.act(ysq[q][:], t2[q][:], AF.Silu, ["t2_%d" % q, "cvecs"], ["ysq%d" % q],
                      bias=cvecs[:, i, 2:3], scale=cvecs[:, i, 1:2])
                o = P.dma("sp", convT[128 * i:128 * i + 128, tsl], ysq[q][:], reads=["ysq%d" % q], key="oc%d" % q)
                P.must_finish(o)

        for g in range(12):
            s = load_w(2048 + 256 * g)
            for half in range(2):
                ch = 2 * g + half
                us = ch % 2
                P.op("pool", lambda e, us=us, ch=ch: e.tensor_tensor(
                    out=Dq[us][:], in0=identb[:].unsqueeze(1).to_broadcast([128, 4, 128]),
                    in1=gcws[:, ch, :].unsqueeze(2).to_broadcast([128, 4, 128]), op=ALU.mult),
                    ["identb", "gcws"], ["Dq%d" % us])
                for bi, (c0, n) in enumerate(blocks):
                    b = inproj(s, 128 * half, c0, n)
                    C.cp("act" if bi % 2 == 0 else "dve", ubuf[us][:, c0:c0 + n], pu[b][:, 0:n], [],
                         ["pu%d" % b, "ubuf%d_%d" % (us, bi)])
                o = next_ost()
                for tb in range(4):
                    tsl = slice(512 * tb, 512 * tb + 512)
                    b = state["pc"]
                    state["pc"] ^= 1
                    base = HALO + 512 * tb - 3
                    for j in range(4):
                        C.mm(pc[b][:], Dq[us][:, j, :], ubuf[us][:, base + j:base + j + 512], j == 0, j == 3,
                             ["Dq%d" % us, "ubuf%d_%d" % (us, tb), "ubuf%d_%d" % (us, tb + 1)], ["pc%d" % b])
                    if ch >= 16:
                        C.act(ost[o][:, tsl], pc[b][:], AF.Silu, [], ["pc%d" % b, "ost%d" % o])
                    else:
                        q = tb % 2
                        C.act(sv[q][:], pc[b][:], AF.Silu, [], ["pc%d" % b, "sv%d" % q])
                        C.tt("pool", ysq[q][:], sv[q][:], sv[q][:], ALU.mult, ["sv%d" % q], ["ysq%d" % q])
                        C.mm(pst[q][:], onesb[:], ysq[q][:], True, True, ["onesb", "ysq%d" % q], ["pst%d" % q])
                        C.act(t1[q][:], pst[q][:], AF.Sqrt, [], ["pst%d" % q, "t1_%d" % q], bias=L2_EPS)
                        C.recip(t2[q][:], t1[q][:], ["t1_%d" % q], ["t2_%d" % q])
                        if ch < 8:
                            C.stt("dve", ost[o][:, tsl], sv[q][:], 128.0 ** -0.5, t2[q][:], ALU.mult, ALU.mult,
                                  ["sv%d" % q, "t2_%d" % q], ["ost%d" % o])
                        else:
                            C.tt("dve", ost[o][:, tsl], sv[q][:], t2[q][:], ALU.mult, ["sv%d" % q, "t2_%d" % q], ["ost%d" % o])
                d = P.dma("sp", qkvT[128 * ch:128 * ch + 128, :], ost[o][:], reads=["ost%d" % o], key="oq%d" % o)
                P.must_finish(d)

        for g in range(4):
            s = load_w(5120 + 256 * g)
            for half in range(2):
                ch = 2 * g + half
                o = next_ost()
                for (c0, n) in blocks[1:]:
                    b = inproj(s, 128 * half, c0, n)
                    C.act(ost[o][:, c0 - HALO:c0 - HALO + n], pu[b][:, 0:n], AF.Silu, [], ["pu%d" % b, "ost%d" % o])
                d = P.dma("sp", gzT[128 * ch:128 * ch + 128, :], ost[o][:], reads=["ost%d" % o], key="oq%d" % o)
                P.must_finish(d)

        P.dma("pool", wsm[:], wv[:, :, 6144:6160], writes=["wsm"], key="wsm")
        for (c0, n) in blocks[1:]:
            tsl = slice(c0 - HALO, c0 - HALO + n)
            b = state["pu"]
            state["pu"] = (b + 1) % 4
            for k in range(KC):
                C.mm(pu[b][0:NHEAD, 0:n], wsm[:, k, 0:8], xs[:, k, c0:c0 + n], k == 0, k == KC - 1,
                     ["wsm", "xs%d" % k], ["pu%d" % b])
            C.act(bgs[:, 0, tsl], pu[b][0:NHEAD, 0:n], AF.Sigmoid, [], ["pu%d" % b, "bgs0"])
            b = state["pu"]
            state["pu"] = (b + 1) % 4
            for k in range(KC):
                C.mm(pu[b][0:NHEAD, 0:n], wsm[:, k, 8:16], xs[:, k, c0:c0 + n], k == 0, k == KC - 1,
                     ["wsm", "xs%d" % k], ["pu%d" % b])
            C.act(sp1[:, 0:n], pu[b][0:NHEAD, 0:n], AF.Abs, ["hvs"], ["pu%d" % b, "sp1"], bias=hvs[:, 1:2])
            C.act(sp1[:, 0:n], sp1[:, 0:n], AF.Exp, ["sp1"], ["sp1"], scale=-1.0)
            C.act(sp1[:, 0:n], sp1[:, 0:n], AF.Ln, ["sp1"], ["sp1"], bias=1.0)
            C.ts("dve", sp2[:, 0:n], pu[b][0:NHEAD, 0:n], hvs[:, 1:2], ALU.add, ["hvs"], ["pu%d" % b, "sp2"], s2=0.0, op1=ALU.max)
            C.tt("dve", sp2[:, 0:n], sp2[:, 0:n], sp1[:, 0:n], ALU.add, ["sp1", "sp2"], ["sp2"])
            C.ts("dve", bgs[:, 1, tsl], sp2[:, 0:n], negA[:, 0:1], ALU.mult, ["sp2", "negA"], ["bgs1"])
        d = P.dma("sp", bgo.rearrange("(a h) t -> h a t", a=2), bgs[:], reads=["bgs0", "bgs1"], key="obg")
        P.must_finish(d)

        P.emit()
    return nc


def host_inputs_A(x_full, w_in, conv_dw_w, conv_dw_b, conv_ln_g, conv_ln_b, gdn_conv_w, A_log, dt_bias):
    perm = []
    for i in range(8):
        perm += list(range(128 * i, 128 * i + 128)) + list(range(1024 + 128 * i, 1024 + 128 * i + 128))
    perm += list(range(2048, IN_COLS))
    winr = np.ascontiguousarray(w_in[:, perm])
    cdw = np.ascontiguousarray(conv_dw_w.T)
    cvec = np.ascontiguousarray(np.stack([conv_dw_b, conv_ln_g, conv_ln_b], axis=1))
    gcw = np.ascontiguousarray(gdn_conv_w.T)
    hv = np.ascontiguousarray(np.stack([A_log, dt_bias], axis=1))
    ident = np.eye(128, dtype=np.float32)
    ncore = x_full.shape[0] // NT
    maps = []
    for c in range(ncore):
        xt = np.zeros((D_MODEL, NTX), np.float32)
        xt[:, HALO:] = x_full[c * NT:(c + 1) * NT].T
        if c > 0:
            xt[:, :HALO] = x_full[c * NT - HALO:c * NT].T
        maps.append({"xT": xt, "winr": winr, "cdw": cdw, "cvec": cvec, "gcw": gcw, "hv": hv, "ident": ident})
    return maps


NTILE = SEQ // 128
TPB = 16


def _interleave(gens):
    gens = [g for g in gens if g is not None]
    while gens:
        nxt = []
        for g in gens:
            try:
                next(g)
                nxt.append(g)
            except StopIteration:
                pass
        gens = nxt


def build_B(ntile=NTILE):
    nc = bass.Bass("TRN2", target_bir_lowering=False)
    ntok = ntile * 128
    nblk = ntile // TPB
    with ExitStack() as st:
        C = Ctx(nc, st)
        P = C.P
        qT = C.din("qT", [128, ntok], BF16)
        kT = C.din("kT", [128, ntok], BF16)
        vT = C.din("vT", [128, ntok], BF16)
        gzT = C.din("gzT", [128, ntok], BF16)
        bgT = C.din("bgT", [2, ntile, 128])
        consts = C.din("consts", [128, 4, 128])
        normw = C.din("normw", [128, 1])
        oT = C.dout("oT", [128, ntok], BF16)

        cs = C.sb("cs", [128, 4, 128], F32)
        identf, mUi, mUs, mLs = cs[:, 0, :], cs[:, 1, :], cs[:, 2, :], cs[:, 3, :]
        identb = C.sb("identb", [128, 128], BF16)
        onesf = C.sb("onesf", [128, 128], F32)
        nws = C.sb("nws", [128, 1], F32)
        Gr = C.sb("Gr", [128, 2, 128], F32)
        gcol = C.sb("gcol", [128, ntile], F32)
        bcol = C.sb("bcol", [128, ntile], F32)
        gc = C.sb("gc", [128, ntile], F32)
        ngc = C.sb("ngc", [128, ntile], F32)
        glb = C.sb("glb", [128, ntile], F32)
        egl = C.sb("egl", [128, ntile], F32)
        sc1 = C.sb("sc1", [128, ntile], F32)
        sc2 = C.sb("sc2", [128, ntile], F32)
        S = C.sb("S", [128, 128], F32)
        Sb = C.sb("Sb", [128, 128], BF16)
        qs = [C.sb("qs%d" % i, [128, TPB * 128], BF16) for i in range(2)]
        ks = [C.sb("ks%d" % i, [128, TPB * 128], BF16) for i in range(2)]
        vs = [C.sb("vs%d" % i, [128, TPB * 128], BF16) for i in range(2)]
        zs = [C.sb("zs%d" % i, [128, TPB * 128], BF16) for i in range(2)]
        osb = [C.sb("os%d" % i, [128, TPB * 128], BF16) for i in range(2)]

        def two(name, shape, dt):
            return [C.sb("%s%d" % (name, i), shape, dt) for i in range(2)]
        dabs = two("dabs", [128, 128], F32)
        W = two("W", [128, 128], F32)
        egB = two("egB", [128, 128], F32)
        bU = two("bU", [128, 128], F32)
        KW = two("KW", [128, 128], F32)
        Wm = two("Wm", [128, 128], F32)
        AT = two("AT", [128, 128], BF16)
        XY = [[C.sb("XY%d_%d" % (i, j), [128, 256], F32) for j in range(2)] for i in range(2)]
        Pm = [[C.sb("Pm%d_%d" % (i, j), [128, 128], F32) for j in range(2)] for i in range(2)]
        Pb = two("Pb", [128, 128], BF16)
        kbg = two("kbg", [128, 128], BF16)
        kd = two("kd", [128, 128], BF16)
        vb = two("vb", [128, 128], BF16)
        qd = two("qd", [128, 128], BF16)
        nwT = two("nwT", [128, 128], BF16)
        vn = two("vn", [128, 128], BF16)
        junk = two("junk", [128, 128], F32)
        ss = two("ss", [128, 1], F32)
        rt = two("rt", [128, 1], F32)
        rs = two("rs", [128, 1], F32)
        on = two("on", [128, 128], BF16)

        b0 = C.bank("b0"); b1 = C.bank("b1")
        C.nbank += 1
        b2 = st.enter_context(nc.psum_tensor("b2", [128, 1024], BF16))
        b3 = C.bank("b3"); b4 = C.bank("b4"); b5 = C.bank("b5"); b6 = C.bank("b6"); b7 = C.bank("b7")

        P.dma("sp", cs[:], consts, writes=["cs"], key="c0")
        P.dma("sp", nws[:], normw, writes=["nws"], key="c1")
        P.dma("sp", Gr[0:ntile, :, :], bgT.rearrange("a n p -> n a p"), writes=["Gr"], key="c2")
        C.cp("dve", identb[:], identf, ["cs"], ["identb"])
        C.memset("dve", onesf[:], 1.0, ["onesf"])
        C.memset("dve", S[:], 0.0, ["S"])
        C.memset("dve", Sb[:], 0.0, ["Sb"])
        C.mm(b0[:, 0:ntile], Gr[0:ntile, 0, :], identf[0:ntile, 0:ntile], True, True, ["Gr", "cs"], ["b0"])
        C.mm(b0[:, 128:128 + ntile], Gr[0:ntile, 1, :], identf[0:ntile, 0:ntile], True, True, ["Gr", "cs"], ["b0"])
        C.cp("dve", bcol[:], b0[:, 0:ntile], [], ["b0", "bcol"])
        C.cp("dve", gcol[:], b0[:, 128:128 + ntile], [], ["b0", "gcol"])
        C.mm(b1[:, 0:ntile], mUi, gcol[:], True, True, ["cs", "gcol"], ["b1"])
        C.mm(b1[:, 128:128 + ntile], onesf[:], gcol[:], True, True, ["onesf", "gcol"], ["b1"])
        C.cp("dve", gc[:], b1[:, 0:ntile], [], ["b1", "gc"])
        C.cp("dve", glb[:], b1[:, 128:128 + ntile], [], ["b1", "glb"])
        C.ts("dve", ngc[:], gc[:], -1.0, ALU.mult, ["gc"], ["ngc"])
        C.act(egl[:], glb[:], AF.Exp, ["glb"], ["egl"])
        C.tt("dve", sc2[:], glb[:], gc[:], ALU.subtract, ["glb", "gc"], ["sc2"])
        C.act(sc2[:], sc2[:], AF.Exp, ["sc2"], ["sc2"])
        C.act(sc1[:], gc[:], AF.Exp, ["gc"], ["sc1"])
        C.tt("dve", sc1[:], sc1[:], bcol[:], ALU.mult, ["sc1", "bcol"], ["sc1"])
        scal = ["bcol", "gcol", "ngc", "egl", "sc1", "sc2"]

        def load_block(blk):
            s = blk % 2
            sl = slice(blk * TPB * 128, (blk + 1) * TPB * 128)
            P.dma("sp", qs[s][:], qT[:, sl], writes=["qs%d" % s], key="lq%d" % s)
            P.dma("sp", ks[s][:], kT[:, sl], writes=["ks%d" % s], key="lk%d" % s)
            P.dma("sp", vs[s][:], vT[:, sl], writes=["vs%d" % s], key="lv%d" % s)
            P.dma("sp", zs[s][:], gzT[:, sl], writes=["zs%d" % s], key="lz%d" % s)

        def prep(n):
            s = n % 2
            bs = (n // TPB) % 2
            tl = slice((n % TPB) * 128, (n % TPB) * 128 + 128)
            kn, qn, vn_ = ks[bs][:, tl], qs[bs][:, tl], vs[bs][:, tl]
            nn = slice(n, n + 1)
            T = "_%d" % s
            C.mm(b0[:, 0:128], gcol[:, nn].to_broadcast([128, 128]), mUi, True, True, ["gcol", "cs"], ["b0"]); yield
            C.mm(b0[:, 128:256], bcol[:, nn].to_broadcast([128, 128]), identf, True, True, ["bcol", "cs"], ["b0"]); yield
            C.mm(b1[:, 0:128], kn, kn, True, True, ["ks%d" % bs], ["b1"]); yield
            C.mm(b1[:, 128:256], kn, qn, True, True, ["ks%d" % bs, "qs%d" % bs], ["b1"]); yield
            C.tr(b2[:, 0:128], kn, identb[:], ["ks%d" % bs, "identb"], ["b2"]); yield
            C.tr(b2[:, 128:256], vn_, identb[:], ["vs%d" % bs, "identb"], ["b2"]); yield
            C.act(dabs[s][:], b0[:, 0:128], AF.Abs, ["ngc"], ["b0", "dabs" + T], bias=ngc[:, nn]); yield
            C.act(egB[s][:], b0[:, 0:128], AF.Exp, [], ["b0", "egB" + T]); yield
            C.tt("dve", bU[s][:], b0[:, 128:256], mUs, ALU.mult, ["cs"], ["b0", "bU" + T]); yield
            C.act(W[s][:], dabs[s][:], AF.Exp, ["dabs" + T], ["W" + T], scale=-1.0); yield
            C.tt("dve", KW[s][:], b1[:, 0:128], W[s][:], ALU.mult, ["W" + T], ["b1", "KW" + T]); yield
            C.tt("pool", Wm[s][:], W[s][:], mUi, ALU.mult, ["W" + T, "cs"], ["Wm" + T]); yield
            C.tt("dve", AT[s][:], b1[:, 128:256], Wm[s][:], ALU.mult, ["Wm" + T], ["b1", "AT" + T]); yield
            C.tt("pool", XY[s][0][:, 0:128], KW[s][:], bU[s][:], ALU.mult, ["KW" + T, "bU" + T], ["XY%d_0" % s]); yield
            C.tt("pool", dabs[s][:], KW[s][:], mLs, ALU.mult, ["KW" + T, "cs"], ["dabs" + T]); yield
            C.ts("pool", XY[s][0][:, 128:256], dabs[s][:], bcol[:, nn], ALU.mult, ["dabs" + T, "bcol"], ["XY%d_0" % s]); yield
            C.tt("pool", Pm[s][0][:], identf, XY[s][0][:, 0:128], ALU.subtract, ["cs", "XY%d_0" % s], ["Pm%d_0" % s]); yield
            C.act(kbg[s][:], b2[:, 0:128], AF.Identity, ["sc1"], ["b2", "kbg" + T], scale=sc1[:, nn]); yield
            C.act(kd[s][:], b2[:, 0:128], AF.Identity, ["sc2"], ["b2", "kd" + T], scale=sc2[:, nn]); yield
            C.ts("dve", vb[s][:], b2[:, 128:256], bcol[:, nn], ALU.mult, ["bcol"], ["b2", "vb" + T]); yield
            C.tt("pool", qd[s][:], qn, egB[s][:], ALU.mult, ["qs%d" % bs, "egB" + T], ["qd" + T]); yield
            for k in range(1, 7):
                a, b = (k - 1) % 2, k % 2
                Xp, Yp = XY[s][a][:, 0:128], XY[s][a][:, 128:256]
                ka, kb_ = "XY%d_%d" % (s, a), "XY%d_%d" % (s, b)
                if k < 6:
                    C.mm(b3[:, 0:128], Yp, Xp, True, True, [ka], ["b3"]); yield
                C.mm(b3[:, 128:256], Xp, Yp, True, True, [ka], ["b3"]); yield
                if k < 6:
                    C.cp("act" if k % 2 else "dve", XY[s][b][:], b3[:, 0:256], [], ["b3", kb_]); yield
                else:
                    C.cp("dve", XY[s][b][:, 128:256], b3[:, 128:256], [], ["b3", kb_]); yield
                C.mm(b4[:, 0:128], XY[s][b][:, 128:256], Pm[s][a][:], True, True, [kb_, "Pm%d_%d" % (s, a)], ["b4"]); yield
                if k < 6:
                    C.tt("dve", Pm[s][b][:], b4[:, 0:128], Pm[s][a][:], ALU.add, ["Pm%d_%d" % (s, a)], ["b4", "Pm%d_%d" % (s, b)]); yield
                else:
                    C.tt("dve", Pb[s][:], b4[:, 0:128], Pm[s][a][:], ALU.add, ["Pm%d_%d" % (s, a)], ["b4", "Pb" + T]); yield
            C.mm(b4[:, 128:256], kbg[s][:], Pb[s][:], True, True, ["kbg" + T, "Pb" + T], ["b4"]); yield
            C.ts("dve", nwT[s][:], b4[:, 128:256], -1.0, ALU.mult, [], ["b4", "nwT" + T]); yield

        def rec(n):
            s = n % 2
            bs = (n // TPB) % 2
            tl = slice((n % TPB) * 128, (n % TPB) * 128 + 128)
            nn = slice(n, n + 1)
            T = "_%d" % s
            C.mm(b5[:, 0:128], Pb[s][:], vb[s][:], True, False, ["Pb" + T, "vb" + T], ["b5"]); yield
            C.mm(b5[:, 0:128], nwT[s][:], Sb[:], False, True, ["nwT" + T, "Sb"], ["b5"]); yield
            C.cp("dve", vn[s][:], b5[:, 0:128], [], ["b5", "vn" + T]); yield
            C.mm(b6[:, 0:128], qd[s][:], Sb[:], True, False, ["qd" + T, "Sb"], ["b6"]); yield
            C.mm(b6[:, 0:128], AT[s][:], vn[s][:], False, True, ["AT" + T, "vn" + T], ["b6"]); yield
            C.mm(b7[:, 0:128], kd[s][:], vn[s][:], True, True, ["kd" + T, "vn" + T], ["b7"]); yield
            C.stt("dve", S[:], S[:], egl[:, nn], b7[:, 0:128], ALU.mult, ALU.add, ["S", "egl"], ["b7", "S"]); yield
            C.cp("act", Sb[:], S[:], ["S"], ["Sb"]); yield
            C.act(junk[s][:], b6[:, 0:128], AF.Square, [], ["b6", "junk" + T, "ss" + T], accum=ss[s][:]); yield
            C.act(rt[s][:], ss[s][:], AF.Sqrt, ["ss" + T], ["rt" + T], bias=RMS_EPS, scale=1.0 / 128); yield
            C.recip(rs[s][:], rt[s][:], ["rt" + T], ["rs" + T]); yield
            C.act(on[s][:], b6[:, 0:128], AF.Identity, ["rs" + T], ["b6", "on" + T], scale=rs[s][:]); yield
            C.tr(b2[:, 256:384], on[s][:], identb[:], ["on" + T, "identb"], ["b2"]); yield
            C.stt("dve", osb[bs][:, tl], b2[:, 256:384], nws[:, 0:1], zs[bs][:, tl], ALU.mult, ALU.mult,
                  ["nws", "zs%d" % bs], ["b2", "os%d" % bs]); yield
            if n % TPB == TPB - 1:
                blk = n // TPB
                d = P.dma("sp", oT[:, blk * TPB * 128:(blk + 1) * TPB * 128], osb[bs][:], reads=["os%d" % bs], key="oo%d" % bs)
                P.must_finish(d)

        load_block(0)
        for n in range(ntile + 1):
            if n % TPB == 1 and n // TPB + 1 < nblk:
                load_block(n // TPB + 1)
            g1 = prep(n) if n < ntile else None
            g2 = rec(n - 1) if n >= 1 else None
            _interleave([g1, g2])
        P.emit()
    return nc


def b_consts():
    p = np.arange(128)[:, None]
    f = np.arange(128)[None, :]
    c = np.zeros((128, 4, 128), np.float32)
    c[:, 0] = (p == f)
    c[:, 1] = (f >= p)
    c[:, 2] = (f > p)
    c[:, 3] = (p > f)
    return c


TBC = 1024
FG = 11
NFG = FC // FG


def build_C():
    nc = bass.Bass("TRN2", target_bir_lowering=False)
    with ExitStack() as st:
        C = Ctx(nc, st)
        P = C.P
        catT = C.din("catT", [2048, NT], BF16)
        xTd = C.din("xT", [D_MODEL, NT])
        wout = C.din("wout", [2048, D_MODEL])
        wgur = C.din("wgur", [D_MODEL, 2 * D_FF])
        wd = C.din("wd", [D_FF, D_MODEL])
        lnp = C.din("lnp", [D_MODEL, 4])
        x2T = C.dout("x2T", [D_MODEL, NT])

        xb = C.sb("xb", [128, KC, TBC], F32)
        x1b = C.sb("x1b", [128, KC, TBC], BF16)
        catb = C.sb("catb", [128, KC * TBC], BF16)
        catv = catb[:].rearrange("p (k t) -> p k t", k=KC)
        hidv = catb[:, 0:FG * TBC].rearrange("p (k t) -> p k t", k=FG)
        ws = [C.sb("ws%d" % i, [128, KC, 256], BF16) for i in range(2)]
        wds = [C.sb("wds%d" % i, [128, FG, 256], BF16) for i in range(2)]
        lnps = C.sb("lnps", [128, KC, 4], F32)
        onesb = C.sb("onesb", [128, 128], BF16)
        sg = [C.sb("sg%d" % i, [128, 512], F32) for i in range(2)]
        rb = [C.sb("rb%d" % i, [128, 512], BF16) for i in range(2)]
        sq = [C.sb("sq%d" % i, [128, 512], BF16) for i in range(2)]
        t1 = [C.sb("t1_%d" % i, [128, 512], F32) for i in range(2)]
        t2 = [C.sb("t2_%d" % i, [128, 512], F32) for i in range(2)]
        mean = [C.sb("mean%d" % i, [128, 512], F32) for i in range(2)]
        rstd = [C.sb("rstd%d" % i, [128, 512], F32) for i in range(2)]
        tmpa = C.sb("tmpa", [128, 512], F32)
        pb = [C.bank("pb%d" % i) for i in range(6)]
        pst = [C.bank("pst%d" % i) for i in range(2)]

        P.dma("sp", lnps[:], lnp.rearrange("(k p) j -> p k j", p=128), writes=["lnps"], key="c0")
        C.memset("dve", onesb[:], 1.0, ["onesb"])
        state = {"ws": 0, "wds": 0, "pb": 0}
        wov = wout.rearrange("(k p) c -> p k c", p=128)
        wgv = wgur.rearrange("(k p) c -> p k c", p=128)
        wdv = wd.rearrange("(k p) c -> p k c", p=128)
        xv = xTd.rearrange("(k p) t -> p k t", p=128)
        cv = catT.rearrange("(k p) t -> p k t", p=128)
        ov = x2T.rearrange("(k p) t -> p k t", p=128)
        xkeys = ["xb%d" % k for k in range(KC)]

        def load_ws(src, col0):
            s = state["ws"]
            state["ws"] ^= 1
            for h in range(4):
                P.dma("pool", ws[s][:, 4 * h:4 * h + 4, :], src[:, 4 * h:4 * h + 4, col0:col0 + 256], writes=["ws%d" % s], key="ws%d" % s)
            return s

        def nbank():
            b = state["pb"]
            state["pb"] = (b + 1) % 6
            return b

        def layer_norm(blk, gi, bi, make_bf16):
            for hf in range(2):
                hs = slice(512 * hf, 512 * hf + 512)
                for k in range(KC):
                    q = k % 2
                    C.cp("act", rb[q][:], xb[:, k, hs], ["xb%d" % k], ["rb%d" % q])
                    C.tt("pool", sq[q][:], xb[:, k, hs], xb[:, k, hs], ALU.mult, ["xb%d" % k], ["sq%d" % q])
                    C.mm(pst[0][:], onesb[:], rb[q][:], k == 0, k == KC - 1, ["onesb", "rb%d" % q], ["pst0"])
                    C.mm(pst[1][:], onesb[:], sq[q][:], k == 0, k == KC - 1, ["onesb", "sq%d" % q], ["pst1"])
                C.ts("dve", mean[hf][:], pst[0][:], 1.0 / D_MODEL, ALU.mult, [], ["pst0", "mean%d" % hf])
                C.tt("dve", tmpa[:], mean[hf][:], mean[hf][:], ALU.mult, ["mean%d" % hf], ["tmpa"])
                C.stt("dve", tmpa[:], pst[1][:], 1.0 / D_MODEL, tmpa[:], ALU.mult, ALU.subtract, ["tmpa"], ["pst1", "tmpa"])
                C.act(tmpa[:], tmpa[:], AF.Sqrt, ["tmpa"], ["tmpa"], bias=LN_EPS)
                C.recip(rstd[hf][:], tmpa[:], ["tmpa"], ["rstd%d" % hf])
                for k in range(KC):
                    q = k % 2
                    C.tt("dve", t1[q][:], xb[:, k, hs], mean[hf][:], ALU.subtract, ["xb%d" % k, "mean%d" % hf], ["t1_%d" % q])
                    C.tt("pool", t2[q][:], t1[q][:], rstd[hf][:], ALU.mult, ["t1_%d" % q, "rstd%d" % hf], ["t2_%d" % q])
                    C.act(xb[:, k, hs], t2[q][:], AF.Identity, ["t2_%d" % q, "lnps"], ["xb%d" % k],
                          bias=lnps[:, k, bi:bi + 1], scale=lnps[:, k, gi:gi + 1])
                    if make_bf16:
                        C.cp("dve", x1b[:, k, hs], xb[:, k, hs], ["xb%d" % k], ["x1b%d" % k])

        for blk in range(NT // TBC):
            ts_ = slice(blk * TBC, (blk + 1) * TBC)
            for k in range(KC):
                P.dma("sp", xb[:, k, :], xv[:, k, ts_], writes=["xb%d" % k], key="lx%d" % k)
            for k in range(KC):
                P.dma("sp", catv[:, k, :], cv[:, k, ts_], writes=["catbuf"], key="lc")
            for cg in range(8):
                s = load_ws(wov, 256 * cg)
                for half in range(2):
                    c = 2 * cg + half
                    for hf in range(2):
                        hs = slice(512 * hf, 512 * hf + 512)
                        b = nbank()
                        for m in range(KC):
                            C.mm(pb[b][:], ws[s][:, m, 128 * half:128 * half + 128], catv[:, m, hs], m == 0, m == KC - 1,
                                 ["ws%d" % s, "catbuf"], ["pb%d" % b])
                        C.stt("dve", xb[:, c, hs], xb[:, c, hs], ALPHA, pb[b][:], ALU.mult, ALU.add, ["xb%d" % c], ["pb%d" % b, "xb%d" % c])
            layer_norm(blk, 0, 1, True)
            for fg in range(NFG):
                for fi in range(FG):
                    f = fg * FG + fi
                    s = load_ws(wgv, 256 * f)
                    for hf in range(2):
                        hs = slice(512 * hf, 512 * hf + 512)
                        bg_ = nbank()
                        for k in range(KC):
                            C.mm(pb[bg_][:], ws[s][:, k, 0:128], x1b[:, k, hs], k == 0, k == KC - 1, ["ws%d" % s, "x1b%d" % k], ["pb%d" % bg_])
                        bu = nbank()
                        for k in range(KC):
                            C.mm(pb[bu][:], ws[s][:, k, 128:256], x1b[:, k, hs], k == 0, k == KC - 1, ["ws%d" % s, "x1b%d" % k], ["pb%d" % bu])
                        q = hf
                        C.act(sg[q][:], pb[bg_][:], AF.Silu, [], ["pb%d" % bg_, "sg%d" % q])
                        C.tt("dve", hidv[:, fi, hs], pb[bu][:], sg[q][:], ALU.mult, ["sg%d" % q], ["pb%d" % bu, "catbuf"])
                for cg in range(8):
                    s = state["wds"]
                    state["wds"] ^= 1
                    P.dma("pool", wds[s][:], wdv[:, fg * FG:(fg + 1) * FG, 256 * cg:256 * cg + 256], writes=["wds%d" % s], key="wds%d" % s)
                    for half in range(2):
                        c = 2 * cg + half
                        for hf in range(2):
                            hs = slice(512 * hf, 512 * hf + 512)
                            b = nbank()
                            for fi in range(FG):
                                C.mm(pb[b][:], wds[s][:, fi, 128 * half:128 * half + 128], hidv[:, fi, hs], fi == 0, fi == FG - 1,
                                     ["wds%d" % s, "catbuf"], ["pb%d" % b])
                            if fg == 0:
                                C.stt("dve", xb[:, c, hs], xb[:, c, hs], ALPHA, pb[b][:], ALU.mult, ALU.add, ["xb%d" % c], ["pb%d" % b, "xb%d" % c])
                            else:
                                C.tt("dve", xb[:, c, hs], xb[:, c, hs], pb[b][:], ALU.add, ["xb%d" % c], ["pb%d" % b, "xb%d" % c])
            layer_norm(blk, 2, 3, False)
            for k in range(KC):
                d = P.dma("sp", ov[:, k, ts_], xb[:, k, :], reads=["xb%d" % k], key="ox%d" % k)
                P.must_finish(d)
        P.emit()
    return nc


def host_inputs_C(w_out, w_gate_up, w_down, ln1_g, ln1_b, ln2_g, ln2_b):
    perm = []
    for f in range(FC):
        perm += list(range(128 * f, 128 * f + 128)) + list(range(D_FF + 128 * f, D_FF + 128 * f + 128))
    wgur = np.ascontiguousarray(w_gate_up[:, perm])
    lnp = np.ascontiguousarray(np.stack([ln1_g, ln1_b, ln2_g, ln2_b], axis=1))
    return {"wout": np.ascontiguousarray(w_out), "wgur": wgur, "wd": np.ascontiguousarray(w_down), "lnp": lnp}


_PROGS = {}


def _prog(name):
    if name not in _PROGS:
        _PROGS[name] = {"A": build_A, "B": build_B, "C": build_C}[name]()
    return _PROGS[name]


def _run(name, maps):
    res = run_bass_kernel_spmd(_prog(name), maps, core_ids=list(range(NCORES)))
    return res.results


def kernel(x, w_in, conv_dw_w, conv_dw_b, conv_ln_g, conv_ln_b, gdn_conv_w, gdn_A_log,
           gdn_dt_bias, gdn_norm_w, w_out, ln1_g, ln1_b, w_gate_up, w_down, ln2_g, ln2_b):
    f = lambda a: np.asarray(a, dtype=np.float32)
    xcur = f(x)[0]
    xT_cores = [np.ascontiguousarray(xcur[c * NT:(c + 1) * NT].T) for c in range(NCORES)]
    consts = b_consts()
    for l in range(DEPTH):
        mapsA = host_inputs_A_T(xT_cores, f(w_in[l]), f(conv_dw_w[l]), f(conv_dw_b[l]), f(conv_ln_g[l]), f(conv_ln_b[l]),
                                f(gdn_conv_w[l]), f(gdn_A_log[l]), f(gdn_dt_bias[l]))
        ra = _run("A", mapsA)
        mapsB = []
        for h in range(NHEAD):
            cat = lambda nm, r0: np.ascontiguousarray(np.concatenate([np.asarray(ra[c][nm])[r0:r0 + 128] for c in range(NCORES)], axis=1))
            beta = np.concatenate([np.asarray(ra[c]["bg"])[h] for c in range(NCORES)])
            g = np.concatenate([np.asarray(ra[c]["bg"])[NHEAD + h] for c in range(NCORES)])
            mapsB.append({"qT": cat("qkvT", 128 * h), "kT": cat("qkvT", 1024 + 128 * h), "vT": cat("qkvT", 2048 + 128 * h),
                          "gzT": cat("gzT", 128 * h),
                          "bgT": np.ascontiguousarray(np.stack([beta.reshape(NTILE, 128), g.reshape(NTILE, 128)])),
                          "consts": consts, "normw": np.ascontiguousarray(f(gdn_norm_w[l]).reshape(128, 1))})
        rb = _run("B", mapsB)
        wc = host_inputs_C(f(w_out[l]), f(w_gate_up[l]), f(w_down[l]), f(ln1_g[l]), f(ln1_b[l]), f(ln2_g[l]), f(ln2_b[l]))
        mapsC = []
        for c in range(NCORES):
            catT = np.concatenate([np.asarray(ra[c]["convT"])] + [np.asarray(rb[h]["oT"])[:, c * NT:(c + 1) * NT] for h in range(NHEAD)], axis=0)
            m = dict(wc)
            m["catT"] = np.ascontiguousarray(catT)
            m["xT"] = xT_cores[c]
            mapsC.append(m)
        rc = _run("C", mapsC)
        xT_cores = [np.ascontiguousarray(np.asarray(rc[c]["x2T"])) for c in range(NCORES)]
    out = np.concatenate([xT_cores[c].T for c in range(NCORES)], axis=0)
    return np.ascontiguousarray(out[None]).astype(np.float32)


def host_inputs_A_T(xT_cores, w_in, conv_dw_w, conv_dw_b, conv_ln_g, conv_ln_b, gdn_conv_w, A_log, dt_bias):
    perm = []
    for i in range(8):
        perm += list(range(128 * i, 128 * i + 128)) + list(range(1024 + 128 * i, 1024 + 128 * i + 128))
    perm += list(range(2048, IN_COLS))
    winr = np.ascontiguousarray(w_in[:, perm])
    cdw = np.ascontiguousarray(conv_dw_w.T)
    cvec = np.ascontiguousarray(np.stack([conv_dw_b, conv_ln_g, conv_ln_b], axis=1))
    gcw = np.ascontiguousarray(gdn_conv_w.T)
    hv = np.ascontiguousarray(np.stack([A_log, dt_bias], axis=1))
    ident = np.eye(128, dtype=np.float32)
    maps = []
    for c in range(len(xT_cores)):
        xt = np.zeros((D_MODEL, NTX), np.float32)
        xt[:, HALO:] = xT_cores[c]
        if c > 0:
            xt[:, :HALO] = xT_cores[c - 1][:, NT - HALO:]
        maps.append({"xT": xt, "winr": winr, "cdw": cdw, "cvec": cvec, "gcw": gcw, "hv": hv, "ident": ident})
    return maps
```

```python
import numpy as np
from contextlib import ExitStack
import concourse.bass as bass
import concourse.mybir as mybir
from concourse.bass_utils import run_bass_kernel_spmd

F32 = mybir.dt.float32
BF16 = mybir.dt.bfloat16
ALU = mybir.AluOpType
AF = mybir.ActivationFunctionType

ENGS = ("pe", "act", "dve", "pool", "sp")

D_MODEL = 2048
SEQ = 16384
DEPTH = 2
NCORES = 8
NT = SEQ // NCORES
HALO = 32
NTX = NT + HALO
KC = D_MODEL // 128
CONV_CH = 1024
GDN_W = 1024
NHEAD = 8
D_FF = 5632
FC = D_FF // 128
IN_COLS = 6160
ALPHA = (2 * DEPTH) ** 0.25
LN_EPS = 1e-5
RMS_EPS = 1e-6
L2_EPS = 1e-6


class _Op:
    __slots__ = ("eng", "fn", "deps", "is_dma", "semkey", "signal", "count", "inc")

    def __init__(self, eng, fn, is_dma, semkey, inc=16):
        self.inc = inc
        self.eng = eng
        self.fn = fn
        self.deps = []
        self.is_dma = is_dma
        self.semkey = semkey
        self.signal = False
        self.count = 0


class Prog:
    def __init__(self, nc):
        self.nc = nc
        self.ops = {e: [] for e in ENGS}
        self.last_w = {}
        self.readers = {}
        self.final_waits = []

    def _add(self, eng, fn, reads, writes, is_dma=False, semkey=None, inc=16):
        op = _Op(eng, fn, is_dma, semkey, inc)
        deps = {}
        for r in reads:
            w = self.last_w.get(r)
            if w is not None:
                deps[id(w)] = (w, "raw")
        for wkey in writes:
            for rd in self.readers.get(wkey, ()):
                if id(rd) not in deps:
                    deps[id(rd)] = (rd, "war")
            w = self.last_w.get(wkey)
            if w is not None and id(w) not in deps:
                deps[id(w)] = (w, "waw")
        for r in reads:
            self.readers.setdefault(r, []).append(op)
        for wkey in writes:
            self.last_w[wkey] = op
            self.readers[wkey] = []
        for (d, kind) in deps.values():
            if d is op:
                continue
            if (not d.is_dma) and (not is_dma) and d.eng == eng:
                if eng == "pe" or kind != "raw":
                    continue
            op.deps.append(d)
            d.signal = True
        self.ops[eng].append(op)
        return op

    def op(self, eng, fn, reads=(), writes=()):
        return self._add(eng, fn, tuple(reads), tuple(writes))

    def dma(self, eng, out, in_, reads=(), writes=(), key=None):
        fn = lambda e: e.dma_start(out=out, in_=in_)
        return self._add(eng, fn, tuple(reads), tuple(writes), is_dma=True, semkey=key)

    def collective(self, kind, ins, outs, reads, writes, key):
        rg = [list(range(NCORES))]
        fn = lambda e: e.collective_compute(kind, ALU.bypass, replica_groups=rg, ins=[a.opt() for a in ins], outs=[a.opt() for a in outs])
        return self._add("pool", fn, tuple(reads), tuple(writes), is_dma=True, semkey=key, inc=1)

    def must_finish(self, op):
        op.signal = True
        self.final_waits.append(op)

    def emit(self):
        nc = self.nc
        keycount = {}
        for e in ENGS:
            c = 0
            for op in self.ops[e]:
                if op.is_dma:
                    k = op.semkey
                    keycount[k] = keycount.get(k, 0) + op.inc
                    op.count = keycount[k]
                elif op.signal:
                    c += 1
                    op.count = c
        with ExitStack() as st:
            esem = {e: st.enter_context(nc.semaphore("s_" + e)) for e in ENGS}
            ksem = {k: st.enter_context(nc.semaphore("k_" + str(k))) for k in keycount}
            block = st.enter_context(nc.Block())

            def run(engname, eh):
                waited = {}
                for op in self.ops[engname]:
                    need = {}
                    for d in op.deps:
                        s = ("k", d.semkey) if d.is_dma else ("e", d.eng)
                        if d.count > need.get(s, 0):
                            need[s] = d.count
                    for s, v in need.items():
                        if waited.get(s, 0) >= v:
                            continue
                        sem = ksem[s[1]] if s[0] == "k" else esem[s[1]]
                        eh.wait_ge(sem, v)
                        waited[s] = v
                    ins = op.fn(eh)
                    if op.is_dma:
                        ins.then_inc(ksem[op.semkey], op.inc)
                    elif op.signal:
                        ins.then_inc(esem[engname], 1)
                if engname == "sp":
                    for op in self.final_waits:
                        sem = ksem[op.semkey] if op.is_dma else esem[op.eng]
                        eh.wait_ge(sem, op.count)

            block.tensor(lambda eh: run("pe", eh))
            block.scalar(lambda eh: run("act", eh))
            block.vector(lambda eh: run("dve", eh))
            block.gpsimd(lambda eh: run("pool", eh))
            block.sync(lambda eh: run("sp", eh))


class Ctx:
    def __init__(self, nc, st):
        self.nc = nc
        self.st = st
        self.P = Prog(nc)
        self.nbank = 0

    def sb(self, name, shape, dt):
        return self.st.enter_context(self.nc.sbuf_tensor(name, list(shape), dt))

    def bank(self, name):
        self.nbank += 1
        assert self.nbank <= 8
        return self.st.enter_context(self.nc.psum_tensor(name, [128, 512], F32))

    def din(self, name, shape, dt=F32):
        return self.nc.dram_tensor(name, list(shape), dt, kind="ExternalInput").ap()

    def dout(self, name, shape, dt=F32):
        return self.nc.dram_tensor(name, list(shape), dt, kind="ExternalOutput").ap()

    def mm(self, out, lhsT, rhs, start, stop, r, w):
        self.P.op("pe", lambda e: e.matmul(out, lhsT=lhsT, rhs=rhs, start=start, stop=stop), r, w)

    def tr(self, out, in_, ident, r, w):
        self.P.op("pe", lambda e: e.transpose(out, in_, ident), r, w)

    def act(self, out, in_, func, r, w, bias=None, scale=None, accum=None, eng="act"):
        kw = {}
        if bias is not None:
            kw["bias"] = bias
        if scale is not None:
            kw["scale"] = scale
        if accum is not None:
            kw["accum_out"] = accum
        self.P.op(eng, lambda e: e.activation(out=out, in_=in_, func=func, **kw), r, w)

    def tt(self, eng, out, in0, in1, op, r, w):
        self.P.op(eng, lambda e: e.tensor_tensor(out=out, in0=in0, in1=in1, op=op), r, w)

    def ts(self, eng, out, in0, s1, op0, r, w, s2=None, op1=None):
        if op1 is None:
            self.P.op(eng, lambda e: e.tensor_scalar(out=out, in0=in0, scalar1=s1, scalar2=None, op0=op0), r, w)
        else:
            self.P.op(eng, lambda e: e.tensor_scalar(out=out, in0=in0, scalar1=s1, scalar2=s2, op0=op0, op1=op1), r, w)

    def stt(self, eng, out, in0, scalar, in1, op0, op1, r, w):
        self.P.op(eng, lambda e: e.scalar_tensor_tensor(out=out, in0=in0, scalar=scalar, in1=in1, op0=op0, op1=op1), r, w)

    def cp(self, eng, out, in_, r, w):
        if eng == "act":
            self.P.op(eng, lambda e: e.copy(out=out, in_=in_), r, w)
        else:
            self.P.op(eng, lambda e: e.tensor_copy(out=out, in_=in_), r, w)

    def recip(self, out, in_, r, w):
        self.P.op("dve", lambda e: e.reciprocal(out=out, in_=in_), r, w)

    def memset(self, eng, ap, val, w):
        self.P.op(eng, lambda e: e.memset(ap, val), (), w)


def build_A():
    nc = bass.Bass("TRN2", target_bir_lowering=False)
    with ExitStack() as st:
        C = Ctx(nc, st)
        P = C.P
        xT = C.din("xT", [D_MODEL, NTX])
        winr = C.din("winr", [D_MODEL, IN_COLS])
        cdw = C.din("cdw", [CONV_CH, 31])
        cvec = C.din("cvec", [CONV_CH, 3])
        gcw = C.din("gcw", [3 * GDN_W, 4])
        hv = C.din("hv", [NHEAD, 2])
        identd = C.din("ident", [128, 128])
        convT = C.dout("convT", [CONV_CH, NT], BF16)
        qkvT = C.dout("qkvT", [3 * GDN_W, NT], BF16)
        gzT = C.dout("gzT", [GDN_W, NT], BF16)
        bgo = C.dout("bg", [2 * NHEAD, NT], F32)

        xs = C.sb("xs", [128, KC, NTX], BF16)
        wb = [C.sb("wb%d" % i, [128, KC, 256], BF16) for i in range(2)]
        wsm = C.sb("wsm", [128, KC, 16], BF16)
        identf = C.sb("identf", [128, 128], F32)
        identb = C.sb("identb", [128, 128], BF16)
        onesb = C.sb("onesb", [128, 128], BF16)
        cdws = C.sb("cdws", [128, 8, 31], F32)
        cvecs = C.sb("cvecs", [128, 8, 3], F32)
        gcws = C.sb("gcws", [128, 24, 4], F32)
        hvs = C.sb("hvs", [NHEAD, 2], F32)
        negA = C.sb("negA", [NHEAD, 1], F32)
        Dc = [C.sb("Dc%d" % i, [128, 31, 128], BF16) for i in range(2)]
        Dq = [C.sb("Dq%d" % i, [128, 4, 128], BF16) for i in range(2)]
        hbuf = [C.sb("hbuf%d" % i, [128, NTX], BF16) for i in range(2)]
        ubuf = [C.sb("ubuf%d" % i, [128, NTX], BF16) for i in range(2)]
        ybuf = C.sb("ybuf", [128, 8, NT], BF16)
        sg = [C.sb("sg%d" % i, [128, 512], F32) for i in range(2)]
        ysq = [C.sb("ysq%d" % i, [128, 512], BF16) for i in range(2)]
        ost = [C.sb("ost%d" % i, [128, NT], BF16) for i in range(3)]
        mean = C.sb("mean", [128, 512], F32)
        rstd = C.sb("rstd", [128, 512], F32)
        tmpa = C.sb("tmpa", [128, 512], F32)
        t1 = [C.sb("t1_%d" % i, [128, 512], F32) for i in range(2)]
        t2 = [C.sb("t2_%d" % i, [128, 512], F32) for i in range(2)]
        sv = [C.sb("sv%d" % i, [128, 512], F32) for i in range(2)]
        bgs = C.sb("bgs", [NHEAD, 2, NT], F32)
        sp1 = C.sb("sp1", [NHEAD, 512], F32)
        sp2 = C.sb("sp2", [NHEAD, 512], F32)

        pu = [C.bank("pu%d" % i) for i in range(4)]
        pc = [C.bank("pc%d" % i) for i in range(2)]
        pst = [C.bank("pst%d" % i) for i in range(2)]

        xv = xT.rearrange("(k p) t -> p k t", p=128)
        for k in range(KC):
            P.dma("pool", xs[:, k, :], xv[:, k, :], writes=["xs%d" % k], key="xs%d" % k)
        P.dma("sp", identf[:], identd, writes=["identf"], key="c0")
        P.dma("sp", cdws[:], cdw.rearrange("(i p) j -> p i j", p=128), writes=["cdws"], key="c1")
        P.dma("sp", cvecs[:], cvec.rearrange("(i p) j -> p i j", p=128), writes=["cvecs"], key="c2")
        P.dma("sp", gcws[:], gcw.rearrange("(i p) j -> p i j", p=128), writes=["gcws"], key="c3")
        P.dma("sp", hvs[:], hv, writes=["hvs"], key="c4")
        C.cp("dve", identb[:], identf[:], ["identf"], ["identb"])
        C.memset("dve", onesb[:], 1.0, ["onesb"])
        C.act(negA[:], hvs[:, 0:1], AF.Exp, ["hvs"], ["negA"])
        C.ts("dve", negA[:], negA[:], -1.0, ALU.mult, ["negA"], ["negA"])

        xs_keys = ["xs%d" % k for k in range(KC)]
        wv = winr.rearrange("(k p) c -> p k c", p=128)
        blocks = [(0, HALO)] + [(HALO + 512 * i, 512) for i in range(4)]

        state = {"wslot": 0, "pu": 0, "pc": 0, "ost": 0, "ngrp": 0}

        def load_w(col0):
            s = state["wslot"]
            state["wslot"] ^= 1
            for h in range(4):
                P.dma("pool", wb[s][:, 4 * h:4 * h + 4, :], wv[:, 4 * h:4 * h + 4, col0:col0 + 256],
                      writes=["wb%d" % s], key="wb%d" % s)
            return s

        def inproj(s, off, c0, n):
            b = state["pu"]
            state["pu"] = (b + 1) % 4
            for k in range(KC):
                C.mm(pu[b][:, 0:n], wb[s][:, k, off:off + 128], xs[:, k, c0:c0 + n], k == 0, k == KC - 1,
                     ["wb%d" % s, "xs%d" % k], ["pu%d" % b])
            return b

        def next_ost():
            o = state["ost"]
            state["ost"] = (o + 1) % 3
            return o

        for i in range(8):
            s = load_w(256 * i)
            hs = i % 2
            P.op("pool", lambda e, hs=hs, i=i: e.tensor_tensor(
                out=Dc[hs][:], in0=identb[:].unsqueeze(1).to_broadcast([128, 31, 128]),
                in1=cdws[:, i, :].unsqueeze(2).to_broadcast([128, 31, 128]), op=ALU.mult),
                ["identb", "cdws"], ["Dc%d" % hs])
            for bi, (c0, n) in enumerate(blocks):
                ba = inproj(s, 0, c0, n)
                bb = inproj(s, 128, c0, n)
                q = (ba // 2) % 2
                C.act(sg[q][:, 0:n], pu[bb][:, 0:n], AF.Sigmoid, [], ["pu%d" % bb, "sg%d" % q])
                C.tt("dve", hbuf[hs][:, c0:c0 + n], pu[ba][:, 0:n], sg[q][:, 0:n], ALU.mult,
                     ["sg%d" % q], ["pu%d" % ba, "hbuf%d_%d" % (hs, bi)])
            for tb in range(4):
                b = state["pc"]
                state["pc"] ^= 1
                base = HALO + 512 * tb - 30
                for j in range(31):
                    C.mm(pc[b][:], Dc[hs][:, j, :], hbuf[hs][:, base + j:base + j + 512], j == 0, j == 30,
                         ["Dc%d" % hs, "hbuf%d_%d" % (hs, tb), "hbuf%d_%d" % (hs, tb + 1)], ["pc%d" % b])
                C.act(ybuf[:, i, 512 * tb:512 * tb + 512], pc[b][:], AF.Identity, ["cvecs"], ["pc%d" % b, "y%d_%d" % (i, tb)],
                      bias=cvecs[:, i, 0:1])

        for tb in range(4):
            tsl = slice(512 * tb, 512 * tb + 512)
            for i in range(8):
                C.mm(pst[0][:], onesb[:], ybuf[:, i, tsl], i == 0, i == 7, ["onesb", "y%d_%d" % (i, tb)], ["pst0"])
            for i in range(8):
                q = i % 2
                C.act(ysq[q][:], ybuf[:, i, tsl], AF.Square, ["y%d_%d" % (i, tb)], ["ysq%d" % q])
                C.mm(pst[1][:], onesb[:], ysq[q][:], i == 0, i == 7, ["onesb", "ysq%d" % q], ["pst1"])
            C.ts("dve", mean[:], pst[0][:], 1.0 / CONV_CH, ALU.mult, [], ["pst0", "mean"])
            C.tt("dve", tmpa[:], mean[:], mean[:], ALU.mult, ["mean"], ["tmpa"])
            C.stt("dve", tmpa[:], pst[1][:], 1.0 / CONV_CH, tmpa[:], ALU.mult, ALU.subtract, ["tmpa"], ["pst1", "tmpa"])
            C.act(tmpa[:], tmpa[:], AF.Sqrt, ["tmpa"], ["tmpa"], bias=LN_EPS)
            C.recip(rstd[:], tmpa[:], ["tmpa"], ["rstd"])
            for i in range(8):
                q = i % 2
                C.tt("dve", t1[q][:], ybuf[:, i, tsl], mean[:], ALU.subtract, ["y%d_%d" % (i, tb), "mean"], ["t1_%d" % q])
                C.tt("pool", t2[q][:], t1[q][:], rstd[:], ALU.mult, ["t1_%d" % q, "rstd"], ["t2_%d" % q])
                C.act(ysq[q][:], t2[q][:], AF.Silu, ["t2_%d" % q, "cvecs"], ["ysq%d" % q],
                      bias=cvecs[:, i, 2:3], scale=cvecs[:, i, 1:2])
                o = P.dma("sp", convT[128 * i:128 * i + 128, tsl], ysq[q][:], reads=["ysq%d" % q], key="oc%d" % q)
                P.must_finish(o)

        for g in range(12):
            s = load_w(2048 + 256 * g)
            for half in range(2):
                ch = 2 * g + half
                us = ch % 2
                P.op("pool", lambda e, us=us, ch=ch: e.tensor_tensor(
                    out=Dq[us][:], in0=identb[:].unsqueeze(1).to_broadcast([128, 4, 128]),
                    in1=gcws[:, ch, :].unsqueeze(2).to_broadcast([128, 4, 128]), op=ALU.mult),
                    ["identb", "gcws"], ["Dq%d" % us])
                for bi, (c0, n) in enumerate(blocks):
                    b = inproj(s, 128 * half, c0, n)
                    C.cp("act" if bi % 2 == 0 else "dve", ubuf[us][:, c0:c0 + n], pu[b][:, 0:n], [],
                         ["pu%d" % b, "ubuf%d_%d" % (us, bi)])
                o = next_ost()
                for tb in range(4):
                    tsl = slice(512 * tb, 512 * tb + 512)
                    b = state["pc"]
                    state["pc"] ^= 1
                    base = HALO + 512 * tb - 3
                    for j in range(4):
                        C.mm(pc[b][:], Dq[us][:, j, :], ubuf[us][:, base + j:base + j + 512], j == 0, j == 3,
                             ["Dq%d" % us, "ubuf%d_%d" % (us, tb), "ubuf%d_%d" % (us, tb + 1)], ["pc%d" % b])
                    if ch >= 16:
                        C.act(ost[o][:, tsl], pc[b][:], AF.Silu, [], ["pc%d" % b, "ost%d" % o])
                    else:
                        q = tb % 2
                        C.act(sv[q][:], pc[b][:], AF.Silu, [], ["pc%d" % b, "sv%d" % q])
                        C.tt("pool", ysq[q][:], sv[q][:], sv[q][:], ALU.mult, ["sv%d" % q], ["ysq%d" % q])
                        C.mm(pst[q][:], onesb[:], ysq[q][:], True, True, ["onesb", "ysq%d" % q], ["pst%d" % q])
                        C.act(t1[q][:], pst[q][:], AF.Sqrt, [], ["pst%d" % q, "t1_%d" % q], bias=L2_EPS)
                        C.recip(t2[q][:], t1[q][:], ["t1_%d" % q], ["t2_%d" % q])
                        if ch < 8:
                            C.stt("dve", ost[o][:, tsl], sv[q][:], 128.0 ** -0.5, t2[q][:], ALU.mult, ALU.mult,
                                  ["sv%d" % q, "t2_%d" % q], ["ost%d" % o])
                        else:
                            C.tt("dve", ost[o][:, tsl], sv[q][:], t2[q][:], ALU.mult, ["sv%d" % q, "t2_%d" % q], ["ost%d" % o])
                d = P.dma("sp", qkvT[128 * ch:128 * ch + 128, :], ost[o][:], reads=["ost%d" % o], key="oq%d" % o)
                P.must_finish(d)

        for g in range(4):
            s = load_w(5120 + 256 * g)
            for half in range(2):
                ch = 2 * g + half
                o = next_ost()
                for (c0, n) in blocks[1:]:
                    b = inproj(s, 128 * half, c0, n)
                    C.act(ost[o][:, c0 - HALO:c0 - HALO + n], pu[b][:, 0:n], AF.Silu, [], ["pu%d" % b, "ost%d" % o])
                d = P.dma("sp", gzT[128 * ch:128 * ch + 128, :], ost[o][:], reads=["ost%d" % o], key="oq%d" % o)
                P.must_finish(d)

        P.dma("pool", wsm[:], wv[:, :, 6144:6160], writes=["wsm"], key="wsm")
        for (c0, n) in blocks[1:]:
            tsl = slice(c0 - HALO, c0 - HALO + n)
            b = state["pu"]
            state["pu"] = (b + 1) % 4
            for k in range(KC):
                C.mm(pu[b][0:NHEAD, 0:n], wsm[:, k, 0:8], xs[:, k, c0:c0 + n], k == 0, k == KC - 1,
                     ["wsm", "xs%d" % k], ["pu%d" % b])
            C.act(bgs[:, 0, tsl], pu[b][0:NHEAD, 0:n], AF.Sigmoid, [], ["pu%d" % b, "bgs0"])
            b = state["pu"]
            state["pu"] = (b + 1) % 4
            for k in range(KC):
                C.mm(pu[b][0:NHEAD, 0:n], wsm[:, k, 8:16], xs[:, k, c0:c0 + n], k == 0, k == KC - 1,
                     ["wsm", "xs%d" % k], ["pu%d" % b])
            C.act(sp1[:, 0:n], pu[b][0:NHEAD, 0:n], AF.Abs, ["hvs"], ["pu%d" % b, "sp1"], bias=hvs[:, 1:2])
            C.act(sp1[:, 0:n], sp1[:, 0:n], AF.Exp, ["sp1"], ["sp1"], scale=-1.0)
            C.act(sp1[:, 0:n], sp1[:, 0:n], AF.Ln, ["sp1"], ["sp1"], bias=1.0)
            C.ts("dve", sp2[:, 0:n], pu[b][0:NHEAD, 0:n], hvs[:, 1:2], ALU.add, ["hvs"], ["pu%d" % b, "sp2"], s2=0.0, op1=ALU.max)
            C.tt("dve", sp2[:, 0:n], sp2[:, 0:n], sp1[:, 0:n], ALU.add, ["sp1", "sp2"], ["sp2"])
            C.ts("dve", bgs[:, 1, tsl], sp2[:, 0:n], negA[:, 0:1], ALU.mult, ["sp2", "negA"], ["bgs1"])
        d = P.dma("sp", bgo.rearrange("(a h) t -> h a t", a=2), bgs[:], reads=["bgs0", "bgs1"], key="obg")
        P.must_finish(d)

        P.emit()
    return nc


def host_inputs_A(x_full, w_in, conv_dw_w, conv_dw_b, conv_ln_g, conv_ln_b, gdn_conv_w, A_log, dt_bias):
    perm = []
    for i in range(8):
        perm += list(range(128 * i, 128 * i + 128)) + list(range(1024 + 128 * i, 1024 + 128 * i + 128))
    perm += list(range(2048, IN_COLS))
    winr = np.ascontiguousarray(w_in[:, perm])
    cdw = np.ascontiguousarray(conv_dw_w.T)
    cvec = np.ascontiguousarray(np.stack([conv_dw_b, conv_ln_g, conv_ln_b], axis=1))
    gcw = np.ascontiguousarray(gdn_conv_w.T)
    hv = np.ascontiguousarray(np.stack([A_log, dt_bias], axis=1))
    ident = np.eye(128, dtype=np.float32)
    ncore = x_full.shape[0] // NT
    maps = []
    for c in range(ncore):
        xt = np.zeros((D_MODEL, NTX), np.float32)
        xt[:, HALO:] = x_full[c * NT:(c + 1) * NT].T
        if c > 0:
            xt[:, :HALO] = x_full[c * NT - HALO:c * NT].T
        maps.append({"xT": xt, "winr": winr, "cdw": cdw, "cvec": cvec, "gcw": gcw, "hv": hv, "ident": ident})
    return maps


NTILE = SEQ // 128
TPB = 16


def _interleave(gens):
    gens = [g for g in gens if g is not None]
    while gens:
        nxt = []
        for g in gens:
            try:
                next(g)
                nxt.append(g)
            except StopIteration:
                pass
        gens = nxt


def build_B(ntile=NTILE):
    nc = bass.Bass("TRN2", target_bir_lowering=False)
    ntok = ntile * 128
    nblk = ntile // TPB
    with ExitStack() as st:
        C = Ctx(nc, st)
        P = C.P
        qT = C.din("qT", [128, ntok], BF16)
        kT = C.din("kT", [128, ntok], BF16)
        vT = C.din("vT", [128, ntok], BF16)
        gzT = C.din("gzT", [128, ntok], BF16)
        bgT = C.din("bgT", [2, ntile, 128])
        consts = C.din("consts", [128, 4, 128])
        normw = C.din("normw", [128, 1])
        oT = C.dout("oT", [128, ntok], BF16)

        cs = C.sb("cs", [128, 4, 128], F32)
        identf, mUi, mUs, mLs = cs[:, 0, :], cs[:, 1, :], cs[:, 2, :], cs[:, 3, :]
        identb = C.sb("identb", [128, 128], BF16)
        onesf = C.sb("onesf", [128, 128], F32)
        nws = C.sb("nws", [128, 1], F32)
        Gr = C.sb("Gr", [128, 2, 128], F32)
        gcol = C.sb("gcol", [128, ntile], F32)
        bcol = C.sb("bcol", [128, ntile], F32)
        gc = C.sb("gc", [128, ntile], F32)
        ngc = C.sb("ngc", [128, ntile], F32)
        glb = C.sb("glb", [128, ntile], F32)
        egl = C.sb("egl", [128, ntile], F32)
        sc1 = C.sb("sc1", [128, ntile], F32)
        sc2 = C.sb("sc2", [128, ntile], F32)
        S = C.sb("S", [128, 128], F32)
        Sb = C.sb("Sb", [128, 128], BF16)
        qs = [C.sb("qs%d" % i, [128, TPB * 128], BF16) for i in range(2)]
        ks = [C.sb("ks%d" % i, [128, TPB * 128], BF16) for i in range(2)]
        vs = [C.sb("vs%d" % i, [128, TPB * 128], BF16) for i in range(2)]
        zs = [C.sb("zs%d" % i, [128, TPB * 128], BF16) for i in range(2)]
        osb = [C.sb("os%d" % i, [128, TPB * 128], BF16) for i in range(2)]

        NSET = 6

        def two(name, shape, dt):
            return [C.sb("%s%d" % (name, i), shape, dt) for i in range(NSET)]
        dabs = two("dabs", [128, 128], F32)
        W = two("W", [128, 128], F32)
        egB = two("egB", [128, 128], F32)
        bU = two("bU", [128, 128], F32)
        KW = two("KW", [128, 128], F32)
        Wm = two("Wm", [128, 128], F32)
        AT = two("AT", [128, 128], BF16)
        XY = [[C.sb("XY%d_%d" % (i, j), [128, 256], F32) for j in range(2)] for i in range(NSET)]
        Pm = [[C.sb("Pm%d_%d" % (i, j), [128, 128], F32) for j in range(2)] for i in range(NSET)]
        Pb = two("Pb", [128, 128], BF16)
        kbg = two("kbg", [128, 128], BF16)
        kd = two("kd", [128, 128], BF16)
        vb = two("vb", [128, 128], BF16)
        qd = two("qd", [128, 128], BF16)
        nwT = two("nwT", [128, 128], BF16)
        vn = two("vn", [128, 128], BF16)
        junk = two("junk", [128, 128], F32)
        ss = two("ss", [128, 1], F32)
        rt = two("rt", [128, 1], F32)
        rs = two("rs", [128, 1], F32)
        on = two("on", [128, 128], BF16)

        b0 = C.bank("b0"); b1 = C.bank("b1")
        C.nbank += 1
        b2 = st.enter_context(nc.psum_tensor("b2", [128, 1024], BF16))
        b3 = C.bank("b3"); b4 = C.bank("b4"); b5 = C.bank("b5"); b6 = C.bank("b6"); b7 = C.bank("b7")

        P.dma("sp", cs[:], consts, writes=["cs"], key="c0")
        P.dma("sp", nws[:], normw, writes=["nws"], key="c1")
        P.dma("sp", Gr[0:ntile, :, :], bgT.rearrange("a n p -> n a p"), writes=["Gr"], key="c2")
        C.cp("dve", identb[:], identf, ["cs"], ["identb"])
        C.memset("dve", onesf[:], 1.0, ["onesf"])
        C.memset("dve", S[:], 0.0, ["S"])
        C.memset("dve", Sb[:], 0.0, ["Sb"])
        C.mm(b0[:, 0:ntile], Gr[0:ntile, 0, :], identf[0:ntile, 0:ntile], True, True, ["Gr", "cs"], ["b0"])
        C.mm(b0[:, 128:128 + ntile], Gr[0:ntile, 1, :], identf[0:ntile, 0:ntile], True, True, ["Gr", "cs"], ["b0"])
        C.cp("dve", bcol[:], b0[:, 0:ntile], [], ["b0", "bcol"])
        C.cp("dve", gcol[:], b0[:, 128:128 + ntile], [], ["b0", "gcol"])
        C.mm(b1[:, 0:ntile], mUi, gcol[:], True, True, ["cs", "gcol"], ["b1"])
        C.mm(b1[:, 128:128 + ntile], onesf[:], gcol[:], True, True, ["onesf", "gcol"], ["b1"])
        C.cp("dve", gc[:], b1[:, 0:ntile], [], ["b1", "gc"])
        C.cp("dve", glb[:], b1[:, 128:128 + ntile], [], ["b1", "glb"])
        C.ts("dve", ngc[:], gc[:], -1.0, ALU.mult, ["gc"], ["ngc"])
        C.act(egl[:], glb[:], AF.Exp, ["glb"], ["egl"])
        C.tt("dve", sc2[:], glb[:], gc[:], ALU.subtract, ["glb", "gc"], ["sc2"])
        C.act(sc2[:], sc2[:], AF.Exp, ["sc2"], ["sc2"])
        C.act(sc1[:], gc[:], AF.Exp, ["gc"], ["sc1"])
        C.tt("dve", sc1[:], sc1[:], bcol[:], ALU.mult, ["sc1", "bcol"], ["sc1"])
        scal = ["bcol", "gcol", "ngc", "egl", "sc1", "sc2"]

        def load_block(blk):
            s = blk % 2
            sl = slice(blk * TPB * 128, (blk + 1) * TPB * 128)
            P.dma("sp", qs[s][:], qT[:, sl], writes=["qs%d" % s], key="lq%d" % s)
            P.dma("sp", ks[s][:], kT[:, sl], writes=["ks%d" % s], key="lk%d" % s)
            P.dma("sp", vs[s][:], vT[:, sl], writes=["vs%d" % s], key="lv%d" % s)
            P.dma("sp", zs[s][:], gzT[:, sl], writes=["zs%d" % s], key="lz%d" % s)

        def setup(n):
            s = n % NSET
            bs = (n // TPB) % 2
            tl = slice((n % TPB) * 128, (n % TPB) * 128 + 128)
            kn, qn, vn_ = ks[bs][:, tl], qs[bs][:, tl], vs[bs][:, tl]
            nn = slice(n, n + 1)
            T = "_%d" % s
            C.mm(b0[:, 0:128], gcol[:, nn].to_broadcast([128, 128]), mUi, True, True, ["gcol", "cs"], ["b0"]); yield
            C.mm(b0[:, 128:256], bcol[:, nn].to_broadcast([128, 128]), identf, True, True, ["bcol", "cs"], ["b0"]); yield
            C.mm(b0[:, 256:384], kn, kn, True, True, ["ks%d" % bs], ["b0"]); yield
            C.mm(b0[:, 384:512], kn, qn, True, True, ["ks%d" % bs, "qs%d" % bs], ["b0"]); yield
            C.tr(b2[:, 0:128], kn, identb[:], ["ks%d" % bs, "identb"], ["b2"]); yield
            C.tr(b2[:, 128:256], vn_, identb[:], ["vs%d" % bs, "identb"], ["b2"]); yield
            C.act(dabs[s][:], b0[:, 0:128], AF.Abs, ["ngc"], ["b0", "dabs" + T], bias=ngc[:, nn]); yield
            C.act(egB[s][:], b0[:, 0:128], AF.Exp, [], ["b0", "egB" + T]); yield
            C.tt("dve", bU[s][:], b0[:, 128:256], mUs, ALU.mult, ["cs"], ["b0", "bU" + T]); yield
            C.act(W[s][:], dabs[s][:], AF.Exp, ["dabs" + T], ["W" + T], scale=-1.0); yield
            C.tt("dve", KW[s][:], b0[:, 256:384], W[s][:], ALU.mult, ["W" + T], ["b0", "KW" + T]); yield
            C.tt("pool", Wm[s][:], W[s][:], mUi, ALU.mult, ["W" + T, "cs"], ["Wm" + T]); yield
            C.tt("dve", AT[s][:], b0[:, 384:512], Wm[s][:], ALU.mult, ["Wm" + T], ["b0", "AT" + T]); yield
            C.tt("pool", XY[s][0][:, 0:128], KW[s][:], bU[s][:], ALU.mult, ["KW" + T, "bU" + T], ["XY%d_0" % s]); yield
            C.tt("pool", dabs[s][:], KW[s][:], mLs, ALU.mult, ["KW" + T, "cs"], ["dabs" + T]); yield
            C.ts("pool", XY[s][0][:, 128:256], dabs[s][:], bcol[:, nn], ALU.mult, ["dabs" + T, "bcol"], ["XY%d_0" % s]); yield
            C.tt("pool", Pm[s][0][:], identf, XY[s][0][:, 0:128], ALU.subtract, ["cs", "XY%d_0" % s], ["Pm%d_0" % s]); yield
            C.act(kbg[s][:], b2[:, 0:128], AF.Identity, ["sc1"], ["b2", "kbg" + T], scale=sc1[:, nn]); yield
            C.act(kd[s][:], b2[:, 0:128], AF.Identity, ["sc2"], ["b2", "kd" + T], scale=sc2[:, nn]); yield
            C.ts("dve", vb[s][:], b2[:, 128:256], bcol[:, nn], ALU.mult, ["bcol"], ["b2", "vb" + T]); yield
            C.tt("pool", qd[s][:], qn, egB[s][:], ALU.mult, ["qs%d" % bs, "egB" + T], ["qd" + T]); yield

        def inv(n):
            s = n % NSET
            T = "_%d" % s
            bx, bp = (b3, b4) if n % 2 == 0 else (b1, b7)
            kx, kp = ("b3", "b4") if n % 2 == 0 else ("b1", "b7")
            for k in range(1, 7):
                a, b = (k - 1) % 2, k % 2
                Xp, Yp = XY[s][a][:, 0:128], XY[s][a][:, 128:256]
                ka, kb_ = "XY%d_%d" % (s, a), "XY%d_%d" % (s, b)
                if k < 6:
                    C.mm(bx[:, 0:128], Yp, Xp, True, True, [ka], [kx]); yield
                C.mm(bx[:, 128:256], Xp, Yp, True, True, [ka], [kx]); yield
                if k < 6:
                    C.cp("act" if k % 2 else "dve", XY[s][b][:], bx[:, 0:256], [], [kx, kb_]); yield
                else:
                    C.cp("dve", XY[s][b][:, 128:256], bx[:, 128:256], [], [kx, kb_]); yield
                C.mm(bp[:, 0:128], XY[s][b][:, 128:256], Pm[s][a][:], True, True, [kb_, "Pm%d_%d" % (s, a)], [kp]); yield
                if k < 6:
                    C.tt("dve", Pm[s][b][:], bp[:, 0:128], Pm[s][a][:], ALU.add, ["Pm%d_%d" % (s, a)], [kp, "Pm%d_%d" % (s, b)]); yield
                else:
                    C.tt("dve", Pb[s][:], bp[:, 0:128], Pm[s][a][:], ALU.add, ["Pm%d_%d" % (s, a)], [kp, "Pb" + T]); yield
            C.mm(bp[:, 128:256], kbg[s][:], Pb[s][:], True, True, ["kbg" + T, "Pb" + T], [kp]); yield
            C.ts("dve", nwT[s][:], bp[:, 128:256], -1.0, ALU.mult, [], [kp, "nwT" + T]); yield

        def rec(n):
            s = n % NSET
            bs = (n // TPB) % 2
            tl = slice((n % TPB) * 128, (n % TPB) * 128 + 128)
            nn = slice(n, n + 1)
            T = "_%d" % s
            C.mm(b5[:, 0:128], Pb[s][:], vb[s][:], True, False, ["Pb" + T, "vb" + T], ["b5"]); yield
            C.mm(b5[:, 0:128], nwT[s][:], Sb[:], False, True, ["nwT" + T, "Sb"], ["b5"]); yield
            C.cp("dve", vn[s][:], b5[:, 0:128], [], ["b5", "vn" + T]); yield
            C.mm(b6[:, 0:128], qd[s][:], Sb[:], True, False, ["qd" + T, "Sb"], ["b6"]); yield
            C.mm(b6[:, 0:128], AT[s][:], vn[s][:], False, True, ["AT" + T, "vn" + T], ["b6"]); yield
            C.mm(b6[:, 128:256], kd[s][:], vn[s][:], True, True, ["kd" + T, "vn" + T], ["b6"]); yield
            C.stt("dve", S[:], S[:], egl[:, nn], b6[:, 128:256], ALU.mult, ALU.add, ["S", "egl"], ["b6", "S"]); yield
            C.cp("act", Sb[:], S[:], ["S"], ["Sb"]); yield
            C.act(junk[s][:], b6[:, 0:128], AF.Square, [], ["b6", "junk" + T, "ss" + T], accum=ss[s][:]); yield
            C.act(rt[s][:], ss[s][:], AF.Sqrt, ["ss" + T], ["rt" + T], bias=RMS_EPS, scale=1.0 / 128); yield
            C.recip(rs[s][:], rt[s][:], ["rt" + T], ["rs" + T]); yield
            C.act(on[s][:], b6[:, 0:128], AF.Identity, ["rs" + T], ["b6", "on" + T], scale=rs[s][:]); yield
            C.tr(b2[:, 256:384], on[s][:], identb[:], ["on" + T, "identb"], ["b2"]); yield
            C.stt("dve", osb[bs][:, tl], b2[:, 256:384], nws[:, 0:1], zs[bs][:, tl], ALU.mult, ALU.mult,
                  ["nws", "zs%d" % bs], ["b2", "os%d" % bs]); yield
            if n % TPB == TPB - 1:
                blk = n // TPB
                d = P.dma("sp", oT[:, blk * TPB * 128:(blk + 1) * TPB * 128], osb[bs][:], reads=["os%d" % bs], key="oo%d" % bs)
                P.must_finish(d)

        def chain(*gs):
            for g in gs:
                if g is not None:
                    yield from g

        load_block(0)
        npair = ntile // 2
        for j in range(npair + 2):
            gens = []
            if j < npair:
                gens += [chain(setup(2 * j), setup(2 * j + 1))]
            if 1 <= j <= npair:
                gens += [inv(2 * j - 2), inv(2 * j - 1)]
            if j >= 2:
                gens += [chain(rec(2 * j - 4), rec(2 * j - 3))]
            _interleave(gens)
            if j >= 1 and (j - 1) % (TPB // 2) == 0 and (j - 1) // (TPB // 2) < nblk and j > 1:
                pass
            if j % (TPB // 2) == 1 and (j // (TPB // 2)) + 1 < nblk:
                load_block(j // (TPB // 2) + 1)
        P.emit()
    return nc


def b_consts():
    p = np.arange(128)[:, None]
    f = np.arange(128)[None, :]
    c = np.zeros((128, 4, 128), np.float32)
    c[:, 0] = (p == f)
    c[:, 1] = (f >= p)
    c[:, 2] = (f > p)
    c[:, 3] = (p > f)
    return c


TBC = 1024
FG = 11
NFG = FC // FG


def build_C():
    nc = bass.Bass("TRN2", target_bir_lowering=False)
    with ExitStack() as st:
        C = Ctx(nc, st)
        P = C.P
        catT = C.din("catT", [2048, NT], BF16)
        xTd = C.din("xT", [D_MODEL, NT])
        wout = C.din("wout", [2048, D_MODEL])
        wgur = C.din("wgur", [D_MODEL, 2 * D_FF])
        wd = C.din("wd", [D_FF, D_MODEL])
        lnp = C.din("lnp", [D_MODEL, 4])
        x2T = C.dout("x2T", [D_MODEL, NT])

        xb = C.sb("xb", [128, KC, TBC], F32)
        x1b = C.sb("x1b", [128, KC, TBC], BF16)
        catb = C.sb("catb", [128, KC * TBC], BF16)
        catv = catb[:].rearrange("p (k t) -> p k t", k=KC)
        hidv = catb[:, 0:FG * TBC].rearrange("p (k t) -> p k t", k=FG)
        ws = [C.sb("ws%d" % i, [128, KC, 256], BF16) for i in range(2)]
        wds = [C.sb("wds%d" % i, [128, FG, 256], BF16) for i in range(2)]
        lnps = C.sb("lnps", [128, KC, 4], F32)
        onesb = C.sb("onesb", [128, 128], BF16)
        sg = [C.sb("sg%d" % i, [128, 512], F32) for i in range(2)]
        rb = [C.sb("rb%d" % i, [128, 512], BF16) for i in range(2)]
        sq = [C.sb("sq%d" % i, [128, 512], BF16) for i in range(2)]
        t1 = [C.sb("t1_%d" % i, [128, 512], F32) for i in range(2)]
        t2 = [C.sb("t2_%d" % i, [128, 512], F32) for i in range(2)]
        mean = [C.sb("mean%d" % i, [128, 512], F32) for i in range(2)]
        rstd = [C.sb("rstd%d" % i, [128, 512], F32) for i in range(2)]
        tmpa = C.sb("tmpa", [128, 512], F32)
        pb = [C.bank("pb%d" % i) for i in range(6)]
        pst = [C.bank("pst%d" % i) for i in range(2)]

        P.dma("sp", lnps[:], lnp.rearrange("(k p) j -> p k j", p=128), writes=["lnps"], key="c0")
        C.memset("dve", onesb[:], 1.0, ["onesb"])
        state = {"ws": 0, "wds": 0, "pb": 0}
        wov = wout.rearrange("(k p) c -> p k c", p=128)
        wgv = wgur.rearrange("(k p) c -> p k c", p=128)
        wdv = wd.rearrange("(k p) c -> p k c", p=128)
        xv = xTd.rearrange("(k p) t -> p k t", p=128)
        cv = catT.rearrange("(k p) t -> p k t", p=128)
        ov = x2T.rearrange("(k p) t -> p k t", p=128)
        xkeys = ["xb%d" % k for k in range(KC)]

        def load_ws(src, col0):
            s = state["ws"]
            state["ws"] ^= 1
            for h in range(4):
                P.dma("pool", ws[s][:, 4 * h:4 * h + 4, :], src[:, 4 * h:4 * h + 4, col0:col0 + 256], writes=["ws%d" % s], key="ws%d" % s)
            return s

        def nbank():
            b = state["pb"]
            state["pb"] = (b + 1) % 6
            return b

        def layer_norm(blk, gi, bi, make_bf16):
            for hf in range(2):
                hs = slice(512 * hf, 512 * hf + 512)
                for k in range(KC):
                    q = k % 2
                    C.cp("act", rb[q][:], xb[:, k, hs], ["xb%d" % k], ["rb%d" % q])
                    C.tt("pool", sq[q][:], xb[:, k, hs], xb[:, k, hs], ALU.mult, ["xb%d" % k], ["sq%d" % q])
                    C.mm(pst[0][:], onesb[:], rb[q][:], k == 0, k == KC - 1, ["onesb", "rb%d" % q], ["pst0"])
                    C.mm(pst[1][:], onesb[:], sq[q][:], k == 0, k == KC - 1, ["onesb", "sq%d" % q], ["pst1"])
                C.ts("dve", mean[hf][:], pst[0][:], 1.0 / D_MODEL, ALU.mult, [], ["pst0", "mean%d" % hf])
                C.tt("dve", tmpa[:], mean[hf][:], mean[hf][:], ALU.mult, ["mean%d" % hf], ["tmpa"])
                C.stt("dve", tmpa[:], pst[1][:], 1.0 / D_MODEL, tmpa[:], ALU.mult, ALU.subtract, ["tmpa"], ["pst1", "tmpa"])
                C.act(tmpa[:], tmpa[:], AF.Sqrt, ["tmpa"], ["tmpa"], bias=LN_EPS)
                C.recip(rstd[hf][:], tmpa[:], ["tmpa"], ["rstd%d" % hf])
                for k in range(KC):
                    q = k % 2
                    C.tt("dve", t1[q][:], xb[:, k, hs], mean[hf][:], ALU.subtract, ["xb%d" % k, "mean%d" % hf], ["t1_%d" % q])
                    C.tt("pool", t2[q][:], t1[q][:], rstd[hf][:], ALU.mult, ["t1_%d" % q, "rstd%d" % hf], ["t2_%d" % q])
                    C.act(xb[:, k, hs], t2[q][:], AF.Identity, ["t2_%d" % q, "lnps"], ["xb%d" % k],
                          bias=lnps[:, k, bi:bi + 1], scale=lnps[:, k, gi:gi + 1])
                    if make_bf16:
                        C.cp("dve", x1b[:, k, hs], xb[:, k, hs], ["xb%d" % k], ["x1b%d" % k])

        for blk in range(NT // TBC):
            ts_ = slice(blk * TBC, (blk + 1) * TBC)
            for k in range(KC):
                P.dma("sp", xb[:, k, :], xv[:, k, ts_], writes=["xb%d" % k], key="lx%d" % k)
            for k in range(KC):
                P.dma("sp", catv[:, k, :], cv[:, k, ts_], writes=["catbuf"], key="lc")
            for cg in range(8):
                s = load_ws(wov, 256 * cg)
                for half in range(2):
                    c = 2 * cg + half
                    for hf in range(2):
                        hs = slice(512 * hf, 512 * hf + 512)
                        b = nbank()
                        for m in range(KC):
                            C.mm(pb[b][:], ws[s][:, m, 128 * half:128 * half + 128], catv[:, m, hs], m == 0, m == KC - 1,
                                 ["ws%d" % s, "catbuf"], ["pb%d" % b])
                        C.stt("dve", xb[:, c, hs], xb[:, c, hs], ALPHA, pb[b][:], ALU.mult, ALU.add, ["xb%d" % c], ["pb%d" % b, "xb%d" % c])
            layer_norm(blk, 0, 1, True)
            for fg in range(NFG):
                for fi in range(FG):
                    f = fg * FG + fi
                    s = load_ws(wgv, 256 * f)
                    for hf in range(2):
                        hs = slice(512 * hf, 512 * hf + 512)
                        bg_ = nbank()
                        for k in range(KC):
                            C.mm(pb[bg_][:], ws[s][:, k, 0:128], x1b[:, k, hs], k == 0, k == KC - 1, ["ws%d" % s, "x1b%d" % k], ["pb%d" % bg_])
                        bu = nbank()
                        for k in range(KC):
                            C.mm(pb[bu][:], ws[s][:, k, 128:256], x1b[:, k, hs], k == 0, k == KC - 1, ["ws%d" % s, "x1b%d" % k], ["pb%d" % bu])
                        q = hf
                        C.act(sg[q][:], pb[bg_][:], AF.Silu, [], ["pb%d" % bg_, "sg%d" % q])
                        C.tt("dve", hidv[:, fi, hs], pb[bu][:], sg[q][:], ALU.mult, ["sg%d" % q], ["pb%d" % bu, "catbuf"])
                for cg in range(8):
                    s = state["wds"]
                    state["wds"] ^= 1
                    P.dma("pool", wds[s][:], wdv[:, fg * FG:(fg + 1) * FG, 256 * cg:256 * cg + 256], writes=["wds%d" % s], key="wds%d" % s)
                    for half in range(2):
                        c = 2 * cg + half
                        for hf in range(2):
                            hs = slice(512 * hf, 512 * hf + 512)
                            b = nbank()
                            for fi in range(FG):
                                C.mm(pb[b][:], wds[s][:, fi, 128 * half:128 * half + 128], hidv[:, fi, hs], fi == 0, fi == FG - 1,
                                     ["wds%d" % s, "catbuf"], ["pb%d" % b])
                            if fg == 0:
                                C.stt("dve", xb[:, c, hs], xb[:, c, hs], ALPHA, pb[b][:], ALU.mult, ALU.add, ["xb%d" % c], ["pb%d" % b, "xb%d" % c])
                            else:
                                C.tt("dve", xb[:, c, hs], xb[:, c, hs], pb[b][:], ALU.add, ["xb%d" % c], ["pb%d" % b, "xb%d" % c])
            layer_norm(blk, 2, 3, False)
            for k in range(KC):
                d = P.dma("sp", ov[:, k, ts_], xb[:, k, :], reads=["xb%d" % k], key="ox%d" % k)
                P.must_finish(d)
        P.emit()
    return nc


def host_inputs_C(w_out, w_gate_up, w_down, ln1_g, ln1_b, ln2_g, ln2_b):
    perm = []
    for f in range(FC):
        perm += list(range(128 * f, 128 * f + 128)) + list(range(D_FF + 128 * f, D_FF + 128 * f + 128))
    wgur = np.ascontiguousarray(w_gate_up[:, perm])
    lnp = np.ascontiguousarray(np.stack([ln1_g, ln1_b, ln2_g, ln2_b], axis=1))
    return {"wout": np.ascontiguousarray(w_out), "wgur": wgur, "wd": np.ascontiguousarray(w_down), "lnp": lnp}


_PROGS = {}


def _prog(name):
    if name not in _PROGS:
        _PROGS[name] = {"A": build_A, "B": build_B, "C": build_C}[name]()
    return _PROGS[name]


def _run(name, maps):
    res = run_bass_kernel_spmd(_prog(name), maps, core_ids=list(range(NCORES)))
    return res.results


def kernel(x, w_in, conv_dw_w, conv_dw_b, conv_ln_g, conv_ln_b, gdn_conv_w, gdn_A_log,
           gdn_dt_bias, gdn_norm_w, w_out, ln1_g, ln1_b, w_gate_up, w_down, ln2_g, ln2_b):
    f = lambda a: np.asarray(a, dtype=np.float32)
    xcur = f(x)[0]
    xT_cores = [np.ascontiguousarray(xcur[c * NT:(c + 1) * NT].T) for c in range(NCORES)]
    consts = b_consts()
    for l in range(DEPTH):
        mapsA = host_inputs_A_T(xT_cores, f(w_in[l]), f(conv_dw_w[l]), f(conv_dw_b[l]), f(conv_ln_g[l]), f(conv_ln_b[l]),
                                f(gdn_conv_w[l]), f(gdn_A_log[l]), f(gdn_dt_bias[l]))
        ra = _run("A", mapsA)
        mapsB = []
        for h in range(NHEAD):
            cat = lambda nm, r0: np.ascontiguousarray(np.concatenate([np.asarray(ra[c][nm])[r0:r0 + 128] for c in range(NCORES)], axis=1))
            beta = np.concatenate([np.asarray(ra[c]["bg"])[h] for c in range(NCORES)])
            g = np.concatenate([np.asarray(ra[c]["bg"])[NHEAD + h] for c in range(NCORES)])
            mapsB.append({"qT": cat("qkvT", 128 * h), "kT": cat("qkvT", 1024 + 128 * h), "vT": cat("qkvT", 2048 + 128 * h),
                          "gzT": cat("gzT", 128 * h),
                          "bgT": np.ascontiguousarray(np.stack([beta.reshape(NTILE, 128), g.reshape(NTILE, 128)])),
                          "consts": consts, "normw": np.ascontiguousarray(f(gdn_norm_w[l]).reshape(128, 1))})
        rb = _run("B", mapsB)
        wc = host_inputs_C(f(w_out[l]), f(w_gate_up[l]), f(w_down[l]), f(ln1_g[l]), f(ln1_b[l]), f(ln2_g[l]), f(ln2_b[l]))
        mapsC = []
        for c in range(NCORES):
            catT = np.concatenate([np.asarray(ra[c]["convT"])] + [np.asarray(rb[h]["oT"])[:, c * NT:(c + 1) * NT] for h in range(NHEAD)], axis=0)
            m = dict(wc)
            m["catT"] = np.ascontiguousarray(catT)
            m["xT"] = xT_cores[c]
            mapsC.append(m)
        rc = _run("C", mapsC)
        xT_cores = [np.ascontiguousarray(np.asarray(rc[c]["x2T"])) for c in range(NCORES)]
    out = np.concatenate([xT_cores[c].T for c in range(NCORES)], axis=0)
    return np.ascontiguousarray(out[None]).astype(np.float32)


def host_inputs_A_T(xT_cores, w_in, conv_dw_w, conv_dw_b, conv_ln_g, conv_ln_b, gdn_conv_w, A_log, dt_bias):
    perm = []
    for i in range(8):
        perm += list(range(128 * i, 128 * i + 128)) + list(range(1024 + 128 * i, 1024 + 128 * i + 128))
    perm += list(range(2048, IN_COLS))
    winr = np.ascontiguousarray(w_in[:, perm])
    cdw = np.ascontiguousarray(conv_dw_w.T)
    cvec = np.ascontiguousarray(np.stack([conv_dw_b, conv_ln_g, conv_ln_b], axis=1))
    gcw = np.ascontiguousarray(gdn_conv_w.T)
    hv = np.ascontiguousarray(np.stack([A_log, dt_bias], axis=1))
    ident = np.eye(128, dtype=np.float32)
    maps = []
    for c in range(len(xT_cores)):
        xt = np.zeros((D_MODEL, NTX), np.float32)
        xt[:, HALO:] = xT_cores[c]
        if c > 0:
            xt[:, :HALO] = xT_cores[c - 1][:, NT - HALO:]
        maps.append({"xT": xt, "winr": winr, "cdw": cdw, "cvec": cvec, "gcw": gcw, "hv": hv, "ident": ident})
    return maps
```

```python
import numpy as np
from contextlib import ExitStack
import concourse.bass as bass
import concourse.mybir as mybir
from concourse.bass_utils import run_bass_kernel_spmd

F32 = mybir.dt.float32
BF16 = mybir.dt.bfloat16
ALU = mybir.AluOpType
AF = mybir.ActivationFunctionType

ENGS = ("pe", "act", "dve", "pool", "sp")

D_MODEL = 2048
SEQ = 16384
DEPTH = 2
NCORES = 8
NT = SEQ // NCORES
HALO = 32
NTX = NT + HALO
KC = D_MODEL // 128
CONV_CH = 1024
GDN_W = 1024
NHEAD = 8
D_FF = 5632
FC = D_FF // 128
IN_COLS = 6160
ALPHA = (2 * DEPTH) ** 0.25
LN_EPS = 1e-5
RMS_EPS = 1e-6
L2_EPS = 1e-6


class _Op:
    __slots__ = ("eng", "fn", "deps", "is_dma", "semkey", "signal", "count", "inc")

    def __init__(self, eng, fn, is_dma, semkey, inc=16):
        self.inc = inc
        self.eng = eng
        self.fn = fn
        self.deps = []
        self.is_dma = is_dma
        self.semkey = semkey
        self.signal = False
        self.count = 0


class Prog:
    def __init__(self, nc):
        self.nc = nc
        self.ops = {e: [] for e in ENGS}
        self.last_w = {}
        self.readers = {}
        self.final_waits = []

    def _add(self, eng, fn, reads, writes, is_dma=False, semkey=None, inc=16):
        op = _Op(eng, fn, is_dma, semkey, inc)
        deps = {}
        for r in reads:
            w = self.last_w.get(r)
            if w is not None:
                deps[id(w)] = (w, "raw")
        for wkey in writes:
            for rd in self.readers.get(wkey, ()):
                if id(rd) not in deps:
                    deps[id(rd)] = (rd, "war")
            w = self.last_w.get(wkey)
            if w is not None and id(w) not in deps:
                deps[id(w)] = (w, "waw")
        for r in reads:
            self.readers.setdefault(r, []).append(op)
        for wkey in writes:
            self.last_w[wkey] = op
            self.readers[wkey] = []
        for (d, kind) in deps.values():
            if d is op:
                continue
            if (not d.is_dma) and (not is_dma) and d.eng == eng:
                if eng == "pe" or kind != "raw":
                    continue
            op.deps.append(d)
            d.signal = True
        self.ops[eng].append(op)
        return op

    def op(self, eng, fn, reads=(), writes=()):
        return self._add(eng, fn, tuple(reads), tuple(writes))

    def dma(self, eng, out, in_, reads=(), writes=(), key=None):
        fn = lambda e: e.dma_start(out=out, in_=in_)
        return self._add(eng, fn, tuple(reads), tuple(writes), is_dma=True, semkey=key)

    def collective(self, kind, ins, outs, reads, writes, key):
        rg = [list(range(NCORES))]
        fn = lambda e: e.collective_compute(kind, ALU.bypass, replica_groups=rg, ins=[a.opt() for a in ins], outs=[a.opt() for a in outs])
        return self._add("pool", fn, tuple(reads), tuple(writes), is_dma=True, semkey=key, inc=1)

    def must_finish(self, op):
        op.signal = True
        self.final_waits.append(op)

    def emit(self):
        nc = self.nc
        keycount = {}
        for e in ENGS:
            c = 0
            for op in self.ops[e]:
                if op.is_dma:
                    k = op.semkey
                    keycount[k] = keycount.get(k, 0) + op.inc
                    op.count = keycount[k]
                elif op.signal:
                    c += 1
                    op.count = c
        with ExitStack() as st:
            esem = {e: st.enter_context(nc.semaphore("s_" + e)) for e in ENGS}
            ksem = {k: st.enter_context(nc.semaphore("k_" + str(k))) for k in keycount}
            block = st.enter_context(nc.Block())

            def run(engname, eh):
                waited = {}
                for op in self.ops[engname]:
                    need = {}
                    for d in op.deps:
                        s = ("k", d.semkey) if d.is_dma else ("e", d.eng)
                        if d.count > need.get(s, 0):
                            need[s] = d.count
                    for s, v in need.items():
                        if waited.get(s, 0) >= v:
                            continue
                        sem = ksem[s[1]] if s[0] == "k" else esem[s[1]]
                        eh.wait_ge(sem, v)
                        waited[s] = v
                    ins = op.fn(eh)
                    if op.is_dma:
                        ins.then_inc(ksem[op.semkey], op.inc)
                    elif op.signal:
                        ins.then_inc(esem[engname], 1)
                if engname == "sp":
                    for op in self.final_waits:
                        sem = ksem[op.semkey] if op.is_dma else esem[op.eng]
                        eh.wait_ge(sem, op.count)

            block.tensor(lambda eh: run("pe", eh))
            block.scalar(lambda eh: run("act", eh))
            block.vector(lambda eh: run("dve", eh))
            block.gpsimd(lambda eh: run("pool", eh))
            block.sync(lambda eh: run("sp", eh))


class Ctx:
    def __init__(self, nc, st):
        self.nc = nc
        self.st = st
        self.P = Prog(nc)
        self.nbank = 0

    def sb(self, name, shape, dt):
        return self.st.enter_context(self.nc.sbuf_tensor(name, list(shape), dt))

    def bank(self, name):
        self.nbank += 1
        assert self.nbank <= 8
        return self.st.enter_context(self.nc.psum_tensor(name, [128, 512], F32))

    def din(self, name, shape, dt=F32):
        return self.nc.dram_tensor(name, list(shape), dt, kind="ExternalInput").ap()

    def dout(self, name, shape, dt=F32):
        return self.nc.dram_tensor(name, list(shape), dt, kind="ExternalOutput").ap()

    def mm(self, out, lhsT, rhs, start, stop, r, w):
        self.P.op("pe", lambda e: e.matmul(out, lhsT=lhsT, rhs=rhs, start=start, stop=stop), r, w)

    def tr(self, out, in_, ident, r, w):
        self.P.op("pe", lambda e: e.transpose(out, in_, ident), r, w)

    def act(self, out, in_, func, r, w, bias=None, scale=None, accum=None, eng="act"):
        kw = {}
        if bias is not None:
            kw["bias"] = bias
        if scale is not None:
            kw["scale"] = scale
        if accum is not None:
            kw["accum_out"] = accum
        self.P.op(eng, lambda e: e.activation(out=out, in_=in_, func=func, **kw), r, w)

    def tt(self, eng, out, in0, in1, op, r, w):
        self.P.op(eng, lambda e: e.tensor_tensor(out=out, in0=in0, in1=in1, op=op), r, w)

    def ts(self, eng, out, in0, s1, op0, r, w, s2=None, op1=None):
        if op1 is None:
            self.P.op(eng, lambda e: e.tensor_scalar(out=out, in0=in0, scalar1=s1, scalar2=None, op0=op0), r, w)
        else:
            self.P.op(eng, lambda e: e.tensor_scalar(out=out, in0=in0, scalar1=s1, scalar2=s2, op0=op0, op1=op1), r, w)

    def stt(self, eng, out, in0, scalar, in1, op0, op1, r, w):
        self.P.op(eng, lambda e: e.scalar_tensor_tensor(out=out, in0=in0, scalar=scalar, in1=in1, op0=op0, op1=op1), r, w)

    def cp(self, eng, out, in_, r, w):
        if eng == "act":
            self.P.op(eng, lambda e: e.copy(out=out, in_=in_), r, w)
        else:
            self.P.op(eng, lambda e: e.tensor_copy(out=out, in_=in_), r, w)

    def recip(self, out, in_, r, w):
        self.P.op("dve", lambda e: e.reciprocal(out=out, in_=in_), r, w)

    def memset(self, eng, ap, val, w):
        self.P.op(eng, lambda e: e.memset(ap, val), (), w)


def build_A():
    nc = bass.Bass("TRN2", target_bir_lowering=False)
    with ExitStack() as st:
        C = Ctx(nc, st)
        P = C.P
        xT = C.din("xT", [D_MODEL, NTX])
        winr = C.din("winr", [D_MODEL, IN_COLS])
        cdw = C.din("cdw", [CONV_CH, 31])
        cvec = C.din("cvec", [CONV_CH, 3])
        gcw = C.din("gcw", [3 * GDN_W, 4])
        hv = C.din("hv", [NHEAD, 2])
        identd = C.din("ident", [128, 128])
        convT = C.dout("convT", [CONV_CH, NT], BF16)
        qkvT = C.dout("qkvT", [3 * GDN_W, NT], BF16)
        gzT = C.dout("gzT", [GDN_W, NT], BF16)
        bgo = C.dout("bg", [2 * NHEAD, NT], F32)

        xs = C.sb("xs", [128, KC, NTX], BF16)
        wb = [C.sb("wb%d" % i, [128, KC, 256], BF16) for i in range(2)]
        wsm = C.sb("wsm", [128, KC, 16], BF16)
        identf = C.sb("identf", [128, 128], F32)
        identb = C.sb("identb", [128, 128], BF16)
        onesb = C.sb("onesb", [128, 128], BF16)
        cdws = C.sb("cdws", [128, 8, 31], F32)
        cvecs = C.sb("cvecs", [128, 8, 3], F32)
        gcws = C.sb("gcws", [128, 24, 4], F32)
        hvs = C.sb("hvs", [NHEAD, 2], F32)
        negA = C.sb("negA", [NHEAD, 1], F32)
        Dc = [C.sb("Dc%d" % i, [128, 31, 128], BF16) for i in range(2)]
        Dq = [C.sb("Dq%d" % i, [128, 4, 128], BF16) for i in range(2)]
        hbuf = [C.sb("hbuf%d" % i, [128, NTX], BF16) for i in range(2)]
        ubuf = [C.sb("ubuf%d" % i, [128, NTX], BF16) for i in range(2)]
        ybuf = C.sb("ybuf", [128, 8, NT], BF16)
        sg = [C.sb("sg%d" % i, [128, 512], F32) for i in range(2)]
        ysq = [C.sb("ysq%d" % i, [128, 512], BF16) for i in range(2)]
        ost = [C.sb("ost%d" % i, [128, NT], BF16) for i in range(3)]
        mean = C.sb("mean", [128, 512], F32)
        rstd = C.sb("rstd", [128, 512], F32)
        tmpa = C.sb("tmpa", [128, 512], F32)
        t1 = [C.sb("t1_%d" % i, [128, 512], F32) for i in range(2)]
        t2 = [C.sb("t2_%d" % i, [128, 512], F32) for i in range(2)]
        sv = [C.sb("sv%d" % i, [128, 512], F32) for i in range(2)]
        sv4 = [C.sb("sv4_%d" % i, [128, 512], F32) for i in range(4)]
        ysq4 = [C.sb("ysq4_%d" % i, [128, 512], BF16) for i in range(4)]
        bgs = C.sb("bgs", [NHEAD, 2, 512], F32)
        sp1 = C.sb("sp1", [NHEAD, 512], F32)
        sp2 = C.sb("sp2", [NHEAD, 512], F32)

        pu = [C.bank("pu%d" % i) for i in range(4)]
        pc = [C.bank("pc%d" % i) for i in range(2)]
        pst = [C.bank("pst%d" % i) for i in range(2)]

        xv = xT.rearrange("(k p) t -> p k t", p=128)
        for k in range(KC):
            P.dma("pool", xs[:, k, :], xv[:, k, :], writes=["xs%d" % k], key="xs%d" % k)
        P.dma("sp", identf[:], identd, writes=["identf"], key="c0")
        P.dma("sp", cdws[:], cdw.rearrange("(i p) j -> p i j", p=128), writes=["cdws"], key="c1")
        P.dma("sp", cvecs[:], cvec.rearrange("(i p) j -> p i j", p=128), writes=["cvecs"], key="c2")
        P.dma("sp", gcws[:], gcw.rearrange("(i p) j -> p i j", p=128), writes=["gcws"], key="c3")
        P.dma("sp", hvs[:], hv, writes=["hvs"], key="c4")
        C.cp("dve", identb[:], identf[:], ["identf"], ["identb"])
        C.memset("dve", onesb[:], 1.0, ["onesb"])
        C.act(negA[:], hvs[:, 0:1], AF.Exp, ["hvs"], ["negA"])
        C.ts("dve", negA[:], negA[:], -1.0, ALU.mult, ["negA"], ["negA"])

        xs_keys = ["xs%d" % k for k in range(KC)]
        wv = winr.rearrange("(k p) c -> p k c", p=128)
        blocks = [(0, HALO)] + [(HALO + 512 * i, 512) for i in range(4)]

        state = {"wslot": 0, "pu": 0, "pc": 0, "ost": 0, "ngrp": 0}

        def load_w(col0):
            s = state["wslot"]
            state["wslot"] ^= 1
            for h in range(4):
                P.dma("pool", wb[s][:, 4 * h:4 * h + 4, :], wv[:, 4 * h:4 * h + 4, col0:col0 + 256],
                      writes=["wb%d" % s], key="wb%d" % s)
            return s

        def inproj(s, off, c0, n):
            b = state["pu"]
            state["pu"] = (b + 1) % 4
            for k in range(KC):
                C.mm(pu[b][:, 0:n], wb[s][:, k, off:off + 128], xs[:, k, c0:c0 + n], k == 0, k == KC - 1,
                     ["wb%d" % s, "xs%d" % k], ["pu%d" % b])
            return b

        def next_ost():
            o = state["ost"]
            state["ost"] = (o + 1) % 3
            return o

        for i in range(8):
            s = load_w(256 * i)
            hs = i % 2
            P.op("pool", lambda e, hs=hs, i=i: e.tensor_tensor(
                out=Dc[hs][:], in0=identb[:].unsqueeze(1).to_broadcast([128, 31, 128]),
                in1=cdws[:, i, :].unsqueeze(2).to_broadcast([128, 31, 128]), op=ALU.mult),
                ["identb", "cdws"], ["Dc%d" % hs])
            for bi, (c0, n) in enumerate(blocks):
                ba = inproj(s, 0, c0, n)
                bb = inproj(s, 128, c0, n)
                q = (ba // 2) % 2
                C.act(sg[q][:, 0:n], pu[bb][:, 0:n], AF.Sigmoid, [], ["pu%d" % bb, "sg%d" % q])
                C.tt("dve", hbuf[hs][:, c0:c0 + n], pu[ba][:, 0:n], sg[q][:, 0:n], ALU.mult,
                     ["sg%d" % q], ["pu%d" % ba, "hbuf%d_%d" % (hs, bi)])
            for tb in range(4):
                b = state["pc"]
                state["pc"] ^= 1
                base = HALO + 512 * tb - 30
                for j in range(31):
                    C.mm(pc[b][:], Dc[hs][:, j, :], hbuf[hs][:, base + j:base + j + 512], j == 0, j == 30,
                         ["Dc%d" % hs, "hbuf%d_%d" % (hs, tb), "hbuf%d_%d" % (hs, tb + 1)], ["pc%d" % b])
                C.act(ybuf[:, i, 512 * tb:512 * tb + 512], pc[b][:], AF.Identity, ["cvecs"], ["pc%d" % b, "y%d_%d" % (i, tb)],
                      bias=cvecs[:, i, 0:1])

        for tb in range(4):
            tsl = slice(512 * tb, 512 * tb + 512)
            for i in range(8):
                C.mm(pst[0][:], onesb[:], ybuf[:, i, tsl], i == 0, i == 7, ["onesb", "y%d_%d" % (i, tb)], ["pst0"])
            for i in range(8):
                q = i % 2
                C.act(ysq[q][:], ybuf[:, i, tsl], AF.Square, ["y%d_%d" % (i, tb)], ["ysq%d" % q])
                C.mm(pst[1][:], onesb[:], ysq[q][:], i == 0, i == 7, ["onesb", "ysq%d" % q], ["pst1"])
            C.ts("dve", mean[:], pst[0][:], 1.0 / CONV_CH, ALU.mult, [], ["pst0", "mean"])
            C.tt("dve", tmpa[:], mean[:], mean[:], ALU.mult, ["mean"], ["tmpa"])
            C.stt("dve", tmpa[:], pst[1][:], 1.0 / CONV_CH, tmpa[:], ALU.mult, ALU.subtract, ["tmpa"], ["pst1", "tmpa"])
            C.act(tmpa[:], tmpa[:], AF.Sqrt, ["tmpa"], ["tmpa"], bias=LN_EPS)
            C.recip(rstd[:], tmpa[:], ["tmpa"], ["rstd"])
            for i in range(8):
                q = i % 2
                C.tt("dve", t1[q][:], ybuf[:, i, tsl], mean[:], ALU.subtract, ["y%d_%d" % (i, tb), "mean"], ["t1_%d" % q])
                C.tt("pool", t2[q][:], t1[q][:], rstd[:], ALU.mult, ["t1_%d" % q, "rstd"], ["t2_%d" % q])
                C.act(ysq[q][:], t2[q][:], AF.Silu, ["t2_%d" % q, "cvecs"], ["ysq%d" % q],
                      bias=cvecs[:, i, 2:3], scale=cvecs[:, i, 1:2])
                o = P.dma("sp", convT[128 * i:128 * i + 128, tsl], ysq[q][:], reads=["ysq%d" % q], key="oc%d" % q)
                P.must_finish(o)

        for g in range(12):
            s = load_w(2048 + 256 * g)
            for half in range(2):
                ch = 2 * g + half
                us = ch % 2
                P.op("pool", lambda e, us=us, ch=ch: e.tensor_tensor(
                    out=Dq[us][:], in0=identb[:].unsqueeze(1).to_broadcast([128, 4, 128]),
                    in1=gcws[:, ch, :].unsqueeze(2).to_broadcast([128, 4, 128]), op=ALU.mult),
                    ["identb", "gcws"], ["Dq%d" % us])
                for bi, (c0, n) in enumerate(blocks):
                    b = inproj(s, 128 * half, c0, n)
                    C.cp("act" if bi % 2 == 0 else "dve", ubuf[us][:, c0:c0 + n], pu[b][:, 0:n], [],
                         ["pu%d" % b, "ubuf%d_%d" % (us, bi)])
                o = next_ost()
                for tb in range(4):
                    tsl = slice(512 * tb, 512 * tb + 512)
                    b = state["pc"]
                    state["pc"] ^= 1
                    base = HALO + 512 * tb - 3
                    for j in range(4):
                        C.mm(pc[b][:], Dq[us][:, j, :], ubuf[us][:, base + j:base + j + 512], j == 0, j == 3,
                             ["Dq%d" % us, "ubuf%d_%d" % (us, tb), "ubuf%d_%d" % (us, tb + 1)], ["pc%d" % b])
                    if ch >= 16:
                        C.act(ost[o][:, tsl], pc[b][:], AF.Silu, [], ["pc%d" % b, "ost%d" % o])
                    else:
                        C.act(sv4[tb][:], pc[b][:], AF.Silu, [], ["pc%d" % b, "sv4_%d" % tb])
                        C.tt("pool", ysq4[tb][:], sv4[tb][:], sv4[tb][:], ALU.mult, ["sv4_%d" % tb], ["ysq4_%d" % tb])
                if ch < 16:
                    for tb in range(4):
                        tsl = slice(512 * tb, 512 * tb + 512)
                        q = tb % 2
                        C.mm(pst[q][:], onesb[:], ysq4[tb][:], True, True, ["onesb", "ysq4_%d" % tb], ["pst%d" % q])
                        C.act(t1[q][:], pst[q][:], AF.Sqrt, [], ["pst%d" % q, "t1_%d" % q], bias=L2_EPS)
                        C.recip(t2[q][:], t1[q][:], ["t1_%d" % q], ["t2_%d" % q])
                        if ch < 8:
                            C.stt("dve", ost[o][:, tsl], sv4[tb][:], 128.0 ** -0.5, t2[q][:], ALU.mult, ALU.mult,
                                  ["sv4_%d" % tb, "t2_%d" % q], ["ost%d" % o])
                        else:
                            C.tt("dve", ost[o][:, tsl], sv4[tb][:], t2[q][:], ALU.mult, ["sv4_%d" % tb, "t2_%d" % q], ["ost%d" % o])
                d = P.dma("sp", qkvT[128 * ch:128 * ch + 128, :], ost[o][:], reads=["ost%d" % o], key="oq%d" % o)
                P.must_finish(d)

        for g in range(4):
            s = load_w(5120 + 256 * g)
            for half in range(2):
                ch = 2 * g + half
                o = next_ost()
                for (c0, n) in blocks[1:]:
                    b = inproj(s, 128 * half, c0, n)
                    C.act(ost[o][:, c0 - HALO:c0 - HALO + n], pu[b][:, 0:n], AF.Silu, [], ["pu%d" % b, "ost%d" % o])
                d = P.dma("sp", gzT[128 * ch:128 * ch + 128, :], ost[o][:], reads=["ost%d" % o], key="oq%d" % o)
                P.must_finish(d)

        P.dma("pool", wsm[:], wv[:, :, 6144:6160], writes=["wsm"], key="wsm")
        for (c0, n) in blocks[1:]:
            tsl = slice(c0 - HALO, c0 - HALO + n)
            b = state["pu"]
            state["pu"] = (b + 1) % 4
            for k in range(KC):
                C.mm(pu[b][0:NHEAD, 0:n], wsm[:, k, 0:8], xs[:, k, c0:c0 + n], k == 0, k == KC - 1,
                     ["wsm", "xs%d" % k], ["pu%d" % b])
            C.act(bgs[:, 0, :], pu[b][0:NHEAD, 0:n], AF.Sigmoid, [], ["pu%d" % b, "bgs0"])
            b = state["pu"]
            state["pu"] = (b + 1) % 4
            for k in range(KC):
                C.mm(pu[b][0:NHEAD, 0:n], wsm[:, k, 8:16], xs[:, k, c0:c0 + n], k == 0, k == KC - 1,
                     ["wsm", "xs%d" % k], ["pu%d" % b])
            C.act(sp1[:, 0:n], pu[b][0:NHEAD, 0:n], AF.Abs, ["hvs"], ["pu%d" % b, "sp1"], bias=hvs[:, 1:2])
            C.act(sp1[:, 0:n], sp1[:, 0:n], AF.Exp, ["sp1"], ["sp1"], scale=-1.0)
            C.act(sp1[:, 0:n], sp1[:, 0:n], AF.Ln, ["sp1"], ["sp1"], bias=1.0)
            C.ts("dve", sp2[:, 0:n], pu[b][0:NHEAD, 0:n], hvs[:, 1:2], ALU.add, ["hvs"], ["pu%d" % b, "sp2"], s2=0.0, op1=ALU.max)
            C.tt("dve", sp2[:, 0:n], sp2[:, 0:n], sp1[:, 0:n], ALU.add, ["sp1", "sp2"], ["sp2"])
            C.ts("dve", bgs[:, 1, :], sp2[:, 0:n], negA[:, 0:1], ALU.mult, ["sp2", "negA"], ["bgs1"])
            d = P.dma("sp", bgo.rearrange("(a h) t -> h a t", a=2)[:, :, tsl], bgs[:], reads=["bgs0", "bgs1"], key="obg")
            P.must_finish(d)

        P.emit()
    return nc


def host_inputs_A(x_full, w_in, conv_dw_w, conv_dw_b, conv_ln_g, conv_ln_b, gdn_conv_w, A_log, dt_bias):
    perm = []
    for i in range(8):
        perm += list(range(128 * i, 128 * i + 128)) + list(range(1024 + 128 * i, 1024 + 128 * i + 128))
    perm += list(range(2048, IN_COLS))
    winr = np.ascontiguousarray(w_in[:, perm])
    cdw = np.ascontiguousarray(conv_dw_w.T)
    cvec = np.ascontiguousarray(np.stack([conv_dw_b, conv_ln_g, conv_ln_b], axis=1))
    gcw = np.ascontiguousarray(gdn_conv_w.T)
    hv = np.ascontiguousarray(np.stack([A_log, dt_bias], axis=1))
    ident = np.eye(128, dtype=np.float32)
    ncore = x_full.shape[0] // NT
    maps = []
    for c in range(ncore):
        xt = np.zeros((D_MODEL, NTX), np.float32)
        xt[:, HALO:] = x_full[c * NT:(c + 1) * NT].T
        if c > 0:
            xt[:, :HALO] = x_full[c * NT - HALO:c * NT].T
        maps.append({"xT": xt, "winr": winr, "cdw": cdw, "cvec": cvec, "gcw": gcw, "hv": hv, "ident": ident})
    return maps


NTILE = SEQ // 128
TPB = 16


def _interleave(gens):
    gens = [g for g in gens if g is not None]
    while gens:
        nxt = []
        for g in gens:
            try:
                next(g)
                nxt.append(g)
            except StopIteration:
                pass
        gens = nxt


def build_B(ntile=NTILE):
    nc = bass.Bass("TRN2", target_bir_lowering=False)
    ntok = ntile * 128
    nblk = ntile // TPB
    with ExitStack() as st:
        C = Ctx(nc, st)
        P = C.P
        qT = C.din("qT", [128, ntok], BF16)
        kT = C.din("kT", [128, ntok], BF16)
        vT = C.din("vT", [128, ntok], BF16)
        gzT = C.din("gzT", [128, ntok], BF16)
        bgT = C.din("bgT", [2, ntile, 128])
        consts = C.din("consts", [128, 4, 128])
        normw = C.din("normw", [128, 1])
        oT = C.dout("oT", [128, ntok], BF16)

        cs = C.sb("cs", [128, 4, 128], F32)
        identf, mUi, mUs, mLs = cs[:, 0, :], cs[:, 1, :], cs[:, 2, :], cs[:, 3, :]
        identb = C.sb("identb", [128, 128], BF16)
        onesf = C.sb("onesf", [128, 128], F32)
        nws = C.sb("nws", [128, 1], F32)
        Gr = C.sb("Gr", [128, 2, 128], F32)
        gcol = C.sb("gcol", [128, ntile], F32)
        bcol = C.sb("bcol", [128, ntile], F32)
        gc = C.sb("gc", [128, ntile], F32)
        ngc = C.sb("ngc", [128, ntile], F32)
        glb = C.sb("glb", [128, ntile], F32)
        egl = C.sb("egl", [128, ntile], F32)
        sc1 = C.sb("sc1", [128, ntile], F32)
        sc2 = C.sb("sc2", [128, ntile], F32)
        S = C.sb("S", [128, 128], F32)
        Sb = C.sb("Sb", [128, 128], BF16)
        qs = [C.sb("qs%d" % i, [128, TPB * 128], BF16) for i in range(2)]
        ks = [C.sb("ks%d" % i, [128, TPB * 128], BF16) for i in range(2)]
        vs = [C.sb("vs%d" % i, [128, TPB * 128], BF16) for i in range(2)]
        zs = [C.sb("zs%d" % i, [128, TPB * 128], BF16) for i in range(2)]
        osb = [C.sb("os%d" % i, [128, TPB * 128], BF16) for i in range(2)]

        NSET = 6

        def two(name, shape, dt):
            return [C.sb("%s%d" % (name, i), shape, dt) for i in range(NSET)]
        dabs = two("dabs", [128, 128], F32)
        W = two("W", [128, 128], F32)
        egB = two("egB", [128, 128], F32)
        bU = two("bU", [128, 128], F32)
        KW = two("KW", [128, 128], F32)
        Wm = two("Wm", [128, 128], F32)
        AT = two("AT", [128, 128], BF16)
        XY = [[C.sb("XY%d_%d" % (i, j), [128, 256], F32) for j in range(2)] for i in range(NSET)]
        Pm = [[C.sb("Pm%d_%d" % (i, j), [128, 128], F32) for j in range(2)] for i in range(NSET)]
        Pb = two("Pb", [128, 128], BF16)
        kbg = two("kbg", [128, 128], BF16)
        kd = two("kd", [128, 128], BF16)
        vb = two("vb", [128, 128], BF16)
        qd = two("qd", [128, 128], BF16)
        nwT = two("nwT", [128, 128], BF16)
        vn = two("vn", [128, 128], BF16)
        junk = two("junk", [128, 128], F32)
        ss = two("ss", [128, 1], F32)
        rt = two("rt", [128, 1], F32)
        rs = two("rs", [128, 1], F32)
        on = two("on", [128, 128], BF16)

        b0 = C.bank("b0"); b1 = C.bank("b1")
        C.nbank += 1
        b2 = st.enter_context(nc.psum_tensor("b2", [128, 1024], BF16))
        b3 = C.bank("b3"); b4 = C.bank("b4"); b5 = C.bank("b5"); b6 = C.bank("b6"); b7 = C.bank("b7")

        P.dma("sp", cs[:], consts, writes=["cs"], key="c0")
        P.dma("sp", nws[:], normw, writes=["nws"], key="c1")
        P.dma("sp", Gr[0:ntile, :, :], bgT.rearrange("a n p -> n a p"), writes=["Gr"], key="c2")
        C.cp("dve", identb[:], identf, ["cs"], ["identb"])
        C.memset("dve", onesf[:], 1.0, ["onesf"])
        C.memset("dve", S[:], 0.0, ["S"])
        C.memset("dve", Sb[:], 0.0, ["Sb"])
        C.mm(b0[:, 0:ntile], Gr[0:ntile, 0, :], identf[0:ntile, 0:ntile], True, True, ["Gr", "cs"], ["b0"])
        C.mm(b0[:, 128:128 + ntile], Gr[0:ntile, 1, :], identf[0:ntile, 0:ntile], True, True, ["Gr", "cs"], ["b0"])
        C.cp("dve", bcol[:], b0[:, 0:ntile], [], ["b0", "bcol"])
        C.cp("dve", gcol[:], b0[:, 128:128 + ntile], [], ["b0", "gcol"])
        C.mm(b1[:, 0:ntile], mUi, gcol[:], True, True, ["cs", "gcol"], ["b1"])
        C.mm(b1[:, 128:128 + ntile], onesf[:], gcol[:], True, True, ["onesf", "gcol"], ["b1"])
        C.cp("dve", gc[:], b1[:, 0:ntile], [], ["b1", "gc"])
        C.cp("dve", glb[:], b1[:, 128:128 + ntile], [], ["b1", "glb"])
        C.ts("dve", ngc[:], gc[:], -1.0, ALU.mult, ["gc"], ["ngc"])
        C.act(egl[:], glb[:], AF.Exp, ["glb"], ["egl"])
        C.tt("dve", sc2[:], glb[:], gc[:], ALU.subtract, ["glb", "gc"], ["sc2"])
        C.act(sc2[:], sc2[:], AF.Exp, ["sc2"], ["sc2"])
        C.act(sc1[:], gc[:], AF.Exp, ["gc"], ["sc1"])
        C.tt("dve", sc1[:], sc1[:], bcol[:], ALU.mult, ["sc1", "bcol"], ["sc1"])
        scal = ["bcol", "gcol", "ngc", "egl", "sc1", "sc2"]

        def load_block(blk):
            s = blk % 2
            sl = slice(blk * TPB * 128, (blk + 1) * TPB * 128)
            P.dma("sp", qs[s][:], qT[:, sl], writes=["qs%d" % s], key="lq%d" % s)
            P.dma("sp", ks[s][:], kT[:, sl], writes=["ks%d" % s], key="lk%d" % s)
            P.dma("sp", vs[s][:], vT[:, sl], writes=["vs%d" % s], key="lv%d" % s)
            P.dma("sp", zs[s][:], gzT[:, sl], writes=["zs%d" % s], key="lz%d" % s)

        def setup(n):
            s = n % NSET
            bs = (n // TPB) % 2
            tl = slice((n % TPB) * 128, (n % TPB) * 128 + 128)
            kn, qn, vn_ = ks[bs][:, tl], qs[bs][:, tl], vs[bs][:, tl]
            nn = slice(n, n + 1)
            T = "_%d" % s
            C.mm(b0[:, 0:128], gcol[:, nn].to_broadcast([128, 128]), mUi, True, True, ["gcol", "cs"], ["b0"]); yield
            C.mm(b0[:, 128:256], bcol[:, nn].to_broadcast([128, 128]), identf, True, True, ["bcol", "cs"], ["b0"]); yield
            C.mm(b0[:, 256:384], kn, kn, True, True, ["ks%d" % bs], ["b0"]); yield
            C.mm(b0[:, 384:512], kn, qn, True, True, ["ks%d" % bs, "qs%d" % bs], ["b0"]); yield
            C.tr(b2[:, 0:128], kn, identb[:], ["ks%d" % bs, "identb"], ["b2"]); yield
            C.tr(b2[:, 128:256], vn_, identb[:], ["vs%d" % bs, "identb"], ["b2"]); yield
            C.act(dabs[s][:], b0[:, 0:128], AF.Abs, ["ngc"], ["b0", "dabs" + T], bias=ngc[:, nn]); yield
            C.act(egB[s][:], b0[:, 0:128], AF.Exp, [], ["b0", "egB" + T]); yield
            C.tt("dve", bU[s][:], b0[:, 128:256], mUs, ALU.mult, ["cs"], ["b0", "bU" + T]); yield
            C.act(W[s][:], dabs[s][:], AF.Exp, ["dabs" + T], ["W" + T], scale=-1.0); yield
            C.tt("dve", KW[s][:], b0[:, 256:384], W[s][:], ALU.mult, ["W" + T], ["b0", "KW" + T]); yield
            C.tt("pool", Wm[s][:], W[s][:], mUi, ALU.mult, ["W" + T, "cs"], ["Wm" + T]); yield
            C.tt("dve", AT[s][:], b0[:, 384:512], Wm[s][:], ALU.mult, ["Wm" + T], ["b0", "AT" + T]); yield
            C.tt("pool", XY[s][0][:, 0:128], KW[s][:], bU[s][:], ALU.mult, ["KW" + T, "bU" + T], ["XY%d_0" % s]); yield
            C.tt("pool", dabs[s][:], KW[s][:], mLs, ALU.mult, ["KW" + T, "cs"], ["dabs" + T]); yield
            C.ts("pool", XY[s][0][:, 128:256], dabs[s][:], bcol[:, nn], ALU.mult, ["dabs" + T, "bcol"], ["XY%d_0" % s]); yield
            C.tt("pool", Pm[s][0][:], identf, XY[s][0][:, 0:128], ALU.subtract, ["cs", "XY%d_0" % s], ["Pm%d_0" % s]); yield
            C.act(kbg[s][:], b2[:, 0:128], AF.Identity, ["sc1"], ["b2", "kbg" + T], scale=sc1[:, nn]); yield
            C.act(kd[s][:], b2[:, 0:128], AF.Identity, ["sc2"], ["b2", "kd" + T], scale=sc2[:, nn]); yield
            C.ts("dve", vb[s][:], b2[:, 128:256], bcol[:, nn], ALU.mult, ["bcol"], ["b2", "vb" + T]); yield
            C.tt("pool", qd[s][:], qn, egB[s][:], ALU.mult, ["qs%d" % bs, "egB" + T], ["qd" + T]); yield

        def inv(n):
            s = n % NSET
            T = "_%d" % s
            bx, bp = (b3, b4) if n % 2 == 0 else (b1, b7)
            kx, kp = ("b3", "b4") if n % 2 == 0 else ("b1", "b7")
            for k in range(1, 7):
                a, b = (k - 1) % 2, k % 2
                Xp, Yp = XY[s][a][:, 0:128], XY[s][a][:, 128:256]
                ka, kb_ = "XY%d_%d" % (s, a), "XY%d_%d" % (s, b)
                if k < 6:
                    C.mm(bx[:, 0:128], Yp, Xp, True, True, [ka], [kx]); yield
                C.mm(bx[:, 128:256], Xp, Yp, True, True, [ka], [kx]); yield
                if k < 6:
                    C.cp("act" if k % 2 else "dve", XY[s][b][:], bx[:, 0:256], [], [kx, kb_]); yield
                else:
                    C.cp("dve", XY[s][b][:, 128:256], bx[:, 128:256], [], [kx, kb_]); yield
                C.mm(bp[:, 0:128], XY[s][b][:, 128:256], Pm[s][a][:], True, True, [kb_, "Pm%d_%d" % (s, a)], [kp]); yield
                if k < 6:
                    C.tt("dve", Pm[s][b][:], bp[:, 0:128], Pm[s][a][:], ALU.add, ["Pm%d_%d" % (s, a)], [kp, "Pm%d_%d" % (s, b)]); yield
                else:
                    C.tt("dve", Pb[s][:], bp[:, 0:128], Pm[s][a][:], ALU.add, ["Pm%d_%d" % (s, a)], [kp, "Pb" + T]); yield
            C.mm(bp[:, 128:256], kbg[s][:], Pb[s][:], True, True, ["kbg" + T, "Pb" + T], [kp]); yield
            C.ts("dve", nwT[s][:], bp[:, 128:256], -1.0, ALU.mult, [], [kp, "nwT" + T]); yield

        def rec(n):
            s = n % NSET
            bs = (n // TPB) % 2
            tl = slice((n % TPB) * 128, (n % TPB) * 128 + 128)
            nn = slice(n, n + 1)
            T = "_%d" % s
            C.mm(b5[:, 0:128], Pb[s][:], vb[s][:], True, False, ["Pb" + T, "vb" + T], ["b5"]); yield
            C.mm(b5[:, 0:128], nwT[s][:], Sb[:], False, True, ["nwT" + T, "Sb"], ["b5"]); yield
            C.cp("dve", vn[s][:], b5[:, 0:128], [], ["b5", "vn" + T]); yield
            C.mm(b6[:, 0:128], qd[s][:], Sb[:], True, False, ["qd" + T, "Sb"], ["b6"]); yield
            C.mm(b6[:, 0:128], AT[s][:], vn[s][:], False, True, ["AT" + T, "vn" + T], ["b6"]); yield
            C.mm(b6[:, 128:256], kd[s][:], vn[s][:], True, True, ["kd" + T, "vn" + T], ["b6"]); yield
            C.stt("dve", S[:], S[:], egl[:, nn], b6[:, 128:256], ALU.mult, ALU.add, ["S", "egl"], ["b6", "S"]); yield
            C.cp("act", Sb[:], S[:], ["S"], ["Sb"]); yield
            C.act(junk[s][:], b6[:, 0:128], AF.Square, [], ["b6", "junk" + T, "ss" + T], accum=ss[s][:]); yield
            C.act(rt[s][:], ss[s][:], AF.Sqrt, ["ss" + T], ["rt" + T], bias=RMS_EPS, scale=1.0 / 128); yield
            C.recip(rs[s][:], rt[s][:], ["rt" + T], ["rs" + T]); yield
            C.act(on[s][:], b6[:, 0:128], AF.Identity, ["rs" + T], ["b6", "on" + T], scale=rs[s][:]); yield
            C.tr(b2[:, 256:384], on[s][:], identb[:], ["on" + T, "identb"], ["b2"]); yield
            C.stt("dve", osb[bs][:, tl], b2[:, 256:384], nws[:, 0:1], zs[bs][:, tl], ALU.mult, ALU.mult,
                  ["nws", "zs%d" % bs], ["b2", "os%d" % bs]); yield
            if n % TPB == TPB - 1:
                blk = n // TPB
                d = P.dma("sp", oT[:, blk * TPB * 128:(blk + 1) * TPB * 128], osb[bs][:], reads=["os%d" % bs], key="oo%d" % bs)
                P.must_finish(d)

        def chain(*gs):
            for g in gs:
                if g is not None:
                    yield from g

        load_block(0)
        npair = ntile // 2
        for j in range(npair + 2):
            gens = []
            if j < npair:
                gens += [chain(setup(2 * j), setup(2 * j + 1))]
            if 1 <= j <= npair:
                gens += [inv(2 * j - 2), inv(2 * j - 1)]
            if j >= 2:
                gens += [chain(rec(2 * j - 4), rec(2 * j - 3))]
            _interleave(gens)
            if j >= 1 and (j - 1) % (TPB // 2) == 0 and (j - 1) // (TPB // 2) < nblk and j > 1:
                pass
            if j % (TPB // 2) == 1 and (j // (TPB // 2)) + 1 < nblk:
                load_block(j // (TPB // 2) + 1)
        P.emit()
    return nc


def b_consts():
    p = np.arange(128)[:, None]
    f = np.arange(128)[None, :]
    c = np.zeros((128, 4, 128), np.float32)
    c[:, 0] = (p == f)
    c[:, 1] = (f >= p)
    c[:, 2] = (f > p)
    c[:, 3] = (p > f)
    return c


TBC = 1024
FG = 11
NFG = FC // FG


def build_C():
    nc = bass.Bass("TRN2", target_bir_lowering=False)
    with ExitStack() as st:
        C = Ctx(nc, st)
        P = C.P
        catT = C.din("catT", [2048, NT], BF16)
        xTd = C.din("xT", [D_MODEL, NT])
        wout = C.din("wout", [2048, D_MODEL])
        wgur = C.din("wgur", [D_MODEL, 2 * D_FF])
        wd = C.din("wd", [D_FF, D_MODEL])
        lnp = C.din("lnp", [D_MODEL, 4])
        x2T = C.dout("x2T", [D_MODEL, NT])

        xb = C.sb("xb", [128, KC, TBC], F32)
        x1b = C.sb("x1b", [128, KC, TBC], BF16)
        catb = C.sb("catb", [128, KC * TBC], BF16)
        catv = catb[:].rearrange("p (k t) -> p k t", k=KC)
        hidv = catb[:, 0:FG * TBC].rearrange("p (k t) -> p k t", k=FG)
        ws = [C.sb("ws%d" % i, [128, KC, 256], BF16) for i in range(2)]
        wds = [C.sb("wds%d" % i, [128, FG, 256], BF16) for i in range(2)]
        lnps = C.sb("lnps", [128, KC, 4], F32)
        onesb = C.sb("onesb", [128, 128], BF16)
        sg = [C.sb("sg%d" % i, [128, 512], F32) for i in range(2)]
        rb = [C.sb("rb%d" % i, [128, 512], BF16) for i in range(2)]
        sq = [C.sb("sq%d" % i, [128, 512], BF16) for i in range(2)]
        t1 = [C.sb("t1_%d" % i, [128, 512], F32) for i in range(2)]
        t2 = [C.sb("t2_%d" % i, [128, 512], F32) for i in range(2)]
        mean = [C.sb("mean%d" % i, [128, 512], F32) for i in range(2)]
        rstd = [C.sb("rstd%d" % i, [128, 512], F32) for i in range(2)]
        tmpa = C.sb("tmpa", [128, 512], F32)
        pb = [C.bank("pb%d" % i) for i in range(6)]
        pst = [C.bank("pst%d" % i) for i in range(2)]

        P.dma("sp", lnps[:], lnp.rearrange("(k p) j -> p k j", p=128), writes=["lnps"], key="c0")
        C.memset("dve", onesb[:], 1.0, ["onesb"])
        state = {"ws": 0, "wds": 0, "pb": 0}
        wov = wout.rearrange("(k p) c -> p k c", p=128)
        wgv = wgur.rearrange("(k p) c -> p k c", p=128)
        wdv = wd.rearrange("(k p) c -> p k c", p=128)
        xv = xTd.rearrange("(k p) t -> p k t", p=128)
        cv = catT.rearrange("(k p) t -> p k t", p=128)
        ov = x2T.rearrange("(k p) t -> p k t", p=128)
        xkeys = ["xb%d" % k for k in range(KC)]

        def load_ws(src, col0):
            s = state["ws"]
            state["ws"] ^= 1
            for h in range(4):
                P.dma("pool", ws[s][:, 4 * h:4 * h + 4, :], src[:, 4 * h:4 * h + 4, col0:col0 + 256], writes=["ws%d" % s], key="ws%d" % s)
            return s

        def nbank():
            b = state["pb"]
            state["pb"] = (b + 1) % 6
            return b

        def layer_norm(blk, gi, bi, make_bf16):
            for hf in range(2):
                hs = slice(512 * hf, 512 * hf + 512)
                for k in range(KC):
                    q = k % 2
                    C.cp("act", rb[q][:], xb[:, k, hs], ["xb%d" % k], ["rb%d" % q])
                    C.act(sq[q][:], xb[:, k, hs], AF.Square, ["xb%d" % k], ["sq%d" % q])
                    C.mm(pst[0][:], onesb[:], rb[q][:], k == 0, k == KC - 1, ["onesb", "rb%d" % q], ["pst0"])
                    C.mm(pst[1][:], onesb[:], sq[q][:], k == 0, k == KC - 1, ["onesb", "sq%d" % q], ["pst1"])
                C.ts("dve", mean[hf][:], pst[0][:], 1.0 / D_MODEL, ALU.mult, [], ["pst0", "mean%d" % hf])
                C.tt("dve", tmpa[:], mean[hf][:], mean[hf][:], ALU.mult, ["mean%d" % hf], ["tmpa"])
                C.stt("dve", tmpa[:], pst[1][:], 1.0 / D_MODEL, tmpa[:], ALU.mult, ALU.subtract, ["tmpa"], ["pst1", "tmpa"])
                C.act(tmpa[:], tmpa[:], AF.Sqrt, ["tmpa"], ["tmpa"], bias=LN_EPS)
                C.recip(rstd[hf][:], tmpa[:], ["tmpa"], ["rstd%d" % hf])
                for k in range(KC):
                    q = k % 2
                    C.tt("dve", t1[q][:], xb[:, k, hs], mean[hf][:], ALU.subtract, ["xb%d" % k, "mean%d" % hf], ["t1_%d" % q])
                    C.tt("pool", t2[q][:], t1[q][:], rstd[hf][:], ALU.mult, ["t1_%d" % q, "rstd%d" % hf], ["t2_%d" % q])
                    C.act(xb[:, k, hs], t2[q][:], AF.Identity, ["t2_%d" % q, "lnps"], ["xb%d" % k],
                          bias=lnps[:, k, bi:bi + 1], scale=lnps[:, k, gi:gi + 1])
                    if make_bf16:
                        C.cp("dve", x1b[:, k, hs], xb[:, k, hs], ["xb%d" % k], ["x1b%d" % k])

        for blk in range(NT // TBC):
            ts_ = slice(blk * TBC, (blk + 1) * TBC)
            for k in range(KC):
                P.dma("sp", xb[:, k, :], xv[:, k, ts_], writes=["xb%d" % k], key="lx%d" % k)
            for k in range(KC):
                P.dma("sp", catv[:, k, :], cv[:, k, ts_], writes=["catbuf"], key="lc")
            for cg in range(8):
                s = load_ws(wov, 256 * cg)
                for half in range(2):
                    c = 2 * cg + half
                    for hf in range(2):
                        hs = slice(512 * hf, 512 * hf + 512)
                        b = nbank()
                        for m in range(KC):
                            C.mm(pb[b][:], ws[s][:, m, 128 * half:128 * half + 128], catv[:, m, hs], m == 0, m == KC - 1,
                                 ["ws%d" % s, "catbuf"], ["pb%d" % b])
                        C.stt("dve", xb[:, c, hs], xb[:, c, hs], ALPHA, pb[b][:], ALU.mult, ALU.add, ["xb%d" % c], ["pb%d" % b, "xb%d" % c])
            layer_norm(blk, 0, 1, True)
            for fg in range(NFG):
                for fi in range(FG):
                    f = fg * FG + fi
                    s = load_ws(wgv, 256 * f)
                    for hf in range(2):
                        hs = slice(512 * hf, 512 * hf + 512)
                        bg_ = nbank()
                        for k in range(KC):
                            C.mm(pb[bg_][:], ws[s][:, k, 0:128], x1b[:, k, hs], k == 0, k == KC - 1, ["ws%d" % s, "x1b%d" % k], ["pb%d" % bg_])
                        bu = nbank()
                        for k in range(KC):
                            C.mm(pb[bu][:], ws[s][:, k, 128:256], x1b[:, k, hs], k == 0, k == KC - 1, ["ws%d" % s, "x1b%d" % k], ["pb%d" % bu])
                        q = hf
                        C.act(sg[q][:], pb[bg_][:], AF.Silu, [], ["pb%d" % bg_, "sg%d" % q])
                        C.tt("dve", hidv[:, fi, hs], pb[bu][:], sg[q][:], ALU.mult, ["sg%d" % q], ["pb%d" % bu, "catbuf"])
                for cg in range(8):
                    s = state["wds"]
                    state["wds"] ^= 1
                    P.dma("pool", wds[s][:], wdv[:, fg * FG:(fg + 1) * FG, 256 * cg:256 * cg + 256], writes=["wds%d" % s], key="wds%d" % s)
                    for half in range(2):
                        c = 2 * cg + half
                        for hf in range(2):
                            hs = slice(512 * hf, 512 * hf + 512)
                            b = nbank()
                            for fi in range(FG):
                                C.mm(pb[b][:], wds[s][:, fi, 128 * half:128 * half + 128], hidv[:, fi, hs], fi == 0, fi == FG - 1,
                                     ["wds%d" % s, "catbuf"], ["pb%d" % b])
                            if fg == 0:
                                C.stt("dve", xb[:, c, hs], xb[:, c, hs], ALPHA, pb[b][:], ALU.mult, ALU.add, ["xb%d" % c], ["pb%d" % b, "xb%d" % c])
                            else:
                                C.tt("dve", xb[:, c, hs], xb[:, c, hs], pb[b][:], ALU.add, ["xb%d" % c], ["pb%d" % b, "xb%d" % c])
            layer_norm(blk, 2, 3, False)
            for k in range(KC):
                d = P.dma("sp", ov[:, k, ts_], xb[:, k, :], reads=["xb%d" % k], key="ox%d" % k)
                P.must_finish(d)
        P.emit()
    return nc


def host_inputs_C(w_out, w_gate_up, w_down, ln1_g, ln1_b, ln2_g, ln2_b):
    perm = []
    for f in range(FC):
        perm += list(range(128 * f, 128 * f + 128)) + list(range(D_FF + 128 * f, D_FF + 128 * f + 128))
    wgur = np.ascontiguousarray(w_gate_up[:, perm])
    lnp = np.ascontiguousarray(np.stack([ln1_g, ln1_b, ln2_g, ln2_b], axis=1))
    return {"wout": np.ascontiguousarray(w_out), "wgur": wgur, "wd": np.ascontiguousarray(w_down), "lnp": lnp}


_PROGS = {}


def _prog(name):
    if name not in _PROGS:
        _PROGS[name] = {"A": build_A, "B": build_B, "C": build_C}[name]()
    return _PROGS[name]


def _run(name, maps):
    res = run_bass_kernel_spmd(_prog(name), maps, core_ids=list(range(NCORES)))
    return res.results


def kernel(x, w_in, conv_dw_w, conv_dw_b, conv_ln_g, conv_ln_b, gdn_conv_w, gdn_A_log,
           gdn_dt_bias, gdn_norm_w, w_out, ln1_g, ln1_b, w_gate_up, w_down, ln2_g, ln2_b):
    f = lambda a: np.asarray(a, dtype=np.float32)
    xcur = f(x)[0]
    xT_cores = [np.ascontiguousarray(xcur[c * NT:(c + 1) * NT].T) for c in range(NCORES)]
    consts = b_consts()
    for l in range(DEPTH):
        mapsA = host_inputs_A_T(xT_cores, f(w_in[l]), f(conv_dw_w[l]), f(conv_dw_b[l]), f(conv_ln_g[l]), f(conv_ln_b[l]),
                                f(gdn_conv_w[l]), f(gdn_A_log[l]), f(gdn_dt_bias[l]))
        ra = _run("A", mapsA)
        mapsB = []
        for h in range(NHEAD):
            cat = lambda nm, r0: np.ascontiguousarray(np.concatenate([np.asarray(ra[c][nm])[r0:r0 + 128] for c in range(NCORES)], axis=1))
            beta = np.concatenate([np.asarray(ra[c]["bg"])[h] for c in range(NCORES)])
            g = np.concatenate([np.asarray(ra[c]["bg"])[NHEAD + h] for c in range(NCORES)])
            mapsB.append({"qT": cat("qkvT", 128 * h), "kT": cat("qkvT", 1024 + 128 * h), "vT": cat("qkvT", 2048 + 128 * h),
                          "gzT": cat("gzT", 128 * h),
                          "bgT": np.ascontiguousarray(np.stack([beta.reshape(NTILE, 128), g.reshape(NTILE, 128)])),
                          "consts": consts, "normw": np.ascontiguousarray(f(gdn_norm_w[l]).reshape(128, 1))})
        rb = _run("B", mapsB)
        wc = host_inputs_C(f(w_out[l]), f(w_gate_up[l]), f(w_down[l]), f(ln1_g[l]), f(ln1_b[l]), f(ln2_g[l]), f(ln2_b[l]))
        mapsC = []
        for c in range(NCORES):
            catT = np.concatenate([np.asarray(ra[c]["convT"])] + [np.asarray(rb[h]["oT"])[:, c * NT:(c + 1) * NT] for h in range(NHEAD)], axis=0)
            m = dict(wc)
            m["catT"] = np.ascontiguousarray(catT)
            m["xT"] = xT_cores[c]
            mapsC.append(m)
        rc = _run("C", mapsC)
        xT_cores = [np.ascontiguousarray(np.asarray(rc[c]["x2T"])) for c in range(NCORES)]
    out = np.concatenate([xT_cores[c].T for c in range(NCORES)], axis=0)
    return np.ascontiguousarray(out[None]).astype(np.float32)


def host_inputs_A_T(xT_cores, w_in, conv_dw_w, conv_dw_b, conv_ln_g, conv_ln_b, gdn_conv_w, A_log, dt_bias):
    perm = []
    for i in range(8):
        perm += list(range(128 * i, 128 * i + 128)) + list(range(1024 + 128 * i, 1024 + 128 * i + 128))
    perm += list(range(2048, IN_COLS))
    winr = np.ascontiguousarray(w_in[:, perm])
    cdw = np.ascontiguousarray(conv_dw_w.T)
    cvec = np.ascontiguousarray(np.stack([conv_dw_b, conv_ln_g, conv_ln_b], axis=1))
    gcw = np.ascontiguousarray(gdn_conv_w.T)
    hv = np.ascontiguousarray(np.stack([A_log, dt_bias], axis=1))
    ident = np.eye(128, dtype=np.float32)
    maps = []
    for c in range(len(xT_cores)):
        xt = np.zeros((D_MODEL, NTX), np.float32)
        xt[:, HALO:] = xT_cores[c]
        if c > 0:
            xt[:, :HALO] = xT_cores[c - 1][:, NT - HALO:]
        maps.append({"xT": xt, "winr": winr, "cdw": cdw, "cvec": cvec, "gcw": gcw, "hv": hv, "ident": ident})
    return maps
```

```python
import numpy as np
from contextlib import ExitStack
import concourse.bass as bass
import concourse.mybir as mybir
from concourse.bass_utils import run_bass_kernel_spmd

F32 = mybir.dt.float32
BF16 = mybir.dt.bfloat16
ALU = mybir.AluOpType
AF = mybir.ActivationFunctionType

ENGS = ("pe", "act", "dve", "pool", "sp")

D_MODEL = 2048
SEQ = 16384
DEPTH = 2
NCORES = 8
NT = SEQ // NCORES
HALO = 32
NTX = NT + HALO
KC = D_MODEL // 128
CONV_CH = 1024
GDN_W = 1024
NHEAD = 8
D_FF = 5632
FC = D_FF // 128
IN_COLS = 6160
ALPHA = (2 * DEPTH) ** 0.25
LN_EPS = 1e-5
RMS_EPS = 1e-6
L2_EPS = 1e-6


class _Op:
    __slots__ = ("eng", "fn", "deps", "is_dma", "semkey", "signal", "count", "inc")

    def __init__(self, eng, fn, is_dma, semkey, inc=16):
        self.inc = inc
        self.eng = eng
        self.fn = fn
        self.deps = []
        self.is_dma = is_dma
        self.semkey = semkey
        self.signal = False
        self.count = 0


class Prog:
    def __init__(self, nc):
        self.nc = nc
        self.ops = {e: [] for e in ENGS}
        self.last_w = {}
        self.readers = {}
        self.final_waits = []

    def _add(self, eng, fn, reads, writes, is_dma=False, semkey=None, inc=16):
        op = _Op(eng, fn, is_dma, semkey, inc)
        deps = {}
        for r in reads:
            w = self.last_w.get(r)
            if w is not None:
                deps[id(w)] = (w, "raw")
        for wkey in writes:
            for rd in self.readers.get(wkey, ()):
                if id(rd) not in deps:
                    deps[id(rd)] = (rd, "war")
            w = self.last_w.get(wkey)
            if w is not None and id(w) not in deps:
                deps[id(w)] = (w, "waw")
        for r in reads:
            self.readers.setdefault(r, []).append(op)
        for wkey in writes:
            self.last_w[wkey] = op
            self.readers[wkey] = []
        for (d, kind) in deps.values():
            if d is op:
                continue
            if (not d.is_dma) and (not is_dma) and d.eng == eng:
                if eng == "pe":
                    continue
            op.deps.append(d)
            d.signal = True
        self.ops[eng].append(op)
        return op

    def op(self, eng, fn, reads=(), writes=()):
        return self._add(eng, fn, tuple(reads), tuple(writes))

    def dma(self, eng, out, in_, reads=(), writes=(), key=None):
        fn = lambda e: e.dma_start(out=out, in_=in_)
        return self._add(eng, fn, tuple(reads), tuple(writes), is_dma=True, semkey=key)

    def collective(self, kind, ins, outs, reads, writes, key):
        rg = [list(range(NCORES))]
        fn = lambda e: e.collective_compute(kind, ALU.bypass, replica_groups=rg, ins=[a.opt() for a in ins], outs=[a.opt() for a in outs])
        return self._add("pool", fn, tuple(reads), tuple(writes), is_dma=True, semkey=key, inc=1)

    def must_finish(self, op):
        op.signal = True
        self.final_waits.append(op)

    def emit(self):
        nc = self.nc
        keycount = {}
        for e in ENGS:
            c = 0
            for op in self.ops[e]:
                if op.is_dma:
                    k = op.semkey
                    keycount[k] = keycount.get(k, 0) + op.inc
                    op.count = keycount[k]
                elif op.signal:
                    c += 1
                    op.count = c
        with ExitStack() as st:
            esem = {e: st.enter_context(nc.semaphore("s_" + e)) for e in ENGS}
            ksem = {k: st.enter_context(nc.semaphore("k_" + str(k))) for k in keycount}
            block = st.enter_context(nc.Block())

            def run(engname, eh):
                waited = {}
                for op in self.ops[engname]:
                    need = {}
                    for d in op.deps:
                        s = ("k", d.semkey) if d.is_dma else ("e", d.eng)
                        if d.count > need.get(s, 0):
                            need[s] = d.count
                    for s, v in need.items():
                        if waited.get(s, 0) >= v:
                            continue
                        sem = ksem[s[1]] if s[0] == "k" else esem[s[1]]
                        eh.wait_ge(sem, v)
                        waited[s] = v
                    ins = op.fn(eh)
                    if op.is_dma:
                        ins.then_inc(ksem[op.semkey], op.inc)
                    elif op.signal:
                        ins.then_inc(esem[engname], 1)
                if engname == "sp":
                    for op in self.final_waits:
                        sem = ksem[op.semkey] if op.is_dma else esem[op.eng]
                        eh.wait_ge(sem, op.count)

            block.tensor(lambda eh: run("pe", eh))
            block.scalar(lambda eh: run("act", eh))
            block.vector(lambda eh: run("dve", eh))
            block.gpsimd(lambda eh: run("pool", eh))
            block.sync(lambda eh: run("sp", eh))


class Ctx:
    def __init__(self, nc, st):
        self.nc = nc
        self.st = st
        self.P = Prog(nc)
        self.nbank = 0

    def sb(self, name, shape, dt):
        return self.st.enter_context(self.nc.sbuf_tensor(name, list(shape), dt))

    def bank(self, name):
        self.nbank += 1
        assert self.nbank <= 8
        return self.st.enter_context(self.nc.psum_tensor(name, [128, 512], F32))

    def din(self, name, shape, dt=F32):
        return self.nc.dram_tensor(name, list(shape), dt, kind="ExternalInput").ap()

    def dout(self, name, shape, dt=F32):
        return self.nc.dram_tensor(name, list(shape), dt, kind="ExternalOutput").ap()

    def mm(self, out, lhsT, rhs, start, stop, r, w):
        self.P.op("pe", lambda e: e.matmul(out, lhsT=lhsT, rhs=rhs, start=start, stop=stop), r, w)

    def tr(self, out, in_, ident, r, w):
        self.P.op("pe", lambda e: e.transpose(out, in_, ident), r, w)

    def act(self, out, in_, func, r, w, bias=None, scale=None, accum=None, eng="act"):
        kw = {}
        if bias is not None:
            kw["bias"] = bias
        if scale is not None:
            kw["scale"] = scale
        if accum is not None:
            kw["accum_out"] = accum
        self.P.op(eng, lambda e: e.activation(out=out, in_=in_, func=func, **kw), r, w)

    def tt(self, eng, out, in0, in1, op, r, w):
        self.P.op(eng, lambda e: e.tensor_tensor(out=out, in0=in0, in1=in1, op=op), r, w)

    def ts(self, eng, out, in0, s1, op0, r, w, s2=None, op1=None):
        if op1 is None:
            self.P.op(eng, lambda e: e.tensor_scalar(out=out, in0=in0, scalar1=s1, scalar2=None, op0=op0), r, w)
        else:
            self.P.op(eng, lambda e: e.tensor_scalar(out=out, in0=in0, scalar1=s1, scalar2=s2, op0=op0, op1=op1), r, w)

    def stt(self, eng, out, in0, scalar, in1, op0, op1, r, w):
        self.P.op(eng, lambda e: e.scalar_tensor_tensor(out=out, in0=in0, scalar=scalar, in1=in1, op0=op0, op1=op1), r, w)

    def cp(self, eng, out, in_, r, w):
        if eng == "act":
            self.P.op(eng, lambda e: e.copy(out=out, in_=in_), r, w)
        else:
            self.P.op(eng, lambda e: e.tensor_copy(out=out, in_=in_), r, w)

    def recip(self, out, in_, r, w):
        self.P.op("dve", lambda e: e.reciprocal(out=out, in_=in_), r, w)

    def memset(self, eng, ap, val, w):
        self.P.op(eng, lambda e: e.memset(ap, val), (), w)


def build_A():
    nc = bass.Bass("TRN2", target_bir_lowering=False)
    with ExitStack() as st:
        C = Ctx(nc, st)
        P = C.P
        xT = C.din("xT", [D_MODEL, NTX])
        winr = C.din("winr", [D_MODEL, IN_COLS])
        cdw = C.din("cdw", [CONV_CH, 31])
        cvec = C.din("cvec", [CONV_CH, 3])
        gcw = C.din("gcw", [3 * GDN_W, 4])
        hv = C.din("hv", [NHEAD, 2])
        identd = C.din("ident", [128, 128])
        convT = C.dout("convT", [CONV_CH, NT], BF16)
        qkvT = C.dout("qkvT", [3 * GDN_W, NT], BF16)
        gzT = C.dout("gzT", [GDN_W, NT], BF16)
        bgo = C.dout("bg", [2 * NHEAD, NT], F32)

        xs = C.sb("xs", [128, KC, NTX], BF16)
        wb = [C.sb("wb%d" % i, [128, KC, 256], BF16) for i in range(2)]
        wsm = C.sb("wsm", [128, KC, 16], BF16)
        identf = C.sb("identf", [128, 128], F32)
        identb = C.sb("identb", [128, 128], BF16)
        onesb = C.sb("onesb", [128, 128], BF16)
        cdws = C.sb("cdws", [128, 8, 31], F32)
        cvecs = C.sb("cvecs", [128, 8, 3], F32)
        gcws = C.sb("gcws", [128, 24, 4], F32)
        hvs = C.sb("hvs", [NHEAD, 2], F32)
        negA = C.sb("negA", [NHEAD, 1], F32)
        Dc = [C.sb("Dc%d" % i, [128, 31, 128], BF16) for i in range(2)]
        Dq = [C.sb("Dq%d" % i, [128, 4, 128], BF16) for i in range(2)]
        hbuf = [C.sb("hbuf%d" % i, [128, NTX], BF16) for i in range(2)]
        ubuf = [C.sb("ubuf%d" % i, [128, NTX], BF16) for i in range(2)]
        ybuf = C.sb("ybuf", [128, 8, NT], BF16)
        sg = [C.sb("sg%d" % i, [128, 512], F32) for i in range(2)]
        ysq = [C.sb("ysq%d" % i, [128, 512], BF16) for i in range(2)]
        ost = [C.sb("ost%d" % i, [128, NT], BF16) for i in range(3)]
        mean = C.sb("mean", [128, 512], F32)
        rstd = C.sb("rstd", [128, 512], F32)
        tmpa = C.sb("tmpa", [128, 512], F32)
        t1 = [C.sb("t1_%d" % i, [128, 512], F32) for i in range(2)]
        t2 = [C.sb("t2_%d" % i, [128, 512], F32) for i in range(2)]
        sv = [C.sb("sv%d" % i, [128, 512], F32) for i in range(2)]
        sv4 = [C.sb("sv4_%d" % i, [128, 512], F32) for i in range(4)]
        ysq4 = [C.sb("ysq4_%d" % i, [128, 512], BF16) for i in range(4)]
        bgs = C.sb("bgs", [NHEAD, 2, 512], F32)
        sp1 = C.sb("sp1", [NHEAD, 512], F32)
        sp2 = C.sb("sp2", [NHEAD, 512], F32)

        pu = [C.bank("pu%d" % i) for i in range(4)]
        pc = [C.bank("pc%d" % i) for i in range(2)]
        pst = [C.bank("pst%d" % i) for i in range(2)]

        xv = xT.rearrange("(k p) t -> p k t", p=128)
        for k in range(KC):
            P.dma("pool", xs[:, k, :], xv[:, k, :], writes=["xs%d" % k], key="xs%d" % k)
        P.dma("sp", identf[:], identd, writes=["identf"], key="c0")
        P.dma("sp", cdws[:], cdw.rearrange("(i p) j -> p i j", p=128), writes=["cdws"], key="c1")
        P.dma("sp", cvecs[:], cvec.rearrange("(i p) j -> p i j", p=128), writes=["cvecs"], key="c2")
        P.dma("sp", gcws[:], gcw.rearrange("(i p) j -> p i j", p=128), writes=["gcws"], key="c3")
        P.dma("sp", hvs[:], hv, writes=["hvs"], key="c4")
        C.cp("dve", identb[:], identf[:], ["identf"], ["identb"])
        C.memset("dve", onesb[:], 1.0, ["onesb"])
        C.act(negA[:], hvs[:, 0:1], AF.Exp, ["hvs"], ["negA"])
        C.ts("dve", negA[:], negA[:], -1.0, ALU.mult, ["negA"], ["negA"])

        xs_keys = ["xs%d" % k for k in range(KC)]
        wv = winr.rearrange("(k p) c -> p k c", p=128)
        blocks = [(0, HALO)] + [(HALO + 512 * i, 512) for i in range(4)]

        state = {"wslot": 0, "pu": 0, "pc": 0, "ost": 0, "ngrp": 0}

        def load_w(col0):
            s = state["wslot"]
            state["wslot"] ^= 1
            for h in range(4):
                P.dma("pool", wb[s][:, 4 * h:4 * h + 4, :], wv[:, 4 * h:4 * h + 4, col0:col0 + 256],
                      writes=["wb%d" % s], key="wb%d" % s)
            return s

        def inproj(s, off, c0, n):
            b = state["pu"]
            state["pu"] = (b + 1) % 4
            for k in range(KC):
                C.mm(pu[b][:, 0:n], wb[s][:, k, off:off + 128], xs[:, k, c0:c0 + n], k == 0, k == KC - 1,
                     ["wb%d" % s, "xs%d" % k], ["pu%d" % b])
            return b

        def next_ost():
            o = state["ost"]
            state["ost"] = (o + 1) % 3
            return o

        for i in range(8):
            s = load_w(256 * i)
            hs = i % 2
            P.op("pool", lambda e, hs=hs, i=i: e.tensor_tensor(
                out=Dc[hs][:], in0=identb[:].unsqueeze(1).to_broadcast([128, 31, 128]),
                in1=cdws[:, i, :].unsqueeze(2).to_broadcast([128, 31, 128]), op=ALU.mult),
                ["identb", "cdws"], ["Dc%d" % hs])
            for bi, (c0, n) in enumerate(blocks):
                ba = inproj(s, 0, c0, n)
                bb = inproj(s, 128, c0, n)
                q = (ba // 2) % 2
                C.act(sg[q][:, 0:n], pu[bb][:, 0:n], AF.Sigmoid, [], ["pu%d" % bb, "sg%d" % q])
                C.tt("dve", hbuf[hs][:, c0:c0 + n], pu[ba][:, 0:n], sg[q][:, 0:n], ALU.mult,
                     ["sg%d" % q], ["pu%d" % ba, "hbuf%d_%d" % (hs, bi)])
            for tb in range(4):
                b = state["pc"]
                state["pc"] ^= 1
                base = HALO + 512 * tb - 30
                for j in range(31):
                    C.mm(pc[b][:], Dc[hs][:, j, :], hbuf[hs][:, base + j:base + j + 512], j == 0, j == 30,
                         ["Dc%d" % hs, "hbuf%d_%d" % (hs, tb), "hbuf%d_%d" % (hs, tb + 1)], ["pc%d" % b])
                C.act(ybuf[:, i, 512 * tb:512 * tb + 512], pc[b][:], AF.Identity, ["cvecs"], ["pc%d" % b, "y%d_%d" % (i, tb)],
                      bias=cvecs[:, i, 0:1])

        for tb in range(4):
            tsl = slice(512 * tb, 512 * tb + 512)
            for i in range(8):
                C.mm(pst[0][:], onesb[:], ybuf[:, i, tsl], i == 0, i == 7, ["onesb", "y%d_%d" % (i, tb)], ["pst0"])
            for i in range(8):
                q = i % 2
                C.act(ysq[q][:], ybuf[:, i, tsl], AF.Square, ["y%d_%d" % (i, tb)], ["ysq%d" % q])
                C.mm(pst[1][:], onesb[:], ysq[q][:], i == 0, i == 7, ["onesb", "ysq%d" % q], ["pst1"])
            C.ts("dve", mean[:], pst[0][:], 1.0 / CONV_CH, ALU.mult, [], ["pst0", "mean"])
            C.tt("dve", tmpa[:], mean[:], mean[:], ALU.mult, ["mean"], ["tmpa"])
            C.stt("dve", tmpa[:], pst[1][:], 1.0 / CONV_CH, tmpa[:], ALU.mult, ALU.subtract, ["tmpa"], ["pst1", "tmpa"])
            C.act(tmpa[:], tmpa[:], AF.Sqrt, ["tmpa"], ["tmpa"], bias=LN_EPS)
            C.recip(rstd[:], tmpa[:], ["tmpa"], ["rstd"])
            for i in range(8):
                q = i % 2
                C.tt("dve", t1[q][:], ybuf[:, i, tsl], mean[:], ALU.subtract, ["y%d_%d" % (i, tb), "mean"], ["t1_%d" % q])
                C.tt("pool", t2[q][:], t1[q][:], rstd[:], ALU.mult, ["t1_%d" % q, "rstd"], ["t2_%d" % q])
                C.act(ysq[q][:], t2[q][:], AF.Silu, ["t2_%d" % q, "cvecs"], ["ysq%d" % q],
                      bias=cvecs[:, i, 2:3], scale=cvecs[:, i, 1:2])
                o = P.dma("sp", convT[128 * i:128 * i + 128, tsl], ysq[q][:], reads=["ysq%d" % q], key="oc%d" % q)
                P.must_finish(o)

        for g in range(12):
            s = load_w(2048 + 256 * g)
            for half in range(2):
                ch = 2 * g + half
                us = ch % 2
                P.op("pool", lambda e, us=us, ch=ch: e.tensor_tensor(
                    out=Dq[us][:], in0=identb[:].unsqueeze(1).to_broadcast([128, 4, 128]),
                    in1=gcws[:, ch, :].unsqueeze(2).to_broadcast([128, 4, 128]), op=ALU.mult),
                    ["identb", "gcws"], ["Dq%d" % us])
                for bi, (c0, n) in enumerate(blocks):
                    b = inproj(s, 128 * half, c0, n)
                    C.cp("act" if bi % 2 == 0 else "dve", ubuf[us][:, c0:c0 + n], pu[b][:, 0:n], [],
                         ["pu%d" % b, "ubuf%d_%d" % (us, bi)])
                o = next_ost()
                for tb in range(4):
                    tsl = slice(512 * tb, 512 * tb + 512)
                    b = state["pc"]
                    state["pc"] ^= 1
                    base = HALO + 512 * tb - 3
                    for j in range(4):
                        C.mm(pc[b][:], Dq[us][:, j, :], ubuf[us][:, base + j:base + j + 512], j == 0, j == 3,
                             ["Dq%d" % us, "ubuf%d_%d" % (us, tb), "ubuf%d_%d" % (us, tb + 1)], ["pc%d" % b])
                    if ch >= 16:
                        C.act(ost[o][:, tsl], pc[b][:], AF.Silu, [], ["pc%d" % b, "ost%d" % o])
                    else:
                        C.act(sv4[tb][:], pc[b][:], AF.Silu, [], ["pc%d" % b, "sv4_%d" % tb])
                        C.tt("pool", ysq4[tb][:], sv4[tb][:], sv4[tb][:], ALU.mult, ["sv4_%d" % tb], ["ysq4_%d" % tb])
                if ch < 16:
                    for tb in range(4):
                        tsl = slice(512 * tb, 512 * tb + 512)
                        q = tb % 2
                        C.mm(pst[q][:], onesb[:], ysq4[tb][:], True, True, ["onesb", "ysq4_%d" % tb], ["pst%d" % q])
                        C.act(t1[q][:], pst[q][:], AF.Sqrt, [], ["pst%d" % q, "t1_%d" % q], bias=L2_EPS)
                        C.recip(t2[q][:], t1[q][:], ["t1_%d" % q], ["t2_%d" % q])
                        if ch < 8:
                            C.stt("dve", ost[o][:, tsl], sv4[tb][:], 128.0 ** -0.5, t2[q][:], ALU.mult, ALU.mult,
                                  ["sv4_%d" % tb, "t2_%d" % q], ["ost%d" % o])
                        else:
                            C.tt("dve", ost[o][:, tsl], sv4[tb][:], t2[q][:], ALU.mult, ["sv4_%d" % tb, "t2_%d" % q], ["ost%d" % o])
                d = P.dma("sp", qkvT[128 * ch:128 * ch + 128, :], ost[o][:], reads=["ost%d" % o], key="oq%d" % o)
                P.must_finish(d)

        for g in range(4):
            s = load_w(5120 + 256 * g)
            for half in range(2):
                ch = 2 * g + half
                o = next_ost()
                for (c0, n) in blocks[1:]:
                    b = inproj(s, 128 * half, c0, n)
                    C.act(ost[o][:, c0 - HALO:c0 - HALO + n], pu[b][:, 0:n], AF.Silu, [], ["pu%d" % b, "ost%d" % o])
                d = P.dma("sp", gzT[128 * ch:128 * ch + 128, :], ost[o][:], reads=["ost%d" % o], key="oq%d" % o)
                P.must_finish(d)

        P.dma("pool", wsm[:], wv[:, :, 6144:6160], writes=["wsm"], key="wsm")
        for (c0, n) in blocks[1:]:
            tsl = slice(c0 - HALO, c0 - HALO + n)
            b = state["pu"]
            state["pu"] = (b + 1) % 4
            for k in range(KC):
                C.mm(pu[b][0:NHEAD, 0:n], wsm[:, k, 0:8], xs[:, k, c0:c0 + n], k == 0, k == KC - 1,
                     ["wsm", "xs%d" % k], ["pu%d" % b])
            C.act(bgs[:, 0, :], pu[b][0:NHEAD, 0:n], AF.Sigmoid, [], ["pu%d" % b, "bgs0"])
            b = state["pu"]
            state["pu"] = (b + 1) % 4
            for k in range(KC):
                C.mm(pu[b][0:NHEAD, 0:n], wsm[:, k, 8:16], xs[:, k, c0:c0 + n], k == 0, k == KC - 1,
                     ["wsm", "xs%d" % k], ["pu%d" % b])
            C.act(sp1[:, 0:n], pu[b][0:NHEAD, 0:n], AF.Abs, ["hvs"], ["pu%d" % b, "sp1"], bias=hvs[:, 1:2])
            C.act(sp1[:, 0:n], sp1[:, 0:n], AF.Exp, ["sp1"], ["sp1"], scale=-1.0)
            C.act(sp1[:, 0:n], sp1[:, 0:n], AF.Ln, ["sp1"], ["sp1"], bias=1.0)
            C.ts("dve", sp2[:, 0:n], pu[b][0:NHEAD, 0:n], hvs[:, 1:2], ALU.add, ["hvs"], ["pu%d" % b, "sp2"], s2=0.0, op1=ALU.max)
            C.tt("dve", sp2[:, 0:n], sp2[:, 0:n], sp1[:, 0:n], ALU.add, ["sp1", "sp2"], ["sp2"])
            C.ts("dve", bgs[:, 1, :], sp2[:, 0:n], negA[:, 0:1], ALU.mult, ["sp2", "negA"], ["bgs1"])
            d = P.dma("sp", bgo.rearrange("(a h) t -> h a t", a=2)[:, :, tsl], bgs[:], reads=["bgs0", "bgs1"], key="obg")
            P.must_finish(d)

        P.emit()
    return nc


def host_inputs_A(x_full, w_in, conv_dw_w, conv_dw_b, conv_ln_g, conv_ln_b, gdn_conv_w, A_log, dt_bias):
    perm = []
    for i in range(8):
        perm += list(range(128 * i, 128 * i + 128)) + list(range(1024 + 128 * i, 1024 + 128 * i + 128))
    perm += list(range(2048, IN_COLS))
    winr = np.ascontiguousarray(w_in[:, perm])
    cdw = np.ascontiguousarray(conv_dw_w.T)
    cvec = np.ascontiguousarray(np.stack([conv_dw_b, conv_ln_g, conv_ln_b], axis=1))
    gcw = np.ascontiguousarray(gdn_conv_w.T)
    hv = np.ascontiguousarray(np.stack([A_log, dt_bias], axis=1))
    ident = np.eye(128, dtype=np.float32)
    ncore = x_full.shape[0] // NT
    maps = []
    for c in range(ncore):
        xt = np.zeros((D_MODEL, NTX), np.float32)
        xt[:, HALO:] = x_full[c * NT:(c + 1) * NT].T
        if c > 0:
            xt[:, :HALO] = x_full[c * NT - HALO:c * NT].T
        maps.append({"xT": xt, "winr": winr, "cdw": cdw, "cvec": cvec, "gcw": gcw, "hv": hv, "ident": ident})
    return maps


NTILE = SEQ // 128
TPB = 16


def _interleave(gens):
    gens = [g for g in gens if g is not None]
    while gens:
        nxt = []
        for g in gens:
            try:
                next(g)
                nxt.append(g)
            except StopIteration:
                pass
        gens = nxt


def build_B(ntile=NTILE):
    nc = bass.Bass("TRN2", target_bir_lowering=False)
    ntok = ntile * 128
    nblk = ntile // TPB
    with ExitStack() as st:
        C = Ctx(nc, st)
        P = C.P
        qT = C.din("qT", [128, ntok], BF16)
        kT = C.din("kT", [128, ntok], BF16)
        vT = C.din("vT", [128, ntok], BF16)
        gzT = C.din("gzT", [128, ntok], BF16)
        bgT = C.din("bgT", [2, ntile, 128])
        consts = C.din("consts", [128, 4, 128])
        normw = C.din("normw", [128, 1])
        oT = C.dout("oT", [128, ntok], BF16)

        cs = C.sb("cs", [128, 4, 128], F32)
        identf, mUi, mUs, mLs = cs[:, 0, :], cs[:, 1, :], cs[:, 2, :], cs[:, 3, :]
        identb = C.sb("identb", [128, 128], BF16)
        onesf = C.sb("onesf", [128, 128], F32)
        nws = C.sb("nws", [128, 1], F32)
        Gr = C.sb("Gr", [128, 2, 128], F32)
        gcol = C.sb("gcol", [128, ntile], F32)
        bcol = C.sb("bcol", [128, ntile], F32)
        gc = C.sb("gc", [128, ntile], F32)
        ngc = C.sb("ngc", [128, ntile], F32)
        glb = C.sb("glb", [128, ntile], F32)
        egl = C.sb("egl", [128, ntile], F32)
        sc1 = C.sb("sc1", [128, ntile], F32)
        sc2 = C.sb("sc2", [128, ntile], F32)
        S = C.sb("S", [128, 128], F32)
        Sb = C.sb("Sb", [128, 128], BF16)
        qs = [C.sb("qs%d" % i, [128, TPB * 128], BF16) for i in range(2)]
        ks = [C.sb("ks%d" % i, [128, TPB * 128], BF16) for i in range(2)]
        vs = [C.sb("vs%d" % i, [128, TPB * 128], BF16) for i in range(2)]
        zs = [C.sb("zs%d" % i, [128, TPB * 128], BF16) for i in range(2)]
        osb = [C.sb("os%d" % i, [128, TPB * 128], BF16) for i in range(2)]

        NSET = 6

        def two(name, shape, dt):
            return [C.sb("%s%d" % (name, i), shape, dt) for i in range(NSET)]
        dabs = two("dabs", [128, 128], F32)
        W = two("W", [128, 128], F32)
        egB = two("egB", [128, 128], F32)
        bU = two("bU", [128, 128], F32)
        KW = two("KW", [128, 128], F32)
        Wm = two("Wm", [128, 128], F32)
        AT = two("AT", [128, 128], BF16)
        XY = [[C.sb("XY%d_%d" % (i, j), [128, 256], F32) for j in range(2)] for i in range(NSET)]
        Pm = [[C.sb("Pm%d_%d" % (i, j), [128, 128], F32) for j in range(2)] for i in range(NSET)]
        Pb = two("Pb", [128, 128], BF16)
        kbg = two("kbg", [128, 128], BF16)
        kd = two("kd", [128, 128], BF16)
        vb = two("vb", [128, 128], BF16)
        qd = two("qd", [128, 128], BF16)
        nwT = two("nwT", [128, 128], BF16)
        vn = two("vn", [128, 128], BF16)
        junk = two("junk", [128, 128], F32)
        ss = two("ss", [128, 1], F32)
        rt = two("rt", [128, 1], F32)
        rs = two("rs", [128, 1], F32)
        on = two("on", [128, 128], BF16)

        b0 = C.bank("b0"); b1 = C.bank("b1")
        C.nbank += 1
        b2 = st.enter_context(nc.psum_tensor("b2", [128, 1024], BF16))
        b3 = C.bank("b3"); b4 = C.bank("b4"); b5 = C.bank("b5"); b6 = C.bank("b6"); b7 = C.bank("b7")

        P.dma("sp", cs[:], consts, writes=["cs"], key="c0")
        P.dma("sp", nws[:], normw, writes=["nws"], key="c1")
        P.dma("sp", Gr[0:ntile, :, :], bgT.rearrange("a n p -> n a p"), writes=["Gr"], key="c2")
        C.cp("dve", identb[:], identf, ["cs"], ["identb"])
        C.memset("dve", onesf[:], 1.0, ["onesf"])
        C.memset("dve", S[:], 0.0, ["S"])
        C.memset("dve", Sb[:], 0.0, ["Sb"])
        C.mm(b0[:, 0:ntile], Gr[0:ntile, 0, :], identf[0:ntile, 0:ntile], True, True, ["Gr", "cs"], ["b0"])
        C.mm(b0[:, 128:128 + ntile], Gr[0:ntile, 1, :], identf[0:ntile, 0:ntile], True, True, ["Gr", "cs"], ["b0"])
        C.cp("dve", bcol[:], b0[:, 0:ntile], [], ["b0", "bcol"])
        C.cp("dve", gcol[:], b0[:, 128:128 + ntile], [], ["b0", "gcol"])
        C.mm(b1[:, 0:ntile], mUi, gcol[:], True, True, ["cs", "gcol"], ["b1"])
        C.mm(b1[:, 128:128 + ntile], onesf[:], gcol[:], True, True, ["onesf", "gcol"], ["b1"])
        C.cp("dve", gc[:], b1[:, 0:ntile], [], ["b1", "gc"])
        C.cp("dve", glb[:], b1[:, 128:128 + ntile], [], ["b1", "glb"])
        C.ts("dve", ngc[:], gc[:], -1.0, ALU.mult, ["gc"], ["ngc"])
        C.act(egl[:], glb[:], AF.Exp, ["glb"], ["egl"])
        C.tt("dve", sc2[:], glb[:], gc[:], ALU.subtract, ["glb", "gc"], ["sc2"])
        C.act(sc2[:], sc2[:], AF.Exp, ["sc2"], ["sc2"])
        C.act(sc1[:], gc[:], AF.Exp, ["gc"], ["sc1"])
        C.tt("dve", sc1[:], sc1[:], bcol[:], ALU.mult, ["sc1", "bcol"], ["sc1"])
        scal = ["bcol", "gcol", "ngc", "egl", "sc1", "sc2"]

        def load_block(blk):
            s = blk % 2
            sl = slice(blk * TPB * 128, (blk + 1) * TPB * 128)
            P.dma("sp", qs[s][:], qT[:, sl], writes=["qs%d" % s], key="lq%d" % s)
            P.dma("sp", ks[s][:], kT[:, sl], writes=["ks%d" % s], key="lk%d" % s)
            P.dma("sp", vs[s][:], vT[:, sl], writes=["vs%d" % s], key="lv%d" % s)
            P.dma("sp", zs[s][:], gzT[:, sl], writes=["zs%d" % s], key="lz%d" % s)

        def setup(n):
            s = n % NSET
            bs = (n // TPB) % 2
            tl = slice((n % TPB) * 128, (n % TPB) * 128 + 128)
            kn, qn, vn_ = ks[bs][:, tl], qs[bs][:, tl], vs[bs][:, tl]
            nn = slice(n, n + 1)
            T = "_%d" % s
            C.mm(b0[:, 0:128], gcol[:, nn].to_broadcast([128, 128]), mUi, True, True, ["gcol", "cs"], ["b0"]); yield
            C.mm(b0[:, 128:256], bcol[:, nn].to_broadcast([128, 128]), identf, True, True, ["bcol", "cs"], ["b0"]); yield
            C.mm(b0[:, 256:384], kn, kn, True, True, ["ks%d" % bs], ["b0"]); yield
            C.mm(b0[:, 384:512], kn, qn, True, True, ["ks%d" % bs, "qs%d" % bs], ["b0"]); yield
            C.tr(b2[:, 0:128], kn, identb[:], ["ks%d" % bs, "identb"], ["b2"]); yield
            C.tr(b2[:, 128:256], vn_, identb[:], ["vs%d" % bs, "identb"], ["b2"]); yield
            C.act(dabs[s][:], b0[:, 0:128], AF.Abs, ["ngc"], ["b0", "dabs" + T], bias=ngc[:, nn]); yield
            C.act(egB[s][:], b0[:, 0:128], AF.Exp, [], ["b0", "egB" + T]); yield
            C.tt("dve", bU[s][:], b0[:, 128:256], mUs, ALU.mult, ["cs"], ["b0", "bU" + T]); yield
            C.act(W[s][:], dabs[s][:], AF.Exp, ["dabs" + T], ["W" + T], scale=-1.0); yield
            C.tt("dve", KW[s][:], b0[:, 256:384], W[s][:], ALU.mult, ["W" + T], ["b0", "KW" + T]); yield
            C.tt("pool", Wm[s][:], W[s][:], mUi, ALU.mult, ["W" + T, "cs"], ["Wm" + T]); yield
            C.tt("dve", AT[s][:], b0[:, 384:512], Wm[s][:], ALU.mult, ["Wm" + T], ["b0", "AT" + T]); yield
            C.tt("pool", XY[s][0][:, 0:128], KW[s][:], bU[s][:], ALU.mult, ["KW" + T, "bU" + T], ["XY%d_0" % s]); yield
            C.tt("pool", dabs[s][:], KW[s][:], mLs, ALU.mult, ["KW" + T, "cs"], ["dabs" + T]); yield
            C.ts("pool", XY[s][0][:, 128:256], dabs[s][:], bcol[:, nn], ALU.mult, ["dabs" + T, "bcol"], ["XY%d_0" % s]); yield
            C.tt("pool", Pm[s][0][:], identf, XY[s][0][:, 0:128], ALU.subtract, ["cs", "XY%d_0" % s], ["Pm%d_0" % s]); yield
            C.act(kbg[s][:], b2[:, 0:128], AF.Identity, ["sc1"], ["b2", "kbg" + T], scale=sc1[:, nn]); yield
            C.act(kd[s][:], b2[:, 0:128], AF.Identity, ["sc2"], ["b2", "kd" + T], scale=sc2[:, nn]); yield
            C.ts("dve", vb[s][:], b2[:, 128:256], bcol[:, nn], ALU.mult, ["bcol"], ["b2", "vb" + T]); yield
            C.tt("pool", qd[s][:], qn, egB[s][:], ALU.mult, ["qs%d" % bs, "egB" + T], ["qd" + T]); yield

        def inv(n):
            s = n % NSET
            T = "_%d" % s
            bx, bp = (b3, b4) if n % 2 == 0 else (b1, b7)
            kx, kp = ("b3", "b4") if n % 2 == 0 else ("b1", "b7")
            for k in range(1, 7):
                a, b = (k - 1) % 2, k % 2
                Xp, Yp = XY[s][a][:, 0:128], XY[s][a][:, 128:256]
                ka, kb_ = "XY%d_%d" % (s, a), "XY%d_%d" % (s, b)
                if k < 6:
                    C.mm(bx[:, 0:128], Yp, Xp, True, True, [ka], [kx]); yield
                C.mm(bx[:, 128:256], Xp, Yp, True, True, [ka], [kx]); yield
                if k < 6:
                    C.cp("act" if k % 2 else "dve", XY[s][b][:], bx[:, 0:256], [], [kx, kb_]); yield
                else:
                    C.cp("dve", XY[s][b][:, 128:256], bx[:, 128:256], [], [kx, kb_]); yield
                C.mm(bp[:, 0:128], XY[s][b][:, 128:256], Pm[s][a][:], True, True, [kb_, "Pm%d_%d" % (s, a)], [kp]); yield
                if k < 6:
                    C.tt("dve", Pm[s][b][:], bp[:, 0:128], Pm[s][a][:], ALU.add, ["Pm%d_%d" % (s, a)], [kp, "Pm%d_%d" % (s, b)]); yield
                else:
                    C.tt("dve", Pb[s][:], bp[:, 0:128], Pm[s][a][:], ALU.add, ["Pm%d_%d" % (s, a)], [kp, "Pb" + T]); yield
            C.mm(bp[:, 128:256], kbg[s][:], Pb[s][:], True, True, ["kbg" + T, "Pb" + T], [kp]); yield
            C.ts("dve", nwT[s][:], bp[:, 128:256], -1.0, ALU.mult, [], [kp, "nwT" + T]); yield

        def rec(n):
            s = n % NSET
            bs = (n // TPB) % 2
            tl = slice((n % TPB) * 128, (n % TPB) * 128 + 128)
            nn = slice(n, n + 1)
            T = "_%d" % s
            C.mm(b5[:, 0:128], Pb[s][:], vb[s][:], True, False, ["Pb" + T, "vb" + T], ["b5"]); yield
            C.mm(b5[:, 0:128], nwT[s][:], Sb[:], False, True, ["nwT" + T, "Sb"], ["b5"]); yield
            C.cp("dve", vn[s][:], b5[:, 0:128], [], ["b5", "vn" + T]); yield
            C.mm(b6[:, 0:128], qd[s][:], Sb[:], True, False, ["qd" + T, "Sb"], ["b6"]); yield
            C.mm(b6[:, 0:128], AT[s][:], vn[s][:], False, True, ["AT" + T, "vn" + T], ["b6"]); yield
            C.mm(b6[:, 128:256], kd[s][:], vn[s][:], True, True, ["kd" + T, "vn" + T], ["b6"]); yield
            C.stt("dve", S[:], S[:], egl[:, nn], b6[:, 128:256], ALU.mult, ALU.add, ["S", "egl"], ["b6", "S"]); yield
            C.cp("act", Sb[:], S[:], ["S"], ["Sb"]); yield
            C.act(junk[s][:], b6[:, 0:128], AF.Square, [], ["b6", "junk" + T, "ss" + T], accum=ss[s][:]); yield
            C.act(rt[s][:], ss[s][:], AF.Sqrt, ["ss" + T], ["rt" + T], bias=RMS_EPS, scale=1.0 / 128); yield
            C.recip(rs[s][:], rt[s][:], ["rt" + T], ["rs" + T]); yield
            C.act(on[s][:], b6[:, 0:128], AF.Identity, ["rs" + T], ["b6", "on" + T], scale=rs[s][:]); yield
            C.tr(b2[:, 256:384], on[s][:], identb[:], ["on" + T, "identb"], ["b2"]); yield
            C.stt("dve", osb[bs][:, tl], b2[:, 256:384], nws[:, 0:1], zs[bs][:, tl], ALU.mult, ALU.mult,
                  ["nws", "zs%d" % bs], ["b2", "os%d" % bs]); yield
            if n % TPB == TPB - 1:
                blk = n // TPB
                d = P.dma("sp", oT[:, blk * TPB * 128:(blk + 1) * TPB * 128], osb[bs][:], reads=["os%d" % bs], key="oo%d" % bs)
                P.must_finish(d)

        def chain(*gs):
            for g in gs:
                if g is not None:
                    yield from g

        load_block(0)
        npair = ntile // 2
        for j in range(npair + 2):
            gens = []
            if j < npair:
                gens += [chain(setup(2 * j), setup(2 * j + 1))]
            if 1 <= j <= npair:
                gens += [inv(2 * j - 2), inv(2 * j - 1)]
            if j >= 2:
                gens += [chain(rec(2 * j - 4), rec(2 * j - 3))]
            _interleave(gens)
            if j >= 1 and (j - 1) % (TPB // 2) == 0 and (j - 1) // (TPB // 2) < nblk and j > 1:
                pass
            if j % (TPB // 2) == 1 and (j // (TPB // 2)) + 1 < nblk:
                load_block(j // (TPB // 2) + 1)
        P.emit()
    return nc


def b_consts():
    p = np.arange(128)[:, None]
    f = np.arange(128)[None, :]
    c = np.zeros((128, 4, 128), np.float32)
    c[:, 0] = (p == f)
    c[:, 1] = (f >= p)
    c[:, 2] = (f > p)
    c[:, 3] = (p > f)
    return c


TBC = 1024
FG = 11
NFG = FC // FG


def build_C():
    nc = bass.Bass("TRN2", target_bir_lowering=False)
    with ExitStack() as st:
        C = Ctx(nc, st)
        P = C.P
        catT = C.din("catT", [2048, NT], BF16)
        xTd = C.din("xT", [D_MODEL, NT])
        wout = C.din("wout", [2048, D_MODEL])
        wgur = C.din("wgur", [D_MODEL, 2 * D_FF])
        wd = C.din("wd", [D_FF, D_MODEL])
        lnp = C.din("lnp", [D_MODEL, 4])
        x2T = C.dout("x2T", [D_MODEL, NT])

        xb = C.sb("xb", [128, KC, TBC], F32)
        x1b = C.sb("x1b", [128, KC, TBC], BF16)
        catb = C.sb("catb", [128, KC * TBC], BF16)
        catv = catb[:].rearrange("p (k t) -> p k t", k=KC)
        hidv = catb[:, 0:FG * TBC].rearrange("p (k t) -> p k t", k=FG)
        ws = [C.sb("ws%d" % i, [128, KC, 256], BF16) for i in range(2)]
        wds = [C.sb("wds%d" % i, [128, FG, 256], BF16) for i in range(2)]
        lnps = C.sb("lnps", [128, KC, 4], F32)
        onesb = C.sb("onesb", [128, 128], BF16)
        sg = [C.sb("sg%d" % i, [128, 512], F32) for i in range(2)]
        rb = [C.sb("rb%d" % i, [128, 512], BF16) for i in range(2)]
        sq = [C.sb("sq%d" % i, [128, 512], BF16) for i in range(2)]
        t1 = [C.sb("t1_%d" % i, [128, 512], F32) for i in range(2)]
        t2 = [C.sb("t2_%d" % i, [128, 512], F32) for i in range(2)]
        mean = [C.sb("mean%d" % i, [128, 512], F32) for i in range(2)]
        rstd = [C.sb("rstd%d" % i, [128, 512], F32) for i in range(2)]
        tmpa = C.sb("tmpa", [128, 512], F32)
        pb = [C.bank("pb%d" % i) for i in range(6)]
        pst = [C.bank("pst%d" % i) for i in range(2)]

        P.dma("sp", lnps[:], lnp.rearrange("(k p) j -> p k j", p=128), writes=["lnps"], key="c0")
        C.memset("dve", onesb[:], 1.0, ["onesb"])
        state = {"ws": 0, "wds": 0, "pb": 0}
        wov = wout.rearrange("(k p) c -> p k c", p=128)
        wgv = wgur.rearrange("(k p) c -> p k c", p=128)
        wdv = wd.rearrange("(k p) c -> p k c", p=128)
        xv = xTd.rearrange("(k p) t -> p k t", p=128)
        cv = catT.rearrange("(k p) t -> p k t", p=128)
        ov = x2T.rearrange("(k p) t -> p k t", p=128)
        xkeys = ["xb%d" % k for k in range(KC)]

        def load_ws(src, col0):
            s = state["ws"]
            state["ws"] ^= 1
            for h in range(4):
                P.dma("pool", ws[s][:, 4 * h:4 * h + 4, :], src[:, 4 * h:4 * h + 4, col0:col0 + 256], writes=["ws%d" % s], key="ws%d" % s)
            return s

        def nbank():
            b = state["pb"]
            state["pb"] = (b + 1) % 6
            return b

        def layer_norm(blk, gi, bi, make_bf16):
            for hf in range(2):
                hs = slice(512 * hf, 512 * hf + 512)
                for k in range(KC):
                    q = k % 2
                    C.cp("act", rb[q][:], xb[:, k, hs], ["xb%d" % k], ["rb%d" % q])
                    C.act(sq[q][:], xb[:, k, hs], AF.Square, ["xb%d" % k], ["sq%d" % q])
                    C.mm(pst[0][:], onesb[:], rb[q][:], k == 0, k == KC - 1, ["onesb", "rb%d" % q], ["pst0"])
                    C.mm(pst[1][:], onesb[:], sq[q][:], k == 0, k == KC - 1, ["onesb", "sq%d" % q], ["pst1"])
                C.ts("dve", mean[hf][:], pst[0][:], 1.0 / D_MODEL, ALU.mult, [], ["pst0", "mean%d" % hf])
                C.tt("dve", tmpa[:], mean[hf][:], mean[hf][:], ALU.mult, ["mean%d" % hf], ["tmpa"])
                C.stt("dve", tmpa[:], pst[1][:], 1.0 / D_MODEL, tmpa[:], ALU.mult, ALU.subtract, ["tmpa"], ["pst1", "tmpa"])
                C.act(tmpa[:], tmpa[:], AF.Sqrt, ["tmpa"], ["tmpa"], bias=LN_EPS)
                C.recip(rstd[hf][:], tmpa[:], ["tmpa"], ["rstd%d" % hf])
                for k in range(KC):
                    q = k % 2
                    C.tt("dve", t1[q][:], xb[:, k, hs], mean[hf][:], ALU.subtract, ["xb%d" % k, "mean%d" % hf], ["t1_%d" % q])
                    C.tt("pool", t2[q][:], t1[q][:], rstd[hf][:], ALU.mult, ["t1_%d" % q, "rstd%d" % hf], ["t2_%d" % q])
                    C.act(xb[:, k, hs], t2[q][:], AF.Identity, ["t2_%d" % q, "lnps"], ["xb%d" % k],
                          bias=lnps[:, k, bi:bi + 1], scale=lnps[:, k, gi:gi + 1])
                    if make_bf16:
                        C.cp("dve", x1b[:, k, hs], xb[:, k, hs], ["xb%d" % k], ["x1b%d" % k])

        for blk in range(NT // TBC):
            ts_ = slice(blk * TBC, (blk + 1) * TBC)
            for k in range(KC):
                P.dma("sp", xb[:, k, :], xv[:, k, ts_], writes=["xb%d" % k], key="lx%d" % k)
            for k in range(KC):
                P.dma("sp", catv[:, k, :], cv[:, k, ts_], writes=["catbuf"], key="lc")
            for cg in range(8):
                s = load_ws(wov, 256 * cg)
                for half in range(2):
                    c = 2 * cg + half
                    for hf in range(2):
                        hs = slice(512 * hf, 512 * hf + 512)
                        b = nbank()
                        for m in range(KC):
                            C.mm(pb[b][:], ws[s][:, m, 128 * half:128 * half + 128], catv[:, m, hs], m == 0, m == KC - 1,
                                 ["ws%d" % s, "catbuf"], ["pb%d" % b])
                        C.stt("dve", xb[:, c, hs], xb[:, c, hs], ALPHA, pb[b][:], ALU.mult, ALU.add, ["xb%d" % c], ["pb%d" % b, "xb%d" % c])
            layer_norm(blk, 0, 1, True)
            for fg in range(NFG):
                for fi in range(FG):
                    f = fg * FG + fi
                    s = load_ws(wgv, 256 * f)
                    for hf in range(2):
                        hs = slice(512 * hf, 512 * hf + 512)
                        bg_ = nbank()
                        for k in range(KC):
                            C.mm(pb[bg_][:], ws[s][:, k, 0:128], x1b[:, k, hs], k == 0, k == KC - 1, ["ws%d" % s, "x1b%d" % k], ["pb%d" % bg_])
                        bu = nbank()
                        for k in range(KC):
                            C.mm(pb[bu][:], ws[s][:, k, 128:256], x1b[:, k, hs], k == 0, k == KC - 1, ["ws%d" % s, "x1b%d" % k], ["pb%d" % bu])
                        q = hf
                        C.act(sg[q][:], pb[bg_][:], AF.Silu, [], ["pb%d" % bg_, "sg%d" % q])
                        C.tt("dve", hidv[:, fi, hs], pb[bu][:], sg[q][:], ALU.mult, ["sg%d" % q], ["pb%d" % bu, "catbuf"])
                for cg in range(8):
                    s = state["wds"]
                    state["wds"] ^= 1
                    P.dma("pool", wds[s][:], wdv[:, fg * FG:(fg + 1) * FG, 256 * cg:256 * cg + 256], writes=["wds%d" % s], key="wds%d" % s)
                    for half in range(2):
                        c = 2 * cg + half
                        for hf in range(2):
                            hs = slice(512 * hf, 512 * hf + 512)
                            b = nbank()
                            for fi in range(FG):
                                C.mm(pb[b][:], wds[s][:, fi, 128 * half:128 * half + 128], hidv[:, fi, hs], fi == 0, fi == FG - 1,
                                     ["wds%d" % s, "catbuf"], ["pb%d" % b])
                            if fg == 0:
                                C.stt("dve", xb[:, c, hs], xb[:, c, hs], ALPHA, pb[b][:], ALU.mult, ALU.add, ["xb%d" % c], ["pb%d" % b, "xb%d" % c])
                            else:
                                C.tt("dve", xb[:, c, hs], xb[:, c, hs], pb[b][:], ALU.add, ["xb%d" % c], ["pb%d" % b, "xb%d" % c])
            layer_norm(blk, 2, 3, False)
            for k in range(KC):
                d = P.dma("sp", ov[:, k, ts_], xb[:, k, :], reads=["xb%d" % k], key="ox%d" % k)
                P.must_finish(d)
        P.emit()
    return nc


def host_inputs_C(w_out, w_gate_up, w_down, ln1_g, ln1_b, ln2_g, ln2_b):
    perm = []
    for f in range(FC):
        perm += list(range(128 * f, 128 * f + 128)) + list(range(D_FF + 128 * f, D_FF + 128 * f + 128))
    wgur = np.ascontiguousarray(w_gate_up[:, perm])
    lnp = np.ascontiguousarray(np.stack([ln1_g, ln1_b, ln2_g, ln2_b], axis=1))
    return {"wout": np.ascontiguousarray(w_out), "wgur": wgur, "wd": np.ascontiguousarray(w_down), "lnp": lnp}


_PROGS = {}


def _prog(name):
    if name not in _PROGS:
        _PROGS[name] = {"A": build_A, "B": build_B, "C": build_C}[name]()
    return _PROGS[name]


def _run(name, maps):
    res = run_bass_kernel_spmd(_prog(name), maps, core_ids=list(range(NCORES)))
    return res.results


def kernel(x, w_in, conv_dw_w, conv_dw_b, conv_ln_g, conv_ln_b, gdn_conv_w, gdn_A_log,
           gdn_dt_bias, gdn_norm_w, w_out, ln1_g, ln1_b, w_gate_up, w_down, ln2_g, ln2_b):
    f = lambda a: np.asarray(a, dtype=np.float32)
    xcur = f(x)[0]
    xT_cores = [np.ascontiguousarray(xcur[c * NT:(c + 1) * NT].T) for c in range(NCORES)]
    consts = b_consts()
    for l in range(DEPTH):
        mapsA = host_inputs_A_T(xT_cores, f(w_in[l]), f(conv_dw_w[l]), f(conv_dw_b[l]), f(conv_ln_g[l]), f(conv_ln_b[l]),
                                f(gdn_conv_w[l]), f(gdn_A_log[l]), f(gdn_dt_bias[l]))
        ra = _run("A", mapsA)
        mapsB = []
        for h in range(NHEAD):
            cat = lambda nm, r0: np.ascontiguousarray(np.concatenate([np.asarray(ra[c][nm])[r0:r0 + 128] for c in range(NCORES)], axis=1))
            beta = np.concatenate([np.asarray(ra[c]["bg"])[h] for c in range(NCORES)])
            g = np.concatenate([np.asarray(ra[c]["bg"])[NHEAD + h] for c in range(NCORES)])
            mapsB.append({"qT": cat("qkvT", 128 * h), "kT": cat("qkvT", 1024 + 128 * h), "vT": cat("qkvT", 2048 + 128 * h),
                          "gzT": cat("gzT", 128 * h),
                          "bgT": np.ascontiguousarray(np.stack([beta.reshape(NTILE, 128), g.reshape(NTILE, 128)])),
                          "consts": consts, "normw": np.ascontiguousarray(f(gdn_norm_w[l]).reshape(128, 1))})
        rb = _run("B", mapsB)
        wc = host_inputs_C(f(w_out[l]), f(w_gate_up[l]), f(w_down[l]), f(ln1_g[l]), f(ln1_b[l]), f(ln2_g[l]), f(ln2_b[l]))
        mapsC = []
        for c in range(NCORES):
            catT = np.concatenate([np.asarray(ra[c]["convT"])] + [np.asarray(rb[h]["oT"])[:, c * NT:(c + 1) * NT] for h in range(NHEAD)], axis=0)
            m = dict(wc)
            m["catT"] = np.ascontiguousarray(catT)
            m["xT"] = xT_cores[c]
            mapsC.append(m)
        rc = _run("C", mapsC)
        xT_cores = [np.ascontiguousarray(np.asarray(rc[c]["x2T"])) for c in range(NCORES)]
    out = np.concatenate([xT_cores[c].T for c in range(NCORES)], axis=0)
    return np.ascontiguousarray(out[None]).astype(np.float32)


def host_inputs_A_T(xT_cores, w_in, conv_dw_w, conv_dw_b, conv_ln_g, conv_ln_b, gdn_conv_w, A_log, dt_bias):
    perm = []
    for i in range(8):
        perm += list(range(128 * i, 128 * i + 128)) + list(range(1024 + 128 * i, 1024 + 128 * i + 128))
    perm += list(range(2048, IN_COLS))
    winr = np.ascontiguousarray(w_in[:, perm])
    cdw = np.ascontiguousarray(conv_dw_w.T)
    cvec = np.ascontiguousarray(np.stack([conv_dw_b, conv_ln_g, conv_ln_b], axis=1))
    gcw = np.ascontiguousarray(gdn_conv_w.T)
    hv = np.ascontiguousarray(np.stack([A_log, dt_bias], axis=1))
    ident = np.eye(128, dtype=np.float32)
    maps = []
    for c in range(len(xT_cores)):
        xt = np.zeros((D_MODEL, NTX), np.float32)
        xt[:, HALO:] = xT_cores[c]
        if c > 0:
            xt[:, :HALO] = xT_cores[c - 1][:, NT - HALO:]
        maps.append({"xT": xt, "winr": winr, "cdw": cdw, "cvec": cvec, "gcw": gcw, "hv": hv, "ident": ident})
    return maps
```
